# Optimizing a Trainium2 kernel written in Bass

```python
import jax, jax.numpy as jnp
from jax import lax
import numpy as np

D_MODEL = 1024
BATCH = 8
SEQ = 2048
DEPTH = 1

PLE_DIM = 256
MIX_WIDTH = D_MODEL
GDN_WIDTH = D_MODEL // 2
GDN_HEAD_DIM = 128
GDN_HEADS = GDN_WIDTH // GDN_HEAD_DIM
FOX_WIDTH = MIX_WIDTH - GDN_WIDTH
FOX_HEAD_DIM = 64
FOX_HEADS = FOX_WIDTH // FOX_HEAD_DIM
CONV_WIDTH = 4
CHUNK = 64
Q_BLOCK = 128
D_FF = -(-8 * D_MODEL // (3 * 256)) * 256
EPS = 1e-6

IN_SIZES = [3 * GDN_WIDTH, GDN_WIDTH, GDN_HEADS, GDN_HEADS, 3 * FOX_WIDTH, FOX_HEADS]
D_IN = sum(IN_SIZES)
IN_SPLITS = [sum(IN_SIZES[:i + 1]) for i in range(len(IN_SIZES) - 1)]

kernel_name = 'hymba_style_gdn_fox_hybrid'


def rmsnorm(x, w):
    xf = x.astype(jnp.float32)
    y = xf * lax.rsqrt(jnp.mean(xf * xf, axis=-1, keepdims=True) + EPS)
    return (y * w.astype(jnp.float32)).astype(x.dtype)


def l2norm(x):
    return x * lax.rsqrt(jnp.sum(x * x, axis=-1, keepdims=True) + EPS)


def causal_depthwise_conv(x, w):
    c = x.shape[-1]
    return lax.conv_general_dilated(
        x, w.astype(x.dtype)[:, None, :], window_strides=(1,),
        padding=[(CONV_WIDTH - 1, 0)], dimension_numbers=('NWC', 'WIO', 'NWC'),
        feature_group_count=c)


def gated_delta_rule(q, k, v, g, beta):
    b, s, h, dk = q.shape
    dv = v.shape[-1]
    n = s // CHUNK

    def chunk(t):
        return jnp.moveaxis(t.reshape((b, n, CHUNK, h) + t.shape[3:]), 3, 2)

    q = chunk(q * (dk ** -0.5))
    k, v, g, beta = chunk(k), chunk(v), chunk(g), chunk(beta)
    gc = jnp.cumsum(g, axis=-1)
    lower = jnp.tril(jnp.ones((CHUNK, CHUNK), bool))
    strict = jnp.tril(jnp.ones((CHUNK, CHUNK), bool), -1)
    diff = gc[..., :, None] - gc[..., None, :]
    decay = jnp.where(lower, jnp.exp(jnp.where(lower, diff, 0.0)), 0.0)
    kb = k * beta[..., None]
    vb = v * beta[..., None]
    kk = jnp.einsum('bnhid,bnhjd->bnhij', kb, k) * decay
    a_mat = jnp.where(strict, kk, 0.0) + jnp.eye(CHUNK, dtype=kk.dtype)
    rhs = jnp.concatenate([vb, kb * jnp.exp(gc)[..., None]], axis=-1)
    sol = lax.linalg.triangular_solve(a_mat, rhs, left_side=True, lower=True, unit_diagonal=True)
    u, w = sol[..., :dv], sol[..., dv:]
    qk = jnp.where(lower, jnp.einsum('bnhid,bnhjd->bnhij', q, k) * decay, 0.0)

    def step(state, xs):
        q_c, k_c, u_c, w_c, gc_c, qk_c = xs
        v_new = u_c - jnp.einsum('bhcd,bhde->bhce', w_c, state)
        o = (jnp.einsum('bhcd,bhde->bhce', q_c * jnp.exp(gc_c)[..., None], state)
             + jnp.einsum('bhij,bhje->bhie', qk_c, v_new))
        g_last = gc_c[..., -1]
        k_dec = k_c * jnp.exp(g_last[..., None] - gc_c)[..., None]
        state = state * jnp.exp(g_last)[..., None, None] + jnp.einsum('bhcd,bhce->bhde', k_dec, v_new)
        return state, o

    xs = tuple(jnp.moveaxis(t, 1, 0) for t in (q, k, u, w, gc, qk))
    state0 = jnp.zeros((b, h, dk, dv), jnp.float32)
    _, o = lax.scan(step, state0, xs)
    o = jnp.moveaxis(jnp.moveaxis(o, 0, 1), 2, 3)
    return o.reshape(b, s, h, dv)


def forgetting_attention(q, k, v, log_f):
    b, s, h, d = q.shape
    scale = d ** -0.5
    c = jnp.cumsum(log_f.astype(jnp.float32), axis=1).transpose(0, 2, 1)
    q, k, v = (t.transpose(0, 2, 1, 3) for t in (q, k, v))
    outs = []
    for blk in range(s // Q_BLOCK):
        q0, q1 = blk * Q_BLOCK, (blk + 1) * Q_BLOCK
        logits = jnp.einsum('bhqd,bhkd->bhqk', q[:, :, q0:q1], k[:, :, :q1],
                            preferred_element_type=jnp.float32) * scale
        logits = logits + c[:, :, q0:q1, None] - c[:, :, None, :q1]
        causal = jnp.arange(q0, q1)[:, None] >= jnp.arange(q1)[None, :]
        logits = jnp.where(causal, logits, -jnp.inf)
        prob = jax.nn.softmax(logits, axis=-1)
        outs.append(jnp.einsum('bhqk,bhkd->bhqd', prob.astype(v.dtype), v[:, :, :q1]))
    o = jnp.concatenate(outs, axis=2)
    return o.transpose(0, 2, 1, 3)


def setup_inputs(seed: int = 0) -> dict:
    key = jax.random.key(seed)
    ks = jax.random.split(key, 20)
    f32 = jnp.float32
    nrm = lambda k, shape, s: jax.random.normal(k, shape, f32) * s
    gain = lambda k, shape: 1.0 + 0.1 * jax.random.normal(k, shape, f32)
    dt = jnp.exp(jax.random.uniform(ks[5], (DEPTH, GDN_HEADS), f32) * (np.log(0.1) - np.log(0.001)) + np.log(0.001))
    dt_bias = dt + jnp.log(-jnp.expm1(-dt))
    return {
        'x': nrm(ks[0], (BATCH, SEQ, D_MODEL), 1.0),
        'p': nrm(ks[1], (DEPTH, BATCH, SEQ, PLE_DIM), 1.0),
        'attn_norm_w': gain(ks[2], (DEPTH, D_MODEL)),
        'w_in': nrm(ks[3], (DEPTH, D_MODEL, D_IN), D_MODEL ** -0.5),
        'conv_w': nrm(ks[4], (DEPTH, CONV_WIDTH, 3 * GDN_WIDTH), CONV_WIDTH ** -0.5),
        'a_log': jnp.log(jax.random.uniform(ks[6], (DEPTH, GDN_HEADS), f32, 1.0, 16.0)),
        'dt_bias': dt_bias,
        'gdn_norm_w': gain(ks[7], (DEPTH, GDN_HEAD_DIM)),
        'fox_f_bias': 3.0 + 0.5 * jax.random.normal(ks[8], (DEPTH, FOX_HEADS), f32),
        'w_out': nrm(ks[9], (DEPTH, MIX_WIDTH, D_MODEL), MIX_WIDTH ** -0.5),
        'ffn_norm_w': gain(ks[10], (DEPTH, D_MODEL)),
        'w_gate_up': nrm(ks[11], (DEPTH, D_MODEL, 2 * D_FF), D_MODEL ** -0.5),
        'w_down': nrm(ks[12], (DEPTH, D_FF, D_MODEL), D_FF ** -0.5),
        'ple_norm_w': gain(ks[13], (DEPTH, D_MODEL)),
        'w_ple_gate': nrm(ks[14], (DEPTH, D_MODEL, D_MODEL), D_MODEL ** -0.5),
        'w_ple_proj': nrm(ks[15], (DEPTH, PLE_DIM, D_MODEL), PLE_DIM ** -0.5),
        'final_norm_w': gain(ks[16], (D_MODEL,)),
    }


def reference(x, p, attn_norm_w, w_in, conv_w, a_log, dt_bias, gdn_norm_w, fox_f_bias, w_out,
              ffn_norm_w, w_gate_up, w_down, ple_norm_w, w_ple_gate, w_ple_proj, final_norm_w):
    b, s, _ = x.shape
    f32 = jnp.float32
    h = x
    for i in range(DEPTH):
        xn = rmsnorm(h, attn_norm_w[i])
        proj = xn @ w_in[i]
        g_qkv, g_z, g_a, g_b, f_qkv, f_f = jnp.split(proj, IN_SPLITS, axis=-1)

        g_qkv = jax.nn.silu(causal_depthwise_conv(g_qkv, conv_w[i])).astype(f32)
        gq, gk, gv = jnp.split(g_qkv.reshape(b, s, 3 * GDN_HEADS, GDN_HEAD_DIM), 3, axis=2)
        gq, gk = l2norm(gq), l2norm(gk)
        beta = jax.nn.sigmoid(g_b.astype(f32))
        g_decay = -jnp.exp(a_log[i].astype(f32)) * jax.nn.softplus(g_a.astype(f32) + dt_bias[i].astype(f32))
        o_g = gated_delta_rule(gq, gk, gv, g_decay, beta)
        o_g = o_g * lax.rsqrt(jnp.mean(o_g * o_g, axis=-1, keepdims=True) + EPS) * gdn_norm_w[i].astype(f32)
        o_g = o_g * jax.nn.silu(g_z.astype(f32)).reshape(b, s, GDN_HEADS, GDN_HEAD_DIM)
        o_g = o_g.reshape(b, s, GDN_WIDTH).astype(h.dtype)

        fq, fk, fv = jnp.split(f_qkv.reshape(b, s, 3 * FOX_HEADS, FOX_HEAD_DIM), 3, axis=2)
        log_f = jax.nn.log_sigmoid(f_f.astype(f32) + fox_f_bias[i].astype(f32))
        o_f = forgetting_attention(fq, fk, fv, log_f).reshape(b, s, FOX_WIDTH).astype(h.dtype)

        h = h + jnp.concatenate([o_g, o_f], axis=-1) @ w_out[i]

        hn = rmsnorm(h, ffn_norm_w[i])
        gate, up = jnp.split(hn @ w_gate_up[i], 2, axis=-1)
        h = h + (jax.nn.silu(gate) * up) @ w_down[i]

        ple_gate = jax.nn.sigmoid(rmsnorm(h, ple_norm_w[i]) @ w_ple_gate[i])
        h = h + ple_gate * (p[i] @ w_ple_proj[i])
    return rmsnorm(h, final_norm_w)
```

```python
import numpy as np
from contextlib import ExitStack
import concourse.bass as bass
import concourse.mybir as mybir
from concourse.bass_utils import run_bass_kernel_spmd

F32 = mybir.dt.float32
BF16 = mybir.dt.bfloat16
AF = mybir.ActivationFunctionType
ALU = mybir.AluOpType


class Sched:
    ENGS = ('pe', 'act', 'dve', 'pool', 'sp')
    NDMA = {'sp': 8, 'pool': 8, 'act': 4}

    def __init__(self, nc, es):
        self.nc = nc
        self.sem = {}
        for k in self.ENGS:
            self.sem[k] = es.enter_context(nc.semaphore("sem_" + k))
        self.dpool = {}
        for q, n in self.NDMA.items():
            keys = []
            for i in range(n):
                key = "d_%s%d" % (q, i)
                self.sem[key] = es.enter_context(nc.semaphore(key))
                keys.append(key)
            self.dpool[q] = keys
        self.tot = {k: 0 for k in self.sem}
        self.seen = {k: {} for k in self.ENGS}
        self.prog = {k: [] for k in self.ENGS}
        self.last_w = {}
        self.readers = {}
        self.rr = {q: 0 for q in self.dpool}
        self.out_tokens = []
        self.nops = 0
        self.dead = False

    def _deps(self, reads, writes):
        deps = []
        for r in reads:
            t = self.last_w.get(r)
            if t:
                deps.append(t)
        for w in writes:
            t = self.last_w.get(w)
            if t:
                deps.append(t)
            deps.extend(self.readers.get(w, ()))
        return deps

    def _resolve(self, eng, deps):
        need = {}
        seen = self.seen[eng]
        for (k, v) in deps:
            if k == eng and eng == 'pe':
                continue
            if seen.get(k, 0) >= v:
                continue
            if need.get(k, 0) < v:
                need[k] = v
        for k, v in need.items():
            seen[k] = v
        return list(need.items())

    def _commit(self, token, reads, writes):
        for w in writes:
            self.last_w[w] = token
            self.readers[w] = []
        for r in reads:
            self.readers.setdefault(r, []).append(token)

    def op(self, eng, fn, reads=(), writes=(), sig=True):
        if self.dead:
            return
        waits = self._resolve(eng, self._deps(reads, writes))
        if sig:
            self.tot[eng] += 1
            token = (eng, self.tot[eng])
            inc = (eng, 1)
        else:
            token = (eng, self.tot[eng] + 1)
            inc = None
        self.prog[eng].append((waits, fn, inc))
        self._commit(token, reads, writes)
        self.nops += 1

    def dma(self, q, out, in_, reads=(), writes=(), is_out=False):
        if self.dead:
            return
        pool = self.dpool[q]
        key = pool[self.rr[q] % len(pool)]
        self.rr[q] += 1
        deps = self._deps(reads, writes)
        if self.tot[key] > 0:
            deps.append((key, self.tot[key]))
        waits = self._resolve(q, deps)
        self.tot[key] += 16
        token = (key, self.tot[key])
        self.prog[q].append((waits, (lambda e, o=out, i=in_: e.dma_start(out=o, in_=i)), (key, 16)))
        self._commit(token, reads, writes)
        if is_out:
            self.out_tokens.append(token)
        self.nops += 1

    def barrier(self):
        if self.dead:
            return
        allt = [(k, v) for k, v in self.tot.items() if v > 0]
        for e in self.ENGS:
            waits = self._resolve(e, allt)
            if waits:
                self.prog[e].append((waits, None, None))

    def finish(self):
        self.barrier()

    def emit(self, block):
        def mk(k):
            def f(e):
                for waits, fn, inc in self.prog[k]:
                    for (sk, v) in waits:
                        e.wait_ge(self.sem[sk], v)
                    if fn is not None:
                        ins = fn(e)
                        if inc is not None:
                            ins.then_inc(self.sem[inc[0]], inc[1])
            return f
        block.tensor(mk('pe'))
        block.scalar(mk('act'))
        block.vector(mk('dve'))
        block.gpsimd(mk('pool'))
        block.sync(mk('sp'))
        self.prog = {k: [] for k in self.ENGS}


S_LEN = 2048
D = 1024
NT = 16
DFF = 2816
NFF = 22
EPS = 1e-6
C_GQKV, C_GZ, C_GA, C_GB, C_FQ, C_FK, C_FV, C_FF = 0, 1536, 2048, 2052, 2056, 2568, 3080, 3592


def make_consts():
    c = np.zeros((128, 11, 128), np.float32)
    i = np.arange(128)
    c[:, 0, :] = np.eye(128)
    c[:, 1, :] = (i[:, None] <= i[None, :])
    c[:, 2, :] = 1.0
    c[:, 3, :] = (i[:, None] < i[None, :])
    c[:, 4, :] = 0.0
    c[127, 4, :] = 1.0
    c[:, 5, :] = (i[:, None] > i[None, :])
    c[:, 6, :] = -1.0 * (i[:, None] > i[None, :])
    c[:, 7, :] = (i[:, None] // 16 == i[None, :] // 16)
    for n_, sz in ((8, 16), (9, 32), (10, 64)):
        c[:, n_, :] = (i[:, None] // (2 * sz) == i[None, :] // (2 * sz)) & (i[:, None] // sz != i[None, :] // sz)
    return c.reshape(128, 11 * 128)


class _Stop(Exception):
    pass


def build_program(skip_gdn=False, dbg=(), stop_after=None):
    nc = bass.Bass("TRN2", target_bir_lowering=False)

    def din(name, shape):
        return nc.dram_tensor(name, list(shape), F32, kind="ExternalInput").ap()
    x = din("x", [S_LEN, D]); p = din("p", [S_LEN, 256])
    w_in = din("w_in", [D, 3600]); w_out = din("w_out", [D, D])
    w_gu = din("w_gate_up", [D, 2 * DFF]); w_dn = din("w_down", [DFF, D])
    w_pg = din("w_ple_gate", [D, D]); w_pp = din("w_ple_proj", [256, D])
    g_attn = din("attn_norm_w", [D]); g_ffn = din("ffn_norm_w", [D])
    g_ple = din("ple_norm_w", [D]); g_fin = din("final_norm_w", [D])
    conv_wT = din("conv_wT", [128, 12 * 4]); smallv = din("smallv", [64])
    gdn_nw = din("gdn_norm_w", [128]); fox_fb = din("fox_f_bias", [8])
    cst = din("cst", [128, 11 * 128])
    y = nc.dram_tensor("y", [S_LEN, D], F32, kind="ExternalOutput").ap()
    dbg_out = {}

    with ExitStack() as es:
        def sb(name, shape, dt, stack=None):
            try:
                return (stack or es).enter_context(nc.sbuf_tensor(name, list(shape), dt))
            except AssertionError:
                print("ALLOC FAIL", name, shape, dt, "live:", [(k, v) for k, v in alloc_log])
                raise
            finally:
                alloc_log.append((name, int(np.prod(shape[1:])) * (4 if dt == F32 else 2)))
        alloc_log = []
        ps = [es.enter_context(nc.psum_tensor("ps%d" % i, [128, 512], F32)) for i in range(8)]
        S = Sched(nc, es)

        def flush():
            S.barrier()
            with nc.Block() as block:
                S.emit(block)

        def stage_done(name, dumps=()):
            for (nm, ap2d, rows, cols) in dumps:
                if nm in dbg:
                    d = nc.dram_tensor("dbg_" + nm, [rows, cols], ap2d.dtype, kind="ExternalOutput").ap()
                    S.dma('sp', d, ap2d, is_out=True)
                    flush()
            if stop_after == name:
                S.finish()
                flush()
                S.dead = True

        cf = sb("cf", [128, 11, 128], F32)
        cb = sb("cb", [128, 11, 128], BF16)
        S.dma('sp', cf[:].rearrange("p a b -> p (a b)"), cst, writes=['cf'])
        S.dma('pool', cb[:].rearrange("p a b -> p (a b)"), cst, writes=['cb'])
        identb = cb[:, 0, :]; identf = cf[:, 0, :]; trif = cf[:, 1, :]; onesf = cf[:, 2, :]
        mask_le_b = cb[:, 1, :]; mask_lt_f = cf[:, 3, :]; mask_le_f = cf[:, 1, :]; sel127 = cf[:, 4, :]; mask_gt_f = cf[:, 5, :]; mask_gt_neg_f = cf[:, 6, :]

        def psb(i):
            return ps[i][:, :].bitcast(BF16)

        cnt = {'ev': 0}

        def evac_engine():
            cnt['ev'] += 1
            return 'act' if cnt['ev'] % 2 else 'dve'

        def copy_op(eng, out, in_, reads, writes):
            if eng == 'act':
                S.op('act', lambda e: e.activation(out, in_, AF.Copy), reads=reads, writes=writes)
            else:
                S.op(eng, lambda e: e.tensor_copy(out, in_), reads=reads, writes=writes)

        def rms_norm_T(src, src_key, gain, gain_key, dstT, dst_key, c0, scratch, tag, bank):
            junk, ssq, xs = scratch
            k = tag
            S.op('act', lambda e: e.activation(junk[:, :], src, AF.Square, accum_out=ssq[:, 0:1]),
                 reads=[src_key], writes=['junk' + k, 'ssq' + k])
            S.op('act', lambda e: e.activation(ssq[:, 1:2], ssq[:, 0:1], AF.Ln, bias=EPS, scale=1.0 / D),
                 reads=['ssq' + k], writes=['ssq1' + k])
            S.op('act', lambda e: e.activation(ssq[:, 2:3], ssq[:, 1:2], AF.Exp, scale=-0.5),
                 reads=['ssq1' + k], writes=['rstd' + k])
            S.op('dve', lambda e: e.scalar_tensor_tensor(xs[:, :], src, ssq[:, 2:3], gain, ALU.mult, ALU.mult),
                 reads=[src_key, 'rstd' + k, gain_key], writes=['xs' + k])
            pb = psb(bank)
            for kc in range(8):
                S.op('pe', lambda e, kc=kc: e.transpose(pb[:, kc * 128:(kc + 1) * 128], xs[:, kc * 128:(kc + 1) * 128], identb),
                     reads=['xs' + k, 'cb'], writes=['ps%d' % bank], sig=(kc == 7))
            copy_op(evac_engine(), dstT[:, :, c0:c0 + 128], pb.rearrange("p (k t) -> p k t", k=8),
                    reads=['ps%d' % bank], writes=[dst_key])


        def lockstep(n, body):
            recs = []
            for ii_ in range(n):
                rec = []
                S.op = lambda *a, _r=rec, **k: _r.append((a, k))
                try:
                    body(ii_)
                finally:
                    del S.op
                recs.append(rec)
            for k_ in range(max(len(r) for r in recs)):
                for r in recs:
                    if k_ < len(r):
                        a_, kw_ = r[k_]
                        S.op(*a_, **kw_)


        def norm_T_all(tag, src_of, srckey_of, gain, gain_key, stack, emit_dst, banks=(0, 1), pre_tile=None):
            ssq = sb("ssq_" + tag, [128, NT], F32, stack)
            lnv = sb("lnv_" + tag, [128, NT], F32, stack)
            rstd = sb("rstd_" + tag, [128, NT], F32, stack)
            junk = [sb("junk_%s%d" % (tag, i), [128, D], BF16, stack) for i in range(2)]
            xs = [sb("xs_%s%d" % (tag, i), [128, D], BF16, stack) for i in range(2)]

            def stats(g):
                for t in range(4 * g, 4 * g + 4):
                    if pre_tile is not None:
                        pre_tile(t)
                    S.op('act', lambda e, t=t: e.activation(junk[t % 2][:, :], src_of(t), AF.Square, accum_out=ssq[:, t:t + 1]),
                         reads=[srckey_of(t)], writes=['junk_%s%d' % (tag, t % 2), 'ssq_%s%d' % (tag, t)])
                gs = slice(4 * g, 4 * g + 4)
                S.op('act', lambda e: e.activation(lnv[:, gs], ssq[:, gs], AF.Ln, bias=EPS, scale=1.0 / D),
                     reads=['ssq_%s%d' % (tag, t) for t in range(4 * g, 4 * g + 4)], writes=['lnv_%s%d' % (tag, g)])
                S.op('act', lambda e: e.activation(rstd[:, gs], lnv[:, gs], AF.Exp, scale=-0.5),
                     reads=['lnv_%s%d' % (tag, g)], writes=['rstd_%s%d' % (tag, g)])

            def scale(t):
                b = t % 2
                S.op('dve', lambda e, t=t, b=b: e.scalar_tensor_tensor(xs[b][:, :], src_of(t), rstd[:, t:t + 1], gain, ALU.mult, ALU.mult),
                     reads=[srckey_of(t), 'rstd_%s%d' % (tag, t // 4), gain_key], writes=['xs_%s%d' % (tag, b)])

            def apply(g):
                for t in range(4 * g, 4 * g + 4):
                    b = t % 2
                    bank = banks[b]
                    if t == 0:
                        scale(0)
                    if t + 1 < NT:
                        scale(t + 1)
                    pb = psb(bank)
                    for kc in range(8):
                        S.op('pe', lambda e, kc=kc, b=b, pb=pb: e.transpose(pb[:, kc * 128:(kc + 1) * 128], xs[b][:, kc * 128:(kc + 1) * 128], identb),
                             reads=['xs_%s%d' % (tag, b), 'cb'], writes=['ps%d' % bank], sig=(kc == 7))
                    emit_dst(t, pb, 'ps%d' % bank)
            stats(0)
            for g in range(4):
                if g + 1 < 4:
                    stats(g + 1)
                apply(g)
            return rstd

        if True:
          with ExitStack() as mx:
              oT_g = sb("oT_g", [128, 4, S_LEN], BF16, mx)
              xnT = sb("xnT", [128, 8, S_LEN], BF16, mx)
              scal = sb("scal", [128, NT, 16], F32, mx)
              cfx = sb("cfx", [128, NT, 8], F32, mx)
              c123 = sb("c123", [128, 3, NT, 8], BF16, mx)
              with ExitStack() as pa:
                  gA = sb("gA", [128, D], F32, pa)
                  xall = sb("xall", [128, NT, D], F32, pa)
                  wsm = sb("wsm", [128, 8, 16], BF16, pa)
                  S.dma('sp', gA[:], g_attn.partition_broadcast(128), writes=['gA'])
                  S.dma('pool', wsm[:, :, 0:8], w_in[:, C_GA:C_GA + 8].rearrange("(k p) n -> p k n", p=128), writes=['wsm'])
                  S.dma('pool', wsm[:, :, 8:16], w_in[:, C_FF:C_FF + 8].rearrange("(k p) n -> p k n", p=128), writes=['wsm'])
                  for t in range(NT):
                      S.dma('sp', xall[:, t, :], x[t * 128:(t + 1) * 128, :], writes=['xall%d' % t])

                  def dstA(t, pb, bkey):
                      copy_op('act', xnT[:, :, t * 128:(t + 1) * 128], pb.rearrange("p (k t) -> p k t", k=8), reads=[bkey], writes=[bkey, 'xnT%d' % t])
                  norm_T_all('A', lambda t: xall[:, t, :], lambda t: 'xall%d' % t, gA[:, :], 'gA', pa, dstA)
                  for t in range(NT):
                      for kc in range(8):
                          S.op('pe', lambda e, t=t, kc=kc: e.matmul(ps[2][:, t * 16:(t + 1) * 16], xnT[:, kc, t * 128:(t + 1) * 128],
                                                                   wsm[:, kc, :], start=(kc == 0), stop=(kc == 7)),
                               reads=['xnT%d' % t, 'wsm'], writes=['ps2'], sig=(kc == 7 and t == NT - 1))
                  S.op('dve', lambda e: e.tensor_copy(scal[:].rearrange("p t n -> p (t n)"), ps[2][:, 0:NT * 16]),
                       reads=['ps2'], writes=['scal'])
                  flush()
                  stage_done('A', [('xnT', xnT[:, 0, :], 128, S_LEN), ('scal', scal[:].rearrange("p t n -> p (t n)"), 128, NT * 16)])
              with ExitStack() as pb_:
                  fbt = sb("fbt", [128, 8], F32, pb_)
                  tA = sb("tA", [128, NT, 8], F32, pb_)
                  tB = sb("tB", [128, NT, 8], F32, pb_)
                  tC = sb("tC", [128, NT, 8], F32, pb_)
                  pre = sb("pre", [128, NT, 8], F32, pb_)
                  S.dma('sp', fbt[:], fox_fb.partition_broadcast(128), writes=['fbt'])
                  ff = scal[:, :, 8:16]
                  for t in range(NT):
                      S.op('dve', lambda e, t=t: e.tensor_tensor(tA[:, t, :], scal[:, t, 8:16], fbt[:, :], ALU.add),
                           reads=['scal', 'fbt'], writes=['tA'])
                  S.op('act', lambda e: e.activation(tB[:], tA[:], AF.Abs), reads=['tA'], writes=['tB'])
                  S.op('act', lambda e: e.activation(tB[:], tB[:], AF.Exp, scale=-1.0), reads=['tB'], writes=['tB'])
                  S.op('act', lambda e: e.activation(tB[:], tB[:], AF.Ln, bias=1.0), reads=['tB'], writes=['tB'])
                  S.op('dve', lambda e: e.tensor_single_scalar(tC[:], tA[:], 0.0, ALU.min), reads=['tA'], writes=['tC'])
                  S.op('dve', lambda e: e.tensor_tensor(tA[:], tC[:], tB[:], ALU.subtract), reads=['tB', 'tC'], writes=['tA'])
                  flush(); stage_done('B1')
                  lf2 = tA[:].rearrange("p t n -> p (t n)")
                  S.op('pe', lambda e: e.matmul(ps[3][:, 0:128], trif, lf2, start=True, stop=True), reads=['tA', 'cf'], writes=['ps3'])
                  S.op('pe', lambda e: e.matmul(ps[3][:, 128:256], onesf, lf2, start=True, stop=True), reads=['tA', 'cf'], writes=['ps3'])
                  S.op('dve', lambda e: e.tensor_copy(tB[:].rearrange("p t n -> p (t n)"), ps[3][:, 0:128]), reads=['ps3'], writes=['tB'])
                  S.op('act', lambda e: e.activation(tC[:].rearrange("p t n -> p (t n)"), ps[3][:, 128:256], AF.Copy), reads=['ps3'], writes=['tC'])
                  flush(); stage_done('B2')
                  S.op('dve', lambda e: e.memset(pre[:, 0, :], 0.0), writes=['pre'])
                  for t in range(1, NT):
                      S.op('dve', lambda e, t=t: e.tensor_tensor(pre[:, t, :], pre[:, t - 1, :], tC[:, t - 1, :], ALU.add),
                           reads=['pre', 'tC'], writes=['pre'])
                  S.op('dve', lambda e: e.tensor_tensor(cfx[:], tB[:], pre[:], ALU.add), reads=['tB', 'pre'], writes=['cfx'])
                  flush(); stage_done('B3')
                  S.op('dve', lambda e: e.tensor_copy(c123[:, 0, :, :], cfx[:]), reads=['cfx'], writes=['c1'])
                  S.op('dve', lambda e: e.tensor_tensor(tA[:], cfx[:], c123[:, 0, :, :], ALU.subtract), reads=['cfx', 'c1'], writes=['tA'])
                  S.op('dve', lambda e: e.tensor_copy(c123[:, 1, :, :], tA[:]), reads=['tA'], writes=['c2'])
                  S.op('dve', lambda e: e.tensor_tensor(tB[:], tA[:], c123[:, 1, :, :], ALU.subtract), reads=['tA', 'c2'], writes=['tB'])
                  S.op('dve', lambda e: e.tensor_copy(c123[:, 2, :, :], tB[:]), reads=['tB'], writes=['c3'])
                  flush()
                  stage_done('B', [('cfx', cfx[:].rearrange("p t n -> p (t n)"), 128, NT * 8)])


              if not skip_gdn:
                with ExitStack() as pg:
                  wz = sb("wz", [128, 8, 512], BF16, pg)
                  gate = sb("gate", [128, NT, 512], BF16, pg)
                  gnw4 = sb("gnw4", [128, 512], F32, pg)
                  smv = sb("smv", [128, 64], F32, pg)
                  dtb = smv[:, 0:4]
                  alg = smv[:, 4:8]
                  cw = sb("cw", [128, 12, 4], F32, pg)
                  sA = sb("sA", [128, NT, 4], F32, pg); sB = sb("sB", [128, NT, 4], F32, pg); sC = sb("sC", [128, NT, 4], F32, pg)
                  beta = sb("beta", [128, NT, 4], F32, pg); gc = sb("gc", [128, NT, 4], F32, pg)
                  egl = sb("egl", [128, NT, 4], F32, pg); edec = sb("edec", [128, NT, 4], F32, pg); kbs = sb("kbs", [128, NT, 4], F32, pg)
                  wq3 = [sb("wq3_%d" % i, [128, 8, 128], BF16, pg) for i in range(3)]
                  pre = sb("gpre", [128, 4 + S_LEN], BF16, pg)
                  dg = sb("dg", [128, 4, 128], BF16, pg)
                  qf = sb("qf", [128, S_LEN], F32, pg); kf = sb("kf", [128, S_LEN], F32, pg); vfb = sb("vfb", [128, S_LEN], BF16, pg)
                  sqb = [sb("sqb%d" % i, [128, 512], BF16, pg) for i in range(2)]
                  rsb = [sb("rsb%d" % i, [128, 512], F32, pg) for i in range(2)]
                  onesb = cb[:, 2, :]
                  rs = sb("rs", [128, 512], F32, pg)
                  qTb = sb("qTb", [128, S_LEN], BF16, pg); kTb = sb("kTb", [128, S_LEN], BF16, pg)
                  kbg = sb("kbg", [128, NT, 128], BF16, pg); vb = sb("vb", [128, NT, 128], BF16, pg)
                  kdec = [sb("kdec%d" % i, [128, NT, 128], BF16, pg) for i in range(2)]
                  u_ = [sb("u_%d" % i, [128, NT, 128], BF16, pg) for i in range(2)]
                  wT = [sb("wT%d" % i, [128, NT, 128], BF16, pg) for i in range(2)]
                  qkm = [sb("qkm%d" % i, [128, NT, 128], BF16, pg) for i in range(2)]
                  qgT = [sb("qgT%d" % i, [128, S_LEN], BF16, pg) for i in range(2)]
                  NI = 4
                  dgc = [sb("dgc%d" % i, [128, 128], F32, pg) for i in range(NI)]
                  t1 = [sb("t1_%d" % i, [128, 128], F32, pg) for i in range(NI)]
                  t2 = [sb("t2_%d" % i, [128, 128], F32, pg) for i in range(NI)]
                  eg = [sb("eg%d" % i, [128, 128], F32, pg) for i in range(NI)]
                  Pm = [[sb("Pm%d_%d" % (i, j), [128, 128], BF16, pg) for j in range(2)] for i in range(NI)]
                  Qm = [[sb("Qm%d_%d" % (i, j), [128, 128], BF16, pg) for j in range(2)] for i in range(NI)]
                  Ym = [[sb("Ym%d_%d" % (i, j), [128, 128], BF16, pg) for j in range(2)] for i in range(NI)]
                  Uo16 = [sb("Uo16_%d" % i, [128, 128], BF16, pg) for i in range(NI)]
                  Uo32 = [sb("Uo32_%d" % i, [128, 128], BF16, pg) for i in range(NI)]
                  Lo64 = [sb("Lo64_%d" % i, [128, 128], BF16, pg) for i in range(NI)]
                  bd16_b = cb[:, 7, :]; off16_b = cb[:, 8, :]; off32_b = cb[:, 9, :]; off64_b = cb[:, 10, :]
                  Sf = [sb("Sf%d" % j, [128, 128], F32, pg) for j in range(2)]; Sb = [sb("Sb%d" % j, [128, 128], BF16, pg) for j in range(2)]
                  vn = [[sb("vn%d_%d" % (j, i), [128, 128], BF16, pg) for i in range(2)] for j in range(2)]
                  og = [[sb("og%d_%d" % (j, i), [128, 128], BF16, pg) for i in range(2)] for j in range(2)]
                  ost = [[sb("ost%d_%d" % (j, i), [128, 4], F32, pg) for i in range(2)] for j in range(2)]
                  ojk = [[sb("ojk%d_%d" % (j, i), [128, 128], BF16, pg) for i in range(2)] for j in range(2)]
                  S.dma('pool', wz[:], w_in[:, C_GZ:C_GZ + 512].rearrange("(k p) n -> p k n", p=128), writes=['wz'])
                  for i in range(4):
                      S.dma('sp', gnw4[:, i * 128:(i + 1) * 128], gdn_nw.partition_broadcast(128), writes=['gnw4'])
                  S.dma('sp', smv[:], smallv.partition_broadcast(128), writes=['dtb', 'alg'])
                  S.dma('sp', cw[:].rearrange("p a b -> p (a b)"), conv_wT, writes=['cw'])
                  flush(); stage_done('G00')
                  for t in range(NT):
                      S.op('dve', lambda e, t=t: e.tensor_tensor(sA[:, t, :], scal[:, t, 0:4], dtb, ALU.add), reads=['scal', 'dtb'], writes=['sA'])
                  S.op('act', lambda e: e.activation(sB[:], sA[:], AF.Abs), reads=['sA'], writes=['sB'])
                  S.op('act', lambda e: e.activation(sB[:], sB[:], AF.Exp, scale=-1.0), reads=['sB'], writes=['sB'])
                  S.op('act', lambda e: e.activation(sB[:], sB[:], AF.Ln, bias=1.0), reads=['sB'], writes=['sB'])
                  S.op('dve', lambda e: e.tensor_single_scalar(sC[:], sA[:], 0.0, ALU.max), reads=['sA'], writes=['sC'])
                  S.op('dve', lambda e: e.tensor_tensor(sA[:], sC[:], sB[:], ALU.add), reads=['sB', 'sC'], writes=['sA'])
                  S.op('act', lambda e: e.activation(alg, alg, AF.Exp), reads=['alg'], writes=['alg'])
                  S.op('dve', lambda e: e.tensor_scalar(alg, alg, -1.0, None, ALU.mult), reads=['alg'], writes=['alg'])
                  for t in range(NT):
                      S.op('dve', lambda e, t=t: e.tensor_tensor(sA[:, t, :], sA[:, t, :], alg, ALU.mult), reads=['sA', 'alg'], writes=['sA'])
                  flush(); stage_done('G0m')
                  g2 = sA[:].rearrange("p t n -> p (t n)")
                  S.op('pe', lambda e: e.matmul(ps[3][:, 0:64], trif, g2, start=True, stop=True), reads=['sA', 'cf'], writes=['ps3'])
                  S.op('pe', lambda e: e.matmul(ps[3][:, 64:128], onesf, g2, start=True, stop=True), reads=['sA', 'cf'], writes=['ps3'])
                  S.op('dve', lambda e: e.tensor_copy(gc[:].rearrange("p t n -> p (t n)"), ps[3][:, 0:64]), reads=['ps3'], writes=['gc'])
                  S.op('act', lambda e: e.activation(egl[:].rearrange("p t n -> p (t n)"), ps[3][:, 64:128], AF.Exp), reads=['ps3'], writes=['egl'])
                  S.op('dve', lambda e: e.tensor_tensor(edec[:].rearrange("p t n -> p (t n)"), ps[3][:, 64:128], gc[:].rearrange("p t n -> p (t n)"), ALU.subtract),
                       reads=['ps3', 'gc'], writes=['edec'])
                  S.op('act', lambda e: e.activation(edec[:], edec[:], AF.Exp), reads=['edec'], writes=['edec'])
                  flush(); stage_done('G0n')
                  for t in range(NT):
                      S.op('dve', lambda e, t=t: e.tensor_copy(sC[:, t, :], scal[:, t, 4:8]), reads=['scal', 'sC'], writes=['sC'])
                  S.op('act', lambda e: e.activation(beta[:], sC[:], AF.Sigmoid), reads=['sC'], writes=['beta'])
                  S.op('act', lambda e: e.activation(kbs[:], gc[:], AF.Exp), reads=['gc'], writes=['kbs'])
                  S.op('dve', lambda e: e.tensor_tensor(kbs[:], kbs[:], beta[:], ALU.mult), reads=['kbs', 'beta'], writes=['kbs'])
                  flush(); stage_done('G0a')
                  for t in range(NT):
                      bank = t % 2
                      for kc in range(8):
                          S.op('pe', lambda e, t=t, kc=kc, bank=bank: e.matmul(ps[bank][:, :], xnT[:, kc, t * 128:(t + 1) * 128], wz[:, kc, :], start=(kc == 0), stop=(kc == 7)),
                               reads=['xnT%d' % t, 'wz'], writes=['ps%d' % bank], sig=(kc == 7))
                      S.op('act', lambda e, t=t, bank=bank: e.activation(rs[:, :], ps[bank][:, :], AF.Silu), reads=['ps%d' % bank], writes=['rs'])
                      S.op('dve', lambda e, t=t: e.tensor_tensor(gate[:, t, :], rs[:, :], gnw4[:, :], ALU.mult), reads=['rs', 'gnw4'], writes=['gate%d' % t])
                  S.op('pool', lambda e: e.memset(pre[:, 0:4], 0.0), writes=['pre_pad'])
                  flush()
                  stage_done('G1')
                  def gdn_prep(h, si):
                      G = 'g%d_' % si
                      for wi, (cc, dst, dkey) in enumerate(((h, qf, 'qf'), (4 + h, kf, 'kf'), (8 + h, vfb, 'vfb'))):
                          S.dma('pool', wq3[wi][:], w_in[:, cc * 128:(cc + 1) * 128].rearrange("(k p) n -> p k n", p=128), writes=['wq3_%d' % wi])
                          for k in range(4):
                              S.op('dve', lambda e, k=k, cc=cc: e.tensor_scalar(dg[:, k, :], identf, cw[:, cc, k:k + 1], None, ALU.mult), reads=['cf', 'cw'], writes=['dg'])
                          for tb in range(4):
                              bank = 2 + tb % 2
                              for kc in range(8):
                                  S.op('pe', lambda e, wi=wi, kc=kc, tb=tb, bank=bank: e.matmul(ps[bank][:, :], wq3[wi][:, kc, :], xnT[:, kc, tb * 512:(tb + 1) * 512], start=(kc == 0), stop=(kc == 7)),
                                       reads=['xnT%d' % (4 * tb + i) for i in range(4)] + ['wq3_%d' % wi], writes=['ps%d' % bank], sig=(kc == 7))
                              copy_op(evac_engine(), pre[:, 4 + tb * 512:4 + (tb + 1) * 512], ps[bank][:, :], reads=['ps%d' % bank], writes=['ps%d' % bank, 'pre%d' % tb])
                              yield
                          for tb in range(4):
                              bank = 4 + tb % 2
                              rd = ['pre%d' % tb, 'dg', 'pre_pad'] + (['pre%d' % (tb - 1)] if tb > 0 else [])
                              for k in range(4):
                                  S.op('pe', lambda e, k=k, tb=tb, bank=bank: e.matmul(ps[bank][:, :], dg[:, k, :], pre[:, tb * 512 + k + 1:tb * 512 + k + 513], start=(k == 0), stop=(k == 3)),
                                       reads=rd, writes=['ps%d' % bank], sig=(k == 3))
                              S.op('act', lambda e, dst=dst, tb=tb, bank=bank: e.activation(dst[:, tb * 512:(tb + 1) * 512], ps[bank][:, :], AF.Silu),
                                   reads=['ps%d' % bank], writes=['ps%d' % bank, dkey + '%d' % tb])
                              yield
                      for (src, skey, dstb, dbkey, scl) in ((qf, 'qf', qTb, 'qTb', 128.0 ** -0.5), (kf, 'kf', kTb, 'kTb', 1.0)):
                          def l2_front(tb, src=src, skey=skey):
                              cs = slice(tb * 512, (tb + 1) * 512)
                              bb = tb % 2
                              S.op('act', lambda e: e.activation(sqb[bb][:, :], src[:, cs], AF.Square), reads=[skey + '%d' % tb], writes=['sqb%d' % bb])
                              S.op('pe', lambda e: e.matmul(ps[6 + bb][:, :], onesb, sqb[bb][:, :], start=True, stop=True), reads=['sqb%d' % bb, 'cb'], writes=['ps%d' % (6 + bb)])

                          def l2_back(tb, src=src, skey=skey, dstb=dstb, dbkey=dbkey, scl=scl):
                              cs = slice(tb * 512, (tb + 1) * 512)
                              bb = tb % 2
                              S.op('act', lambda e: e.activation(rsb[bb][:, :], ps[6 + bb][:, :], AF.Ln, bias=EPS), reads=['ps%d' % (6 + bb)], writes=['ps%d' % (6 + bb), 'rsb%d' % bb])
                              S.op('act', lambda e: e.activation(rsb[bb][:, :], rsb[bb][:, :], AF.Exp, scale=-0.5), reads=['rsb%d' % bb], writes=['rsb%d' % bb])
                              S.op('dve', lambda e: e.scalar_tensor_tensor(dstb[:, cs], src[:, cs], scl, rsb[bb][:, :], ALU.mult, ALU.mult),
                                   reads=[skey + '%d' % tb, 'rsb%d' % bb], writes=[dbkey + '%d' % tb])
                          l2_front(0)
                          for tb in range(4):
                              if tb + 1 < 4:
                                  l2_front(tb + 1)
                              l2_back(tb)
                              yield
                      for t in range(NT):
                          ts_ = slice(t * 128, (t + 1) * 128)
                          bk, bv = 2 + t % 2, 4 + t % 2
                          kb_, vb_ = 'ps%d' % bk, 'ps%d' % bv
                          pbk = psb(bk)
                          S.op('pe', lambda e, ts_=ts_, pbk=pbk: e.transpose(pbk[:, 0:128], kTb[:, ts_], identb), reads=['kTb%d' % (t // 4), 'cb'], writes=[kb_])
                          S.op('act', lambda e, t=t, pbk=pbk: e.activation(kbg[:, t, :], pbk[:, 0:128], AF.Copy, scale=kbs[:, t, h:h + 1]), reads=[kb_, 'kbs'], writes=[kb_, 'kbg%d' % t])
                          S.op('dve', lambda e, t=t, pbk=pbk: e.tensor_scalar(kdec[si][:, t, :], pbk[:, 0:128], edec[:, t, h:h + 1], None, ALU.mult), reads=[kb_, 'edec'], writes=[kb_, G + 'kdec%d' % t])
                          pbv = psb(bv)
                          S.op('pe', lambda e, ts_=ts_, pbv=pbv: e.transpose(pbv[:, 0:128], vfb[:, ts_], identb), reads=['vfb%d' % (t // 4), 'cb'], writes=[vb_])
                          S.op('dve', lambda e, t=t, pbv=pbv: e.tensor_scalar(vb[:, t, :], pbv[:, 0:128], beta[:, t, h:h + 1], None, ALU.mult), reads=[vb_, 'beta'], writes=[vb_, 'vb%d' % t])
                          if t % 2 == 1:
                              yield
                      for t0 in range(0, NT, NI):
                          def setup_front(ii):
                              t = t0 + ii
                              ts_ = slice(t * 128, (t + 1) * 128)
                              I = str(ii)
                              gcol = gc[:, t, h:h + 1]
                              S.op('dve', lambda e, ii=ii, gcol=gcol: e.tensor_scalar(dgc[ii][:, :], identf, gcol, None, ALU.mult), reads=['cf', 'gc'], writes=['dgc' + I])
                              S.op('pe', lambda e, ii=ii: e.matmul(ps[ii][:, 0:128], onesf, dgc[ii][:, :], start=True, stop=True), reads=['dgc' + I, 'cf'], writes=['ps' + I])
                              Gp = ps[ii][:, 0:128]
                              S.op('pe', lambda e, ii=ii, ts_=ts_: e.matmul(ps[ii][:, 128:256], kTb[:, ts_], kTb[:, ts_], start=True, stop=True), reads=['kTb%d' % (t // 4)], writes=['ps' + I])
                              S.op('pe', lambda e, ii=ii, ts_=ts_: e.matmul(ps[ii][:, 256:384], kTb[:, ts_], qTb[:, ts_], start=True, stop=True), reads=['kTb%d' % (t // 4), 'qTb%d' % (t // 4)], writes=['ps' + I])
                              S.op('dve', lambda e, ii=ii, gcol=gcol, Gp=Gp: e.tensor_scalar(t2[ii][:, :], Gp, gcol, 0.0, ALU.subtract, ALU.min), reads=['ps' + I, 'gc'], writes=['ps' + I, 't2' + I])
                              S.op('act', lambda e, ii=ii: e.activation(t2[ii][:, :], t2[ii][:, :], AF.Exp), reads=['t2' + I], writes=['t2' + I])
                              S.op('pool', lambda e, ii=ii: e.tensor_tensor(t2[ii][:, :], t2[ii][:, :], mask_le_f, ALU.mult), reads=['t2' + I, 'cf'], writes=['t2' + I])
                              S.op('dve', lambda e, ii=ii, gcol=gcol, Gp=Gp: e.tensor_scalar(t1[ii][:, :], Gp, gcol, 0.0, ALU.subtract, ALU.max), reads=['ps' + I, 'gc'], writes=['ps' + I, 't1' + I])
                              S.op('act', lambda e, ii=ii: e.activation(t1[ii][:, :], t1[ii][:, :], AF.Exp, scale=-1.0), reads=['t1' + I], writes=['t1' + I])
                              S.op('pool', lambda e, ii=ii: e.tensor_tensor(t1[ii][:, :], t1[ii][:, :], mask_gt_f, ALU.mult), reads=['t1' + I, 'cf'], writes=['t1' + I])
                              S.op('act', lambda e, ii=ii, Gp=Gp: e.activation(eg[ii][:, :], Gp, AF.Exp), reads=['ps' + I], writes=['ps' + I, 'eg' + I])
                              S.op('pool', lambda e, ii=ii, ts_=ts_: e.tensor_tensor(qgT[si][:, ts_], qTb[:, ts_], eg[ii][:, :], ALU.mult), reads=['eg' + I, 'qTb%d' % (t // 4)], writes=[G + 'qgT%d' % t])
                              S.op('dve', lambda e, ii=ii, t=t, h=h: e.scalar_tensor_tensor(Pm[ii][0][:, :], ps[ii][:, 128:256], beta[:, t, h:h + 1], t1[ii][:, :], ALU.mult, ALU.mult),
                                   reads=['ps' + I, 'beta', 't1' + I], writes=['ps' + I, 'P0_' + I])
                              S.op('dve', lambda e, ii=ii, t=t: e.tensor_tensor(qkm[si][:, t, :], ps[ii][:, 256:384], t2[ii][:, :], ALU.mult), reads=['ps' + I, 't2' + I], writes=['ps' + I, G + 'qkm%d' % t])
                              pbu = psb(ii)
                              S.op('pe', lambda e, ii=ii, pbu=pbu: e.transpose(pbu[:, 768:896], Pm[ii][0][:, :], identb), reads=['P0_' + I, 'cb'], writes=['ps' + I])
                              S.op('act', lambda e, ii=ii, pbu=pbu: e.activation(Qm[ii][0][:, :], pbu[:, 768:896], AF.Copy), reads=['ps' + I], writes=['ps' + I, 'Q0_' + I])
                          def setup_tail(ii):
                              I = str(ii)
                              S.op('pool', lambda e, ii=ii: e.tensor_tensor(Uo16[ii][:, :], Qm[ii][0][:, :], off16_b, ALU.mult), reads=['Q0_' + I, 'cb'], writes=['Uo16_' + I])
                              S.op('pool', lambda e, ii=ii: e.tensor_tensor(Uo32[ii][:, :], Qm[ii][0][:, :], off32_b, ALU.mult), reads=['Q0_' + I, 'cb'], writes=['Uo32_' + I])
                              S.op('pool', lambda e, ii=ii: e.tensor_tensor(Lo64[ii][:, :], Pm[ii][0][:, :], off64_b, ALU.mult), reads=['P0_' + I, 'cb'], writes=['Lo64_' + I])
                              S.op('pool', lambda e, ii=ii: e.tensor_tensor(Pm[ii][0][:, :], Pm[ii][0][:, :], bd16_b, ALU.mult), reads=['P0_' + I, 'cb'], writes=['P0_' + I])
                              S.op('pool', lambda e, ii=ii: e.tensor_tensor(Qm[ii][0][:, :], Qm[ii][0][:, :], bd16_b, ALU.mult), reads=['Q0_' + I, 'cb'], writes=['Q0_' + I])
                              S.op('dve', lambda e, ii=ii: e.tensor_tensor(Ym[ii][0][:, :], identb, Qm[ii][0][:, :], ALU.subtract), reads=['Q0_' + I, 'cb'], writes=['Y0_' + I])
                          lockstep(NI, setup_front)
                          lockstep(NI, setup_tail)
                          for lv in range(1, 4):
                              a, b = (lv - 1) % 2, lv % 2
                              for ii in range(NI):
                                  I = str(ii)
                                  IB = 'ps%d' % (4 + ii)
                                  S.op('pe', lambda e, ii=ii, a=a: e.matmul(ps[4 + ii][:, 0:128], Qm[ii][a][:, :], Pm[ii][a][:, :], start=True, stop=True), reads=['Q%d_' % a + I, 'P%d_' % a + I], writes=[IB])
                                  if lv <= 2:
                                      S.op('pe', lambda e, ii=ii, a=a: e.matmul(ps[4 + ii][:, 128:256], Pm[ii][a][:, :], Qm[ii][a][:, :], start=True, stop=True), reads=['Q%d_' % a + I, 'P%d_' % a + I], writes=[IB])
                                  S.op('act', lambda e, ii=ii, b=b: e.activation(Pm[ii][b][:, :], ps[4 + ii][:, 0:128], AF.Copy), reads=[IB], writes=[IB, 'P%d_' % b + I])
                                  if lv <= 2:
                                      S.op('dve', lambda e, ii=ii, b=b: e.tensor_copy(Qm[ii][b][:, :], ps[4 + ii][:, 128:256]), reads=[IB], writes=[IB, 'Q%d_' % b + I])
                              for ii in range(NI):
                                  I = str(ii)
                                  IB = 'ps%d' % (4 + ii)
                                  S.op('pe', lambda e, ii=ii, a=a, b=b: e.matmul(ps[4 + ii][:, 256:384], Pm[ii][b][:, :], Ym[ii][a][:, :], start=True, stop=True), reads=['P%d_' % b + I, 'Y%d_' % a + I], writes=[IB])
                                  S.op('dve', lambda e, ii=ii, a=a, b=b: e.tensor_tensor(Ym[ii][b][:, :], Ym[ii][a][:, :], ps[4 + ii][:, 256:384], ALU.add), reads=[IB, 'Y%d_' % a + I], writes=[IB, 'Y%d_' % b + I])
                          def tr_to(ii, src, skey, dst, dkey):
                              I = str(ii)
                              IB = 'ps%d' % (4 + ii)
                              pbt = psb(4 + ii)[:, 512:640]
                              S.op('pe', lambda e: e.transpose(pbt, src[:, :], identb), reads=[skey + I, 'cb'], writes=[IB])
                              S.op('act', lambda e: e.activation(dst[:, :], pbt, AF.Copy), reads=[IB], writes=[IB, dkey + I])
                          for ii in range(NI):
                              tr_to(ii, Ym[ii][1], 'Y1_', Pm[ii][0], 'P0_')
                          for (Uo, ukey, d_i, dt_i, m_i) in ((Uo16, 'Uo16_', 0, 1, 0), (Uo32, 'Uo32_', 1, 0, 1)):
                              for ii in range(NI):
                                  I = str(ii)
                                  IB = 'ps%d' % (4 + ii)
                                  S.op('pe', lambda e, ii=ii, Uo=Uo, d_i=d_i: e.matmul(ps[4 + ii][:, 0:128], Uo[ii][:, :], Pm[ii][d_i][:, :], start=True, stop=True), reads=[ukey + I, 'P%d_' % d_i + I], writes=[IB])
                                  S.op('act', lambda e, ii=ii, m_i=m_i: e.activation(Qm[ii][m_i][:, :], ps[4 + ii][:, 0:128], AF.Copy), reads=[IB], writes=[IB, 'Q%d_' % m_i + I])
                              for ii in range(NI):
                                  I = str(ii)
                                  IB = 'ps%d' % (4 + ii)
                                  S.op('pe', lambda e, ii=ii, dt_i=dt_i, m_i=m_i: e.matmul(ps[4 + ii][:, 128:256], Ym[ii][dt_i][:, :], Qm[ii][m_i][:, :], start=True, stop=True), reads=['Y%d_' % dt_i + I, 'Q%d_' % m_i + I], writes=[IB])
                                  S.op('dve', lambda e, ii=ii, d_i=d_i: e.tensor_tensor(Pm[ii][1 - d_i][:, :], Pm[ii][d_i][:, :], ps[4 + ii][:, 128:256], ALU.subtract), reads=[IB, 'P%d_' % d_i + I], writes=[IB, 'P%d_' % (1 - d_i) + I])
                              for ii in range(NI):
                                  tr_to(ii, Pm[ii][1 - d_i], 'P%d_' % (1 - d_i), Ym[ii][1 - dt_i], 'Y%d_' % (1 - dt_i))
                          for ii in range(NI):
                              I = str(ii)
                              IB = 'ps%d' % (4 + ii)
                              S.op('pe', lambda e, ii=ii: e.matmul(ps[4 + ii][:, 0:128], Lo64[ii][:, :], Ym[ii][1][:, :], start=True, stop=True), reads=['Lo64_' + I, 'Y1_' + I], writes=[IB])
                              S.op('act', lambda e, ii=ii: e.activation(Qm[ii][0][:, :], ps[4 + ii][:, 0:128], AF.Copy), reads=[IB], writes=[IB, 'Q0_' + I])
                          for ii in range(NI):
                              I = str(ii)
                              IB = 'ps%d' % (4 + ii)
                              S.op('pe', lambda e, ii=ii: e.matmul(ps[4 + ii][:, 128:256], Pm[ii][0][:, :], Qm[ii][0][:, :], start=True, stop=True), reads=['P0_' + I, 'Q0_' + I], writes=[IB])
                              S.op('dve', lambda e, ii=ii: e.tensor_tensor(Ym[ii][0][:, :], Ym[ii][1][:, :], ps[4 + ii][:, 128:256], ALU.subtract), reads=[IB, 'Y1_' + I], writes=[IB, 'Y0_' + I])
                          for ii in range(NI):
                              t = t0 + ii
                              I = str(ii)
                              IB = 'ps%d' % (4 + ii)
                              TT = Ym[ii][0]
                              S.op('pe', lambda e, ii=ii, TT=TT, t=t: e.matmul(ps[4 + ii][:, 0:128], TT[:, :], vb[:, t, :], start=True, stop=True), reads=['Y0_' + I, 'vb%d' % t], writes=[IB])
                              S.op('pe', lambda e, ii=ii, TT=TT, t=t: e.matmul(ps[4 + ii][:, 128:256], kbg[:, t, :], TT[:, :], start=True, stop=True), reads=['Y0_' + I, 'kbg%d' % t], writes=[IB])
                              S.op('act', lambda e, ii=ii, t=t: e.activation(u_[si][:, t, :], ps[4 + ii][:, 0:128], AF.Copy), reads=[IB], writes=[IB, G + 'u%d' % t])
                              S.op('dve', lambda e, ii=ii, t=t: e.tensor_copy(wT[si][:, t, :], ps[4 + ii][:, 128:256]), reads=[IB], writes=[IB, G + 'wT%d' % t])
                      yield

                  def gdn_scan_multi(heads):
                      for (h, si, j) in heads:
                          S.op('pool', lambda e, j=j: e.memset(Sf[j][:, :], 0.0), writes=['Sf%d' % j])
                          S.op('pool', lambda e, j=j: e.memset(Sb[j][:, :], 0.0), writes=['Sb%d' % j])

                      def scan_post(t, h, si, j):
                          ts_ = slice(t * 128, (t + 1) * 128)
                          b2 = t % 2
                          B2 = '%d_%d' % (j, b2)
                          pk = 'ps%d' % (4 * j + 1 + b2)
                          tk = 'ps%d' % (4 * j + 3)
                          po = ps[4 * j + 1 + b2][:, 0:128]
                          S.op('act', lambda e: e.activation(ojk[j][b2][:, :], po, AF.Square, accum_out=ost[j][b2][:, 0:1]), reads=[pk], writes=[pk, 'ojk' + B2, 'ost' + B2])
                          S.op('act', lambda e: e.activation(ost[j][b2][:, 1:2], ost[j][b2][:, 0:1], AF.Ln, bias=EPS, scale=1.0 / 128), reads=['ost' + B2], writes=['ost1' + B2])
                          S.op('act', lambda e: e.activation(ost[j][b2][:, 2:3], ost[j][b2][:, 1:2], AF.Exp, scale=-0.5), reads=['ost1' + B2], writes=['ost2' + B2])
                          S.op('dve', lambda e: e.scalar_tensor_tensor(og[j][b2][:, :], po, ost[j][b2][:, 2:3], gate[:, t, h * 128:(h + 1) * 128], ALU.mult, ALU.mult),
                               reads=[pk, 'ost2' + B2, 'gate%d' % t], writes=[pk, 'og' + B2])
                          pbo = psb(4 * j + 3)[:, b2 * 128:(b2 + 1) * 128]
                          S.op('pe', lambda e: e.transpose(pbo, og[j][b2][:, :], identb), reads=['og' + B2, 'cb'], writes=[tk])
                          copy_op('act', oT_g[:, h, ts_], pbo, reads=[tk], writes=[tk, 'oT_g%d_%d' % (h, t)])

                      for t in range(NT):
                          ts_ = slice(t * 128, (t + 1) * 128)
                          b2 = t % 2
                          for (h, si, j) in heads:
                              G = 'g%d_' % si
                              B2 = '%d_%d' % (j, b2)
                              k0 = 'ps%d' % (4 * j)
                              pk = 'ps%d' % (4 * j + 1 + b2)
                              S.op('pe', lambda e, t=t, si=si, j=j: e.matmul(ps[4 * j][:, 0:128], wT[si][:, t, :], Sb[j][:, :], start=True, stop=True), reads=[G + 'wT%d' % t, 'Sb%d' % j], writes=[k0])
                              S.op('dve', lambda e, t=t, si=si, j=j, b2=b2: e.tensor_tensor(vn[j][b2][:, :], u_[si][:, t, :], ps[4 * j][:, 0:128], ALU.subtract), reads=[G + 'u%d' % t, k0], writes=[k0, 'vn' + B2])
                              S.op('pe', lambda e, t=t, si=si, j=j, b2=b2: e.matmul(ps[4 * j][:, 128:256], kdec[si][:, t, :], vn[j][b2][:, :], start=True, stop=True), reads=[G + 'kdec%d' % t, 'vn' + B2], writes=[k0])
                              S.op('pe', lambda e, ts_=ts_, si=si, j=j, b2=b2: e.matmul(ps[4 * j + 1 + b2][:, 0:128], qgT[si][:, ts_], Sb[j][:, :], start=True, stop=False), reads=[G + 'qgT%d' % t, 'Sb%d' % j], writes=[pk], sig=False)
                              S.op('pe', lambda e, t=t, si=si, j=j, b2=b2: e.matmul(ps[4 * j + 1 + b2][:, 0:128], qkm[si][:, t, :], vn[j][b2][:, :], start=False, stop=True), reads=[G + 'qkm%d' % t, 'vn' + B2], writes=[pk])
                              S.op('dve', lambda e, t=t, h=h, j=j: e.scalar_tensor_tensor(Sf[j][:, :], Sf[j][:, :], egl[:, t, h:h + 1], ps[4 * j][:, 128:256], ALU.mult, ALU.add), reads=['Sf%d' % j, 'egl', k0], writes=[k0, 'Sf%d' % j])
                              S.op('pool', lambda e, j=j: e.tensor_copy(Sb[j][:, :], Sf[j][:, :]), reads=['Sf%d' % j], writes=['Sb%d' % j])
                          if t >= 1:
                              for (h, si, j) in heads:
                                  scan_post(t - 1, h, si, j)
                      for (h, si, j) in heads:
                          scan_post(NT - 1, h, si, j)

                  def drain(gen):
                      for _ in gen:
                          pass
                  for hp in range(2):
                      drain(gdn_prep(2 * hp, 0))
                      drain(gdn_prep(2 * hp + 1, 1))
                      gdn_scan_multi([(2 * hp, 0, 0), (2 * hp + 1, 1, 1)])
                  flush()
                  stage_done('G', [('oTg%d' % hh, oT_g[:, hh, :], 128, S_LEN) for hh in range(4)])
              oT_f = sb("oT_f", [64, 8, S_LEN], BF16, mx)
              with ExitStack() as pf:
                  wf = sb("wf", [128, 8, 1536], BF16, pf)
                  vaug2 = sb("vaug2", [128, NT, 584], BF16, pf)
                  vaug = vaug2[:, :, 0:520].rearrange("p t (h d) -> p t h d", h=8)
                  qTA = [sb("qTA%d" % i, [128, S_LEN], BF16, pf) for i in range(2)]
                  kTA = [sb("kTA%d" % i, [128, S_LEN], BF16, pf) for i in range(2)]
                  qTB = [sb("qTB%d" % i, [128, S_LEN], BF16, pf) for i in range(2)]
                  kTB = [sb("kTB%d" % i, [128, S_LEN], BF16, pf) for i in range(2)]
                  caqA = sb("caqA", [128, NT, 70], BF16, pf)
                  cakA = sb("cakA", [128, NT, 70], BF16, pf)
                  caqB = sb("caqB", [128, NT, 38], BF16, pf)
                  cakB = sb("cakB", [128, NT, 38], BF16, pf)
                  PT = [sb("PT%d" % i, [128, 512], BF16, pf) for i in range(3)]
                  osb = [sb("osb%d" % i, [65, 512], F32, pf) for i in range(2)]
                  rc = [sb("rc%d" % i, [65, 512], F32, pf) for i in range(2)]
                  for i3 in range(3):
                      S.dma('pool', wf[:, :, i3 * 512:(i3 + 1) * 512],
                            w_in[:, C_FQ + i3 * 512:C_FQ + (i3 + 1) * 512].rearrange("(k p) n -> p k n", p=128), writes=['wf%d' % i3])
                  S.op('pool', lambda e: e.memset(vaug[:, :, :, 64:65], 1.0), writes=['vaug1'])
                  S.op('pool', lambda e: e.memset(vaug2[:, :, 520:584], 0.0), writes=['vaug1'])
                  for i2 in range(2):
                      for (tl, nm, r0, r1) in ((qTA, 'qTA', 64, 128), (kTA, 'kTA', 64, 128), (qTB, 'qTB', 0, 64), (kTB, 'kTB', 0, 64)):
                          S.op('pool', lambda e, i2=i2, tl=tl, r0=r0, r1=r1: e.memset(tl[i2][r0:r1, :], 0.0), writes=['%s%db%d' % (nm, i2, tb_) for tb_ in range(4)])
                  for (c_, nm) in ((caqA, 'caqA'), (cakA, 'cakA'), (caqB, 'caqB'), (cakB, 'cakB')):
                      S.op('pool', lambda e, c_=c_: e.memset(c_[:], 0.0), writes=[nm])
                  S.op('pool', lambda e: e.memset(caqA[:, :, 67:70], 1.0), reads=['caqA'], writes=['caqA'])
                  S.op('pool', lambda e: e.memset(cakA[:, :, 64:67], 1.0), reads=['cakA'], writes=['cakA'])
                  S.op('pool', lambda e: e.memset(caqB[:, :, 35:38], 1.0), reads=['caqB'], writes=['caqB'])
                  S.op('pool', lambda e: e.memset(cakB[:, :, 32:35], 1.0), reads=['cakB'], writes=['cakB'])
                  for t in range(NT):
                      bank = t % 2
                      for kc in range(8):
                          S.op('pe', lambda e, t=t, kc=kc, bank=bank: e.matmul(ps[bank][:, :], xnT[:, kc, t * 128:(t + 1) * 128], wf[:, kc, 1024:1536],
                                                                               start=(kc == 0), stop=(kc == 7)),
                               reads=['xnT%d' % t, 'wf2'], writes=['ps%d' % bank], sig=(kc == 7))
                      copy_op(evac_engine(), vaug[:, t, :, 0:64], ps[bank][:, :].rearrange("p (h d) -> p h d", h=8),
                              reads=['ps%d' % bank], writes=['vaug%d' % t])
                  pending = []
                  pending0 = []
                  for hp in range(4):
                      pb_ = hp % 2
                      hA, hB = 2 * hp, 2 * hp + 1
                      qA, kA, qB, kB = qTA[pb_], kTA[pb_], qTB[pb_], kTB[pb_]
                      nqA, nkA, nqB, nkB = 'qTA%d' % pb_, 'kTA%d' % pb_, 'qTB%d' % pb_, 'kTB%d' % pb_
                      for j in range(3):
                          S.op('dve', lambda e, j=j, hA=hA: e.tensor_copy(caqA[:, :, 64 + j], c123[:, j, :, hA]), reads=['c%d' % (j + 1), 'caqA'], writes=['caqA'])
                          S.op('dve', lambda e, j=j, hA=hA: e.tensor_scalar(cakA[:, :, 67 + j], c123[:, j, :, hA], -1.0, None, ALU.mult), reads=['c%d' % (j + 1), 'cakA'], writes=['cakA'])
                          S.op('dve', lambda e, j=j, hB=hB: e.tensor_copy(caqB[:, :, 32 + j], c123[:, j, :, hB]), reads=['c%d' % (j + 1), 'caqB'], writes=['caqB'])
                          S.op('dve', lambda e, j=j, hB=hB: e.tensor_scalar(cakB[:, :, 35 + j], c123[:, j, :, hB], -1.0, None, ALU.mult), reads=['c%d' % (j + 1), 'cakB'], writes=['cakB'])
                      for tb in range(4):
                          cs = slice(tb * 512, (tb + 1) * 512)
                          xk = ['xnT%d' % (4 * tb + i) for i in range(4)]
                          for kc in range(8):
                              S.op('pe', lambda e, kc=kc, cs=cs, hA=hA: e.matmul(ps[2][:, :], wf[:, kc, hA * 64:hA * 64 + 128], xnT[:, kc, cs], start=(kc == 0), stop=(kc == 7)),
                                   reads=xk + ['wf0'], writes=['ps2'], sig=(kc == 7))
                          S.op('act', lambda e, cs=cs, qA=qA: e.activation(qA[0:64, cs], ps[2][0:64, :], AF.Copy, scale=0.125), reads=['ps2'], writes=['ps2', nqA + 'a%d' % tb])
                          S.op('dve', lambda e, cs=cs, qB=qB: e.tensor_scalar(qB[64:128, cs], ps[2][64:128, :], 0.125, None, ALU.mult), reads=['ps2'], writes=['ps2', nqB + 'a%d' % tb])
                          for kc in range(8):
                              S.op('pe', lambda e, kc=kc, cs=cs, hA=hA: e.matmul(ps[3][:, :], wf[:, kc, 512 + hA * 64:512 + hA * 64 + 128], xnT[:, kc, cs], start=(kc == 0), stop=(kc == 7)),
                                   reads=xk + ['wf1'], writes=['ps3'], sig=(kc == 7))
                          S.op('dve', lambda e, cs=cs, kA=kA: e.tensor_copy(kA[0:64, cs], ps[3][0:64, :]), reads=['ps3'], writes=['ps3', nkA + 'a%d' % tb])
                          S.op('act', lambda e, cs=cs, kB=kB: e.activation(kB[64:128, cs], ps[3][64:128, :], AF.Copy), reads=['ps3'], writes=['ps3', nkB + 'a%d' % tb])
                          for (src_c, M, r0, dstT, dname, bank, eng) in ((caqA, 70, 64, qA, nqA, 4, 'act'), (cakA, 70, 64, kA, nkA, 5, 'dve'),
                                                                     (caqB, 38, 32, qB, nqB, 4, 'act'), (cakB, 38, 32, kB, nkB, 5, 'dve')):
                              ckey = {id(caqA): 'caqA', id(cakA): 'cakA', id(caqB): 'caqB', id(cakB): 'cakB'}[id(src_c)]
                              for i in range(4):
                                  t = 4 * tb + i
                                  S.op('pe', lambda e, t=t, i=i, src_c=src_c, M=M, bank=bank: e.matmul(ps[bank][0:M, i * 128:(i + 1) * 128], src_c[:, t, :], identb, start=True, stop=True),
                                       reads=[ckey, 'cb'], writes=['ps%d' % bank], sig=(i == 3))
                              copy_op(eng, dstT[r0:r0 + 6, cs], ps[bank][r0:r0 + 6, :], reads=['ps%d' % bank], writes=['ps%d' % bank, dname + 'b%d' % tb])
                      for (h, qTh, kTh, qk, kk) in ((hA, qA, kA, nqA, nkA), (hB, qB, kB, nqB, nkB)):
                          qkeys = lambda qg: [qk + 'a%d' % qg, qk + 'b%d' % qg]
                          kkeys = lambda kt: [kk + 'a%d' % (kt // 4), kk + 'b%d' % (kt // 4)]
                          step = 0
                          for qg in range(4):
                              ob = 6 + (qg % 2)
                              nk = 4 * qg + 4
                              items = []
                              for kt in range(nk):
                                  j = kt - 4 * qg
                                  c0 = max(j, 0) * 128
                                  items.append((kt, j, c0))

                              def emit_qk(kt, j, c0, sbk, qg=qg, qTh=qTh, kTh=kTh):
                                  S.op('pe', lambda e: e.matmul(ps[sbk][:, c0:512], kTh[0:128, kt * 128:(kt + 1) * 128],
                                                                qTh[0:128, qg * 512 + c0:(qg + 1) * 512], start=True, stop=True),
                                       reads=kkeys(kt) + qkeys(qg), writes=['ps%d' % sbk])

                              def emit_exp_pv(kt, j, c0, sbk, pi, first, last, h=h, ob=ob):
                                  S.op('act', lambda e: e.activation(PT[pi][:, c0:512], ps[sbk][:, c0:512], AF.Exp),
                                       reads=['ps%d' % sbk], writes=['PT%d' % pi])
                                  if j >= 0:
                                      S.op('dve', lambda e: e.tensor_tensor(PT[pi][:, c0:c0 + 128], PT[pi][:, c0:c0 + 128], mask_le_b, ALU.mult),
                                           reads=['PT%d' % pi, 'cb'], writes=['PT%d' % pi])
                                  S.op('pe', lambda e: e.matmul(ps[ob][:, c0:512], vaug2[:, kt, h * 65:h * 65 + 128], PT[pi][:, c0:512], start=first, stop=last),
                                       reads=['PT%d' % pi, 'vaug%d' % kt, 'vaug1'], writes=['ps%d' % ob], sig=last)

                              emit_qk(*items[0], sbk=(step % 2))
                              for ii, it in enumerate(items):
                                  if ii + 1 < len(items):
                                      emit_qk(*items[ii + 1], sbk=((step + 1) % 2))
                                  if ii == 1:
                                      while pending0:
                                          pending0.pop(0)()
                                  if ii == 3:
                                      while pending:
                                          pending.pop(0)()
                                  for _d in range(NDUMMY):
                                      S.op('pe', lambda e: e.matmul(ps[5][:, 0:256], identb, wf[:, 0, 0:256], start=True, stop=True), reads=['cb', 'wf0'], writes=['ps5'], sig=False)
                                  emit_exp_pv(*it, sbk=(step % 2), pi=(step % 3), first=(ii == 0), last=(ii == len(items) - 1))
                                  step += 1
                              o_ = osb[qg % 2]; r_ = rc[qg % 2]
                              S.op('dve', lambda e, o_=o_, ob=ob: e.tensor_copy(o_[:, :], ps[ob][0:65, :]), reads=['ps%d' % ob], writes=['ps%d' % ob, 'osb%d' % (qg % 2)])

                              def fin0(o_=o_, r_=r_, qg=qg, h=h):
                                  S.op('act', lambda e: e.activation(r_[64:65, :], o_[64:65, :], AF.Ln), reads=['osb%d' % (qg % 2)], writes=['rc%d' % (qg % 2)])
                                  S.op('act', lambda e: e.activation(r_[64:65, :], r_[64:65, :], AF.Exp, scale=-1.0), reads=['rc%d' % (qg % 2)], writes=['rc%d' % (qg % 2)])
                              pending0.append(fin0)

                              def fin(o_=o_, r_=r_, qg=qg, h=h):
                                  S.op('pe', lambda e: e.matmul(ps[4][0:64, :], onesf[64:65, 0:64], r_[64:65, :], start=True, stop=True),
                                       reads=['rc%d' % (qg % 2), 'cf'], writes=['ps4'])
                                  S.op('dve', lambda e: e.tensor_tensor(oT_f[0:64, h, qg * 512:(qg + 1) * 512], o_[0:64, :], ps[4][0:64, :], ALU.mult),
                                       reads=['osb%d' % (qg % 2), 'ps4'], writes=['ps4', 'oT_f%d_%d' % (h, qg)])
                              pending.append(fin)
                  while pending0:
                      pending0.pop(0)()
                  while pending:
                      pending.pop(0)()
                  flush()
                  stage_done('F', [('oTf0', oT_f[0:64, 0, :], 64, S_LEN), ('oTf7', oT_f[0:64, 7, :], 64, S_LEN), ('qT', qTB[1][0:128, :], 128, S_LEN), ('kT', kTB[1][0:128, :], 128, S_LEN)])
              h_sb = es.enter_context(nc.sbuf_tensor("h_sb", [128, NT, D], F32, side="right"))
              with ExitStack() as po:
                  wo_g = sb("wo_g", [128, 4, D], BF16, po)
                  wo_f = sb("wo_f", [128, 4, D], BF16, po)
                  oT_p = sb("oT_p", [128, 4, S_LEN], BF16, po)
                  S.dma('pool', wo_g[:], w_out[0:512, :].rearrange("(k p) n -> p k n", p=128), writes=['wo_g'])
                  S.dma('pool', wo_f[:], w_out[512:1024, :].rearrange("(k p) n -> p k n", p=128), writes=['wo_f'])
                  for hh in range(8):
                      S.dma('sp', oT_p[(hh % 2) * 64:(hh % 2) * 64 + 64, hh // 2, :], oT_f[0:64, hh, :],
                            reads=['oT_f%d_%d' % (hh, qq) for qq in range(4)], writes=['oT_p%d_%d' % (hh // 2, hh % 2)])
                  for t in range(NT):
                      S.dma('sp', h_sb[:, t, :], x[t * 128:(t + 1) * 128, :], writes=['h%d' % t])
                      ts_ = slice(t * 128, (t + 1) * 128)
                      for nh in range(2):
                          bank = (2 * t + nh) % 4
                          ns = slice(nh * 512, (nh + 1) * 512)
                          steps = []
                          if not skip_gdn:
                              for g in range(4):
                                  steps.append((oT_g[:, g, ts_], wo_g[:, g, ns], ['oT_g%d_%d' % (g, t), 'wo_g']))
                          for hp in range(4):
                              steps.append((oT_p[:, hp, ts_], wo_f[:, hp, ns], ['oT_p%d_0' % hp, 'oT_p%d_1' % hp, 'wo_f']))
                          for si, (l_, r_, rd) in enumerate(steps):
                              S.op('pe', lambda e, l_=l_, r_=r_, si=si, bank=bank, n=len(steps): e.matmul(ps[bank][:, :], l_, r_, start=(si == 0), stop=(si == n - 1)),
                                   reads=rd, writes=['ps%d' % bank], sig=(si == len(steps) - 1))
                          S.op('dve', lambda e, t=t, ns=ns, bank=bank: e.tensor_tensor(h_sb[:, t, ns], h_sb[:, t, ns], ps[bank][:, :], ALU.add),
                               reads=['h%d' % t, 'ps%d' % bank], writes=['h%d' % t])
                  flush()
                  stage_done('O', [('h0', h_sb[:, 0, :], 128, D)])
          if True:
              with ExitStack() as pff:
                  hnT = sb("hnT", [128, 8, S_LEN], BF16, pff)
                  gF = sb("gF", [128, D], F32, pff)
                  wgu = sb("wgu", [128, 11, 8, 256], BF16, pff)
                  wdn = sb("wdn", [128, 11, D], BF16, pff)
                  actT = sb("actT", [128, 11, 512], BF16, pff)
                  sg = [sb("sg%d" % i, [128, 512], F32, pff) for i in range(2)]
                  S.dma('sp', gF[:], g_ffn.partition_broadcast(128), writes=['gF'])

                  def load_half(hf):
                      for jj in range(11):
                          j = hf * 11 + jj
                          S.dma('pool', wgu[:, jj, :, 0:128], w_gu[:, j * 128:(j + 1) * 128].rearrange("(k p) n -> p k n", p=128), writes=['wgu%d' % jj])
                          S.dma('pool', wgu[:, jj, :, 128:256], w_gu[:, DFF + j * 128:DFF + (j + 1) * 128].rearrange("(k p) n -> p k n", p=128), writes=['wgu%d' % jj])
                      for jj in range(11):
                          j = hf * 11 + jj
                          S.dma('pool', wdn[:, jj, :], w_dn[j * 128:(j + 1) * 128, :], writes=['wdn%d' % jj])
                  load_half(0)
                  def dstF(t, pb, bkey):
                      copy_op('act', hnT[:, :, t * 128:(t + 1) * 128], pb.rearrange("p (k t) -> p k t", k=8), reads=[bkey], writes=[bkey, 'hnT%d' % t])
                  norm_T_all('F', lambda t: h_sb[:, t, :], lambda t: 'h%d' % t, gF[:, :], 'gF', pff, dstF)
                  for hf in range(2):
                      if hf == 1:
                          load_half(1)
                      for tb in range(4):
                          cs = slice(tb * 512, (tb + 1) * 512)
                          hk = ['hnT%d' % (4 * tb + i) for i in range(4)]
                          for jj in range(11):
                              bg = 2 + (jj % 2) * 2
                              bu = bg + 1
                              for kc in range(8):
                                  S.op('pe', lambda e, jj=jj, kc=kc, cs=cs, bg=bg: e.matmul(ps[bg][:, :], wgu[:, jj, kc, 0:128], hnT[:, kc, cs], start=(kc == 0), stop=(kc == 7)),
                                       reads=hk + ['wgu%d' % jj], writes=['ps%d' % bg], sig=(kc == 7))
                              for kc in range(8):
                                  S.op('pe', lambda e, jj=jj, kc=kc, cs=cs, bu=bu: e.matmul(ps[bu][:, :], wgu[:, jj, kc, 128:256], hnT[:, kc, cs], start=(kc == 0), stop=(kc == 7)),
                                       reads=hk + ['wgu%d' % jj], writes=['ps%d' % bu], sig=(kc == 7))
                              s_ = sg[jj % 2]
                              S.op('act', lambda e, s_=s_, bg=bg: e.activation(s_[:, :], ps[bg][:, :], AF.Silu), reads=['ps%d' % bg], writes=['sg%d' % (jj % 2)])
                              S.op('dve', lambda e, s_=s_, bu=bu, jj=jj: e.tensor_tensor(actT[:, jj, :], s_[:, :], ps[bu][:, :], ALU.mult),
                                   reads=['sg%d' % (jj % 2), 'ps%d' % bu], writes=['actT%d' % jj])
                          for i in range(4):
                              t = 4 * tb + i
                              for nh in range(2):
                                  bank = nh
                                  ns = slice(nh * 512, (nh + 1) * 512)
                                  for jj in range(11):
                                      S.op('pe', lambda e, jj=jj, i=i, ns=ns, bank=bank: e.matmul(ps[bank][:, :], actT[:, jj, i * 128:(i + 1) * 128], wdn[:, jj, ns],
                                                                                                 start=(jj == 0), stop=(jj == 10)),
                                           reads=['actT%d' % jj, 'wdn%d' % jj], writes=['ps%d' % bank], sig=(jj == 10))
                                  S.op('dve', lambda e, t=t, ns=ns, bank=bank: e.tensor_tensor(h_sb[:, t, ns], h_sb[:, t, ns], ps[bank][:, :], ALU.add),
                                       reads=['h%d' % t, 'ps%d' % bank], writes=['h%d' % t])
                  flush()
                  stage_done('FFN', [('h0f', h_sb[:, 0, :], 128, D)])
              with ExitStack() as pp:
                  gP = sb("gP", [128, D], F32, pp)
                  gL = sb("gL", [128, D], F32, pp)
                  wpg = sb("wpg", [128, 8, D], BF16, pp)
                  wpp = sb("wpp", [128, 2, D], BF16, pp)
                  lnT = [sb("lnT%d" % i, [128, 8, 128], BF16, pp) for i in range(NT)]
                  pt = [sb("pt%d" % i, [128, 256], F32, pp) for i in range(3)]
                  ptb = [sb("ptb%d" % i, [128, 256], BF16, pp) for i in range(3)]
                  pT = [sb("pT%d" % i, [128, 2, 128], BF16, pp) for i in range(NT)]
                  sig_ = [sb("sig%d" % i, [128, D], F32, pp) for i in range(2)]
                  yo = [sb("yo%d" % i, [128, D], F32, pp) for i in range(2)]
                  fss = sb("fss", [128, NT], F32, pp); flv = sb("flv", [128, NT], F32, pp); frs = sb("frs", [128, NT], F32, pp)
                  fjunk = [sb("fjunk%d" % i, [128, D], BF16, pp) for i in range(2)]
                  S.dma('sp', gP[:], g_ple.partition_broadcast(128), writes=['gP'])
                  S.dma('sp', gL[:], g_fin.partition_broadcast(128), writes=['gL'])
                  S.dma('pool', wpg[:], w_pg.rearrange("(k p) n -> p k n", p=128), writes=['wpg'])
                  S.dma('pool', wpp[:], w_pp.rearrange("(k p) n -> p k n", p=128), writes=['wpp'])

                  def ple_back(t):
                      b3 = t
                      b = t % 2
                      for nh in range(2):
                          ns = slice(nh * 512, (nh + 1) * 512)
                          bg = 4 + nh
                          bq = 6 + nh
                          for kc in range(8):
                              S.op('pe', lambda e, kc=kc, ns=ns, bg=bg: e.matmul(ps[bg][:, :], lnT[b3][:, kc, :], wpg[:, kc, ns], start=(kc == 0), stop=(kc == 7)),
                                   reads=['lnT%d' % b3, 'wpg'], writes=['ps%d' % bg], sig=(kc == 7))
                          for kc in range(2):
                              S.op('pe', lambda e, kc=kc, ns=ns, bq=bq: e.matmul(ps[bq][:, :], pT[b3][:, kc, :], wpp[:, kc, ns], start=(kc == 0), stop=(kc == 1)),
                                   reads=['pT%d' % b3, 'wpp'], writes=['ps%d' % bq], sig=(kc == 1))
                          S.op('act', lambda e, ns=ns, bg=bg: e.activation(sig_[b][:, ns], ps[bg][:, :], AF.Sigmoid), reads=['ps%d' % bg], writes=['ps%d' % bg, 'sig%d_%d' % (b, nh)])
                          S.op('dve', lambda e, ns=ns, bq=bq: e.tensor_tensor(sig_[b][:, ns], sig_[b][:, ns], ps[bq][:, :], ALU.mult),
                               reads=['sig%d_%d' % (b, nh), 'ps%d' % bq], writes=['ps%d' % bq, 'sig%d_%d' % (b, nh)])
                          S.op('dve', lambda e, ns=ns: e.tensor_tensor(h_sb[:, t, ns], h_sb[:, t, ns], sig_[b][:, ns], ALU.add),
                               reads=['h%d' % t, 'sig%d_%d' % (b, nh)], writes=['h%d' % t])
                      S.op('act', lambda e: e.activation(fjunk[b][:, :], h_sb[:, t, :], AF.Square, accum_out=fss[:, t:t + 1]),
                           reads=['h%d' % t], writes=['fjunk%d' % b, 'fss%d' % t])
                      if t % 4 == 3:
                          g = t // 4
                          gs = slice(4 * g, 4 * g + 4)
                          S.op('act', lambda e: e.activation(flv[:, gs], fss[:, gs], AF.Ln, bias=EPS, scale=1.0 / D),
                               reads=['fss%d' % tt for tt in range(4 * g, 4 * g + 4)], writes=['flv%d' % g])
                          S.op('act', lambda e: e.activation(frs[:, gs], flv[:, gs], AF.Exp, scale=-0.5), reads=['flv%d' % g], writes=['frs%d' % g])
                          for tt in range(4 * g, 4 * g + 4):
                              bb = tt % 2
                              S.op('dve', lambda e, tt=tt, bb=bb: e.scalar_tensor_tensor(yo[bb][:, :], h_sb[:, tt, :], frs[:, tt:tt + 1], gL[:, :], ALU.mult, ALU.mult),
                                   reads=['h%d' % tt, 'frs%d' % g, 'gL'], writes=['yo%d' % bb])
                              S.dma('sp', y[tt * 128:(tt + 1) * 128, :], yo[bb][:, :], reads=['yo%d' % bb], is_out=True)

                  def dstP(t, pb, bkey):
                      b3 = t % 3
                      copy_op('act', lnT[t][:, :, :], pb.rearrange("p (k t) -> p k t", k=8), reads=[bkey], writes=[bkey, 'lnT%d' % t])
                      S.op('pool', lambda e: e.tensor_copy(ptb[b3][:, :], pt[b3][:, :]), reads=['pt%d' % b3], writes=['ptb%d' % b3])
                      if t + 2 < NT:
                          preP(t + 2)
                      bank = 2 + t % 2
                      pb2 = psb(bank)
                      for kc in range(2):
                          S.op('pe', lambda e, kc=kc: e.transpose(pb2[:, kc * 128:(kc + 1) * 128], ptb[b3][:, kc * 128:(kc + 1) * 128], identb),
                               reads=['ptb%d' % b3, 'cb'], writes=['ps%d' % bank], sig=(kc == 1))
                      copy_op('dve', pT[t][:, :, :], pb2[:, 0:256].rearrange("p (k t) -> p k t", k=2), reads=['ps%d' % bank], writes=['ps%d' % bank, 'pT%d' % t])

                  def preP(t):
                      S.dma('sp', pt[t % 3][:], p[t * 128:(t + 1) * 128, :], writes=['pt%d' % (t % 3)])
                  preP(0); preP(1)
                  norm_T_all('P', lambda t: h_sb[:, t, :], lambda t: 'h%d' % t, gP[:, :], 'gP', pp, dstP)
                  for t_ in range(NT):
                      ple_back(t_)
                  S.finish()
                  flush()
    return nc


_CACHE = {}


def kernel(**inputs):
    f32 = lambda a: np.ascontiguousarray(np.asarray(a, dtype=np.float32))
    x = f32(inputs['x']); p = f32(inputs['p'])
    B = x.shape[0]
    shared = {
        'w_in': f32(inputs['w_in'][0]), 'w_out': f32(inputs['w_out'][0]),
        'w_gate_up': f32(inputs['w_gate_up'][0]), 'w_down': f32(inputs['w_down'][0]),
        'w_ple_gate': f32(inputs['w_ple_gate'][0]), 'w_ple_proj': f32(inputs['w_ple_proj'][0]),
        'attn_norm_w': f32(inputs['attn_norm_w'][0]), 'ffn_norm_w': f32(inputs['ffn_norm_w'][0]),
        'ple_norm_w': f32(inputs['ple_norm_w'][0]), 'final_norm_w': f32(inputs['final_norm_w']),
        'conv_wT': f32(np.asarray(inputs['conv_w'][0]).T.reshape(12, 128, 4).transpose(1, 0, 2).reshape(128, 48)),
        'smallv': f32(np.concatenate([np.asarray(inputs['dt_bias'][0]).reshape(-1), np.asarray(inputs['a_log'][0]).reshape(-1), np.zeros(56)])),
        'gdn_norm_w': f32(inputs['gdn_norm_w'][0]), 'fox_f_bias': f32(inputs['fox_f_bias'][0]),
        'cst': make_consts(),
    }
    if 'nc' not in _CACHE:
        _CACHE['nc'] = build_program(skip_gdn=SKIP_GDN)
    nc = _CACHE['nc']
    in_maps = []
    for b in range(B):
        m = dict(shared)
        m['x'] = f32(x[b]); m['p'] = f32(p[0, b])
        in_maps.append(m)
    res = run_bass_kernel_spmd(nc, in_maps, core_ids=list(range(B)))
    return np.stack([np.asarray(r['y'], dtype=np.float32) for r in res.results], axis=0)


SKIP_GDN = False
NDUMMY = 2
```

```python
import numpy as np
from contextlib import ExitStack
import concourse.bass as bass
import concourse.mybir as mybir
from concourse.bass_utils import run_bass_kernel_spmd

F32 = mybir.dt.float32
BF16 = mybir.dt.bfloat16
AF = mybir.ActivationFunctionType
ALU = mybir.AluOpType


class Sched:
    ENGS = ('pe', 'act', 'dve', 'pool', 'sp')
    NDMA = {'sp': 8, 'pool': 8, 'act': 4}

    def __init__(self, nc, es):
        self.nc = nc
        self.sem = {}
        for k in self.ENGS:
            self.sem[k] = es.enter_context(nc.semaphore("sem_" + k))
        self.dpool = {}
        for q, n in self.NDMA.items():
            keys = []
            for i in range(n):
                key = "d_%s%d" % (q, i)
                self.sem[key] = es.enter_context(nc.semaphore(key))
                keys.append(key)
            self.dpool[q] = keys
        self.tot = {k: 0 for k in self.sem}
        self.seen = {k: {} for k in self.ENGS}
        self.prog = {k: [] for k in self.ENGS}
        self.last_w = {}
        self.readers = {}
        self.rr = {q: 0 for q in self.dpool}
        self.out_tokens = []
        self.nops = 0
        self.dead = False

    def _deps(self, reads, writes):
        deps = []
        for r in reads:
            t = self.last_w.get(r)
            if t:
                deps.append(t)
        for w in writes:
            t = self.last_w.get(w)
            if t:
                deps.append(t)
            deps.extend(self.readers.get(w, ()))
        return deps

    def _resolve(self, eng, deps):
        need = {}
        seen = self.seen[eng]
        for (k, v) in deps:
            if k == eng and eng == 'pe':
                continue
            if seen.get(k, 0) >= v:
                continue
            if need.get(k, 0) < v:
                need[k] = v
        for k, v in need.items():
            seen[k] = v
        return list(need.items())

    def _commit(self, token, reads, writes):
        for w in writes:
            self.last_w[w] = token
            self.readers[w] = []
        for r in reads:
            self.readers.setdefault(r, []).append(token)

    def op(self, eng, fn, reads=(), writes=(), sig=True):
        if self.dead:
            return
        waits = self._resolve(eng, self._deps(reads, writes))
        if sig:
            self.tot[eng] += 1
            token = (eng, self.tot[eng])
            inc = (eng, 1)
        else:
            token = (eng, self.tot[eng] + 1)
            inc = None
        self.prog[eng].append((waits, fn, inc))
        self._commit(token, reads, writes)
        self.nops += 1

    def dma(self, q, out, in_, reads=(), writes=(), is_out=False):
        if self.dead:
            return
        pool = self.dpool[q]
        key = pool[self.rr[q] % len(pool)]
        self.rr[q] += 1
        deps = self._deps(reads, writes)
        if self.tot[key] > 0:
            deps.append((key, self.tot[key]))
        waits = self._resolve(q, deps)
        self.tot[key] += 16
        token = (key, self.tot[key])
        self.prog[q].append((waits, (lambda e, o=out, i=in_: e.dma_start(out=o, in_=i)), (key, 16)))
        self._commit(token, reads, writes)
        if is_out:
            self.out_tokens.append(token)
        self.nops += 1

    def barrier(self):
        if self.dead:
            return
        allt = [(k, v) for k, v in self.tot.items() if v > 0]
        for e in self.ENGS:
            waits = self._resolve(e, allt)
            if waits:
                self.prog[e].append((waits, None, None))

    def finish(self):
        self.barrier()

    def emit(self, block):
        def mk(k):
            def f(e):
                for waits, fn, inc in self.prog[k]:
                    for (sk, v) in waits:
                        e.wait_ge(self.sem[sk], v)
                    if fn is not None:
                        ins = fn(e)
                        if inc is not None:
                            ins.then_inc(self.sem[inc[0]], inc[1])
            return f
        block.tensor(mk('pe'))
        block.scalar(mk('act'))
        block.vector(mk('dve'))
        block.gpsimd(mk('pool'))
        block.sync(mk('sp'))
        self.prog = {k: [] for k in self.ENGS}


S_LEN = 2048
D = 1024
NT = 16
DFF = 2816
NFF = 22
EPS = 1e-6
C_GQKV, C_GZ, C_GA, C_GB, C_FQ, C_FK, C_FV, C_FF = 0, 1536, 2048, 2052, 2056, 2568, 3080, 3592


def make_consts():
    c = np.zeros((128, 11, 128), np.float32)
    i = np.arange(128)
    c[:, 0, :] = np.eye(128)
    c[:, 1, :] = (i[:, None] <= i[None, :])
    c[:, 2, :] = 1.0
    c[:, 3, :] = (i[:, None] < i[None, :])
    c[:, 4, :] = 0.0
    c[127, 4, :] = 1.0
    c[:, 5, :] = (i[:, None] > i[None, :])
    c[:, 6, :] = -1.0 * (i[:, None] > i[None, :])
    c[:, 7, :] = (i[:, None] // 16 == i[None, :] // 16)
    for n_, sz in ((8, 16), (9, 32), (10, 64)):
        c[:, n_, :] = (i[:, None] // (2 * sz) == i[None, :] // (2 * sz)) & (i[:, None] // sz != i[None, :] // sz)
    return c.reshape(128, 11 * 128)


class _Stop(Exception):
    pass


def build_program(skip_gdn=False, dbg=(), stop_after=None):
    nc = bass.Bass("TRN2", target_bir_lowering=False)

    def din(name, shape):
        return nc.dram_tensor(name, list(shape), F32, kind="ExternalInput").ap()
    x = din("x", [S_LEN, D]); p = din("p", [S_LEN, 256])
    w_in = din("w_in", [D, 3600]); w_out = din("w_out", [D, D])
    w_gu = din("w_gate_up", [D, 2 * DFF]); w_dn = din("w_down", [DFF, D])
    w_pg = din("w_ple_gate", [D, D]); w_pp = din("w_ple_proj", [256, D])
    g_attn = din("attn_norm_w", [D]); g_ffn = din("ffn_norm_w", [D])
    g_ple = din("ple_norm_w", [D]); g_fin = din("final_norm_w", [D])
    conv_wT = din("conv_wT", [128, 12 * 4]); smallv = din("smallv", [64])
    gdn_nw = din("gdn_norm_w", [128]); fox_fb = din("fox_f_bias", [8])
    cst = din("cst", [128, 11 * 128])
    y = nc.dram_tensor("y", [S_LEN, D], F32, kind="ExternalOutput").ap()
    dbg_out = {}

    with ExitStack() as es:
        def sb(name, shape, dt, stack=None):
            try:
                return (stack or es).enter_context(nc.sbuf_tensor(name, list(shape), dt))
            except AssertionError:
                print("ALLOC FAIL", name, shape, dt, "live:", [(k, v) for k, v in alloc_log])
                raise
            finally:
                alloc_log.append((name, int(np.prod(shape[1:])) * (4 if dt == F32 else 2)))
        alloc_log = []
        ps = [es.enter_context(nc.psum_tensor("ps%d" % i, [128, 512], F32)) for i in range(8)]
        S = Sched(nc, es)

        def flush():
            S.barrier()
            with nc.Block() as block:
                S.emit(block)

        def stage_done(name, dumps=()):
            for (nm, ap2d, rows, cols) in dumps:
                if nm in dbg:
                    d = nc.dram_tensor("dbg_" + nm, [rows, cols], ap2d.dtype, kind="ExternalOutput").ap()
                    S.dma('sp', d, ap2d, is_out=True)
                    flush()
            if stop_after == name:
                S.finish()
                flush()
                S.dead = True

        cf = sb("cf", [128, 11, 128], F32)
        cb = sb("cb", [128, 11, 128], BF16)
        S.dma('sp', cf[:].rearrange("p a b -> p (a b)"), cst, writes=['cf'])
        S.dma('pool', cb[:].rearrange("p a b -> p (a b)"), cst, writes=['cb'])
        identb = cb[:, 0, :]; identf = cf[:, 0, :]; trif = cf[:, 1, :]; onesf = cf[:, 2, :]
        mask_le_b = cb[:, 1, :]; mask_lt_f = cf[:, 3, :]; mask_le_f = cf[:, 1, :]; sel127 = cf[:, 4, :]; mask_gt_f = cf[:, 5, :]; mask_gt_neg_f = cf[:, 6, :]

        def psb(i):
            return ps[i][:, :].bitcast(BF16)

        cnt = {'ev': 0}

        def evac_engine():
            cnt['ev'] += 1
            return 'act' if cnt['ev'] % 2 else 'dve'

        def copy_op(eng, out, in_, reads, writes):
            if eng == 'act':
                S.op('act', lambda e: e.activation(out, in_, AF.Copy), reads=reads, writes=writes)
            else:
                S.op(eng, lambda e: e.tensor_copy(out, in_), reads=reads, writes=writes)

        def rms_norm_T(src, src_key, gain, gain_key, dstT, dst_key, c0, scratch, tag, bank):
            junk, ssq, xs = scratch
            k = tag
            S.op('act', lambda e: e.activation(junk[:, :], src, AF.Square, accum_out=ssq[:, 0:1]),
                 reads=[src_key], writes=['junk' + k, 'ssq' + k])
            S.op('act', lambda e: e.activation(ssq[:, 1:2], ssq[:, 0:1], AF.Ln, bias=EPS, scale=1.0 / D),
                 reads=['ssq' + k], writes=['ssq1' + k])
            S.op('act', lambda e: e.activation(ssq[:, 2:3], ssq[:, 1:2], AF.Exp, scale=-0.5),
                 reads=['ssq1' + k], writes=['rstd' + k])
            S.op('dve', lambda e: e.scalar_tensor_tensor(xs[:, :], src, ssq[:, 2:3], gain, ALU.mult, ALU.mult),
                 reads=[src_key, 'rstd' + k, gain_key], writes=['xs' + k])
            pb = psb(bank)
            for kc in range(8):
                S.op('pe', lambda e, kc=kc: e.transpose(pb[:, kc * 128:(kc + 1) * 128], xs[:, kc * 128:(kc + 1) * 128], identb),
                     reads=['xs' + k, 'cb'], writes=['ps%d' % bank], sig=(kc == 7))
            copy_op(evac_engine(), dstT[:, :, c0:c0 + 128], pb.rearrange("p (k t) -> p k t", k=8),
                    reads=['ps%d' % bank], writes=[dst_key])


        def lockstep(n, body):
            recs = []
            for ii_ in range(n):
                rec = []
                S.op = lambda *a, _r=rec, **k: _r.append((a, k))
                try:
                    body(ii_)
                finally:
                    del S.op
                recs.append(rec)
            for k_ in range(max(len(r) for r in recs)):
                for r in recs:
                    if k_ < len(r):
                        a_, kw_ = r[k_]
                        S.op(*a_, **kw_)


        def norm_T_all(tag, src_of, srckey_of, gain, gain_key, stack, emit_dst, banks=(0, 1), pre_tile=None):
            ssq = sb("ssq_" + tag, [128, NT], F32, stack)
            lnv = sb("lnv_" + tag, [128, NT], F32, stack)
            rstd = sb("rstd_" + tag, [128, NT], F32, stack)
            junk = [sb("junk_%s%d" % (tag, i), [128, D], BF16, stack) for i in range(2)]
            xs = [sb("xs_%s%d" % (tag, i), [128, D], BF16, stack) for i in range(2)]

            def stats(g):
                for t in range(4 * g, 4 * g + 4):
                    if pre_tile is not None:
                        pre_tile(t)
                    S.op('act', lambda e, t=t: e.activation(junk[t % 2][:, :], src_of(t), AF.Square, accum_out=ssq[:, t:t + 1]),
                         reads=[srckey_of(t)], writes=['junk_%s%d' % (tag, t % 2), 'ssq_%s%d' % (tag, t)])
                gs = slice(4 * g, 4 * g + 4)
                S.op('act', lambda e: e.activation(lnv[:, gs], ssq[:, gs], AF.Ln, bias=EPS, scale=1.0 / D),
                     reads=['ssq_%s%d' % (tag, t) for t in range(4 * g, 4 * g + 4)], writes=['lnv_%s%d' % (tag, g)])
                S.op('act', lambda e: e.activation(rstd[:, gs], lnv[:, gs], AF.Exp, scale=-0.5),
                     reads=['lnv_%s%d' % (tag, g)], writes=['rstd_%s%d' % (tag, g)])

            def scale(t):
                b = t % 2
                S.op('dve', lambda e, t=t, b=b: e.scalar_tensor_tensor(xs[b][:, :], src_of(t), rstd[:, t:t + 1], gain, ALU.mult, ALU.mult),
                     reads=[srckey_of(t), 'rstd_%s%d' % (tag, t // 4), gain_key], writes=['xs_%s%d' % (tag, b)])

            def apply(g):
                for t in range(4 * g, 4 * g + 4):
                    b = t % 2
                    bank = banks[b]
                    if t == 0:
                        scale(0)
                    if t + 1 < NT:
                        scale(t + 1)
                    pb = psb(bank)
                    for kc in range(8):
                        S.op('pe', lambda e, kc=kc, b=b, pb=pb: e.transpose(pb[:, kc * 128:(kc + 1) * 128], xs[b][:, kc * 128:(kc + 1) * 128], identb),
                             reads=['xs_%s%d' % (tag, b), 'cb'], writes=['ps%d' % bank], sig=(kc == 7))
                    emit_dst(t, pb, 'ps%d' % bank)
            stats(0)
            for g in range(4):
                if g + 1 < 4:
                    stats(g + 1)
                apply(g)
            return rstd

        if True:
          with ExitStack() as mx:
              oT_g = sb("oT_g", [128, 4, S_LEN], BF16, mx)
              xnT = sb("xnT", [128, 8, S_LEN], BF16, mx)
              scal = sb("scal", [128, NT, 16], F32, mx)
              cfx = sb("cfx", [128, NT, 8], F32, mx)
              c123 = sb("c123", [128, 3, NT, 8], BF16, mx)
              with ExitStack() as pa:
                  gA = sb("gA", [128, D], F32, pa)
                  xall = sb("xall", [128, NT, D], F32, pa)
                  wsm = sb("wsm", [128, 8, 16], BF16, pa)
                  S.dma('sp', gA[:], g_attn.partition_broadcast(128), writes=['gA'])
                  S.dma('pool', wsm[:, :, 0:8], w_in[:, C_GA:C_GA + 8].rearrange("(k p) n -> p k n", p=128), writes=['wsm'])
                  S.dma('pool', wsm[:, :, 8:16], w_in[:, C_FF:C_FF + 8].rearrange("(k p) n -> p k n", p=128), writes=['wsm'])
                  for t in range(NT):
                      S.dma('sp', xall[:, t, :], x[t * 128:(t + 1) * 128, :], writes=['xall%d' % t])

                  def dstA(t, pb, bkey):
                      copy_op('act', xnT[:, :, t * 128:(t + 1) * 128], pb.rearrange("p (k t) -> p k t", k=8), reads=[bkey], writes=[bkey, 'xnT%d' % t])
                  norm_T_all('A', lambda t: xall[:, t, :], lambda t: 'xall%d' % t, gA[:, :], 'gA', pa, dstA)
                  for t in range(NT):
                      for kc in range(8):
                          S.op('pe', lambda e, t=t, kc=kc: e.matmul(ps[2][:, t * 16:(t + 1) * 16], xnT[:, kc, t * 128:(t + 1) * 128],
                                                                   wsm[:, kc, :], start=(kc == 0), stop=(kc == 7)),
                               reads=['xnT%d' % t, 'wsm'], writes=['ps2'], sig=(kc == 7 and t == NT - 1))
                  S.op('dve', lambda e: e.tensor_copy(scal[:].rearrange("p t n -> p (t n)"), ps[2][:, 0:NT * 16]),
                       reads=['ps2'], writes=['scal'])
                  flush()
                  stage_done('A', [('xnT', xnT[:, 0, :], 128, S_LEN), ('scal', scal[:].rearrange("p t n -> p (t n)"), 128, NT * 16)])
              with ExitStack() as pb_:
                  fbt = sb("fbt", [128, 8], F32, pb_)
                  tA = sb("tA", [128, NT, 8], F32, pb_)
                  tB = sb("tB", [128, NT, 8], F32, pb_)
                  tC = sb("tC", [128, NT, 8], F32, pb_)
                  pre = sb("pre", [128, NT, 8], F32, pb_)
                  S.dma('sp', fbt[:], fox_fb.partition_broadcast(128), writes=['fbt'])
                  ff = scal[:, :, 8:16]
                  for t in range(NT):
                      S.op('dve', lambda e, t=t: e.tensor_tensor(tA[:, t, :], scal[:, t, 8:16], fbt[:, :], ALU.add),
                           reads=['scal', 'fbt'], writes=['tA'])
                  S.op('act', lambda e: e.activation(tB[:], tA[:], AF.Abs), reads=['tA'], writes=['tB'])
                  S.op('act', lambda e: e.activation(tB[:], tB[:], AF.Exp, scale=-1.0), reads=['tB'], writes=['tB'])
                  S.op('act', lambda e: e.activation(tB[:], tB[:], AF.Ln, bias=1.0), reads=['tB'], writes=['tB'])
                  S.op('dve', lambda e: e.tensor_single_scalar(tC[:], tA[:], 0.0, ALU.min), reads=['tA'], writes=['tC'])
                  S.op('dve', lambda e: e.tensor_tensor(tA[:], tC[:], tB[:], ALU.subtract), reads=['tB', 'tC'], writes=['tA'])
                  flush(); stage_done('B1')
                  lf2 = tA[:].rearrange("p t n -> p (t n)")
                  S.op('pe', lambda e: e.matmul(ps[3][:, 0:128], trif, lf2, start=True, stop=True), reads=['tA', 'cf'], writes=['ps3'])
                  S.op('pe', lambda e: e.matmul(ps[3][:, 128:256], onesf, lf2, start=True, stop=True), reads=['tA', 'cf'], writes=['ps3'])
                  S.op('dve', lambda e: e.tensor_copy(tB[:].rearrange("p t n -> p (t n)"), ps[3][:, 0:128]), reads=['ps3'], writes=['tB'])
                  S.op('act', lambda e: e.activation(tC[:].rearrange("p t n -> p (t n)"), ps[3][:, 128:256], AF.Copy), reads=['ps3'], writes=['tC'])
                  flush(); stage_done('B2')
                  S.op('dve', lambda e: e.memset(pre[:, 0, :], 0.0), writes=['pre'])
                  for t in range(1, NT):
                      S.op('dve', lambda e, t=t: e.tensor_tensor(pre[:, t, :], pre[:, t - 1, :], tC[:, t - 1, :], ALU.add),
                           reads=['pre', 'tC'], writes=['pre'])
                  S.op('dve', lambda e: e.tensor_tensor(cfx[:], tB[:], pre[:], ALU.add), reads=['tB', 'pre'], writes=['cfx'])
                  flush(); stage_done('B3')
                  S.op('dve', lambda e: e.tensor_copy(c123[:, 0, :, :], cfx[:]), reads=['cfx'], writes=['c1'])
                  S.op('dve', lambda e: e.tensor_tensor(tA[:], cfx[:], c123[:, 0, :, :], ALU.subtract), reads=['cfx', 'c1'], writes=['tA'])
                  S.op('dve', lambda e: e.tensor_copy(c123[:, 1, :, :], tA[:]), reads=['tA'], writes=['c2'])
                  S.op('dve', lambda e: e.tensor_tensor(tB[:], tA[:], c123[:, 1, :, :], ALU.subtract), reads=['tA', 'c2'], writes=['tB'])
                  S.op('dve', lambda e: e.tensor_copy(c123[:, 2, :, :], tB[:]), reads=['tB'], writes=['c3'])
                  flush()
                  stage_done('B', [('cfx', cfx[:].rearrange("p t n -> p (t n)"), 128, NT * 8)])


              if not skip_gdn:
                with ExitStack() as pg:
                  wz = sb("wz", [128, 8, 512], BF16, pg)
                  gate = sb("gate", [128, NT, 512], BF16, pg)
                  gnw4 = sb("gnw4", [128, 512], F32, pg)
                  smv = sb("smv", [128, 64], F32, pg)
                  dtb = smv[:, 0:4]
                  alg = smv[:, 4:8]
                  cw = sb("cw", [128, 12, 4], F32, pg)
                  sA = sb("sA", [128, NT, 4], F32, pg); sB = sb("sB", [128, NT, 4], F32, pg); sC = sb("sC", [128, NT, 4], F32, pg)
                  beta = sb("beta", [128, NT, 4], F32, pg); gc = sb("gc", [128, NT, 4], F32, pg)
                  egl = sb("egl", [128, NT, 4], F32, pg); edec = sb("edec", [128, NT, 4], F32, pg); kbs = sb("kbs", [128, NT, 4], F32, pg)
                  wq3 = [sb("wq3_%d" % i, [128, 8, 128], BF16, pg) for i in range(3)]
                  pre = sb("gpre", [128, 4 + S_LEN], BF16, pg)
                  dg = sb("dg", [128, 4, 128], BF16, pg)
                  qf = sb("qf", [128, S_LEN], F32, pg); kf = sb("kf", [128, S_LEN], F32, pg); vfb = sb("vfb", [128, S_LEN], BF16, pg)
                  sqb = [sb("sqb%d" % i, [128, 512], BF16, pg) for i in range(2)]
                  rsb = [sb("rsb%d" % i, [128, 512], F32, pg) for i in range(2)]
                  onesb = cb[:, 2, :]
                  rs = sb("rs", [128, 512], F32, pg)
                  qTb = sb("qTb", [128, S_LEN], BF16, pg); kTb = sb("kTb", [128, S_LEN], BF16, pg)
                  kbg = sb("kbg", [128, NT, 128], BF16, pg); vb = sb("vb", [128, NT, 128], BF16, pg)
                  kdec = [sb("kdec%d" % i, [128, NT, 128], BF16, pg) for i in range(2)]
                  u_ = [sb("u_%d" % i, [128, NT, 128], BF16, pg) for i in range(2)]
                  wT = [sb("wT%d" % i, [128, NT, 128], BF16, pg) for i in range(2)]
                  qkm = [sb("qkm%d" % i, [128, NT, 128], BF16, pg) for i in range(2)]
                  qgT = [sb("qgT%d" % i, [128, S_LEN], BF16, pg) for i in range(2)]
                  NI = 4
                  dgc = [sb("dgc%d" % i, [128, 128], F32, pg) for i in range(NI)]
                  t1 = [sb("t1_%d" % i, [128, 128], F32, pg) for i in range(NI)]
                  t2 = [sb("t2_%d" % i, [128, 128], F32, pg) for i in range(NI)]
                  eg = [sb("eg%d" % i, [128, 128], F32, pg) for i in range(NI)]
                  Pm = [[sb("Pm%d_%d" % (i, j), [128, 128], BF16, pg) for j in range(2)] for i in range(NI)]
                  Qm = [[sb("Qm%d_%d" % (i, j), [128, 128], BF16, pg) for j in range(2)] for i in range(NI)]
                  Ym = [[sb("Ym%d_%d" % (i, j), [128, 128], BF16, pg) for j in range(2)] for i in range(NI)]
                  Uo16 = [sb("Uo16_%d" % i, [128, 128], BF16, pg) for i in range(NI)]
                  Uo32 = [sb("Uo32_%d" % i, [128, 128], BF16, pg) for i in range(NI)]
                  Lo64 = [sb("Lo64_%d" % i, [128, 128], BF16, pg) for i in range(NI)]
                  bd16_b = cb[:, 7, :]; off16_b = cb[:, 8, :]; off32_b = cb[:, 9, :]; off64_b = cb[:, 10, :]
                  Sf = [sb("Sf%d" % j, [128, 128], F32, pg) for j in range(2)]; Sb = [sb("Sb%d" % j, [128, 128], BF16, pg) for j in range(2)]
                  vn = [[sb("vn%d_%d" % (j, i), [128, 128], BF16, pg) for i in range(2)] for j in range(2)]
                  og = [[sb("og%d_%d" % (j, i), [128, 128], BF16, pg) for i in range(2)] for j in range(2)]
                  ost = [[sb("ost%d_%d" % (j, i), [128, 4], F32, pg) for i in range(2)] for j in range(2)]
                  ojk = [[sb("ojk%d_%d" % (j, i), [128, 128], BF16, pg) for i in range(2)] for j in range(2)]
                  S.dma('pool', wz[:], w_in[:, C_GZ:C_GZ + 512].rearrange("(k p) n -> p k n", p=128), writes=['wz'])
                  for i in range(4):
                      S.dma('sp', gnw4[:, i * 128:(i + 1) * 128], gdn_nw.partition_broadcast(128), writes=['gnw4'])
                  S.dma('sp', smv[:], smallv.partition_broadcast(128), writes=['dtb', 'alg'])
                  S.dma('sp', cw[:].rearrange("p a b -> p (a b)"), conv_wT, writes=['cw'])
                  flush(); stage_done('G00')
                  for t in range(NT):
                      S.op('dve', lambda e, t=t: e.tensor_tensor(sA[:, t, :], scal[:, t, 0:4], dtb, ALU.add), reads=['scal', 'dtb'], writes=['sA'])
                  S.op('act', lambda e: e.activation(sB[:], sA[:], AF.Abs), reads=['sA'], writes=['sB'])
                  S.op('act', lambda e: e.activation(sB[:], sB[:], AF.Exp, scale=-1.0), reads=['sB'], writes=['sB'])
                  S.op('act', lambda e: e.activation(sB[:], sB[:], AF.Ln, bias=1.0), reads=['sB'], writes=['sB'])
                  S.op('dve', lambda e: e.tensor_single_scalar(sC[:], sA[:], 0.0, ALU.max), reads=['sA'], writes=['sC'])
                  S.op('dve', lambda e: e.tensor_tensor(sA[:], sC[:], sB[:], ALU.add), reads=['sB', 'sC'], writes=['sA'])
                  S.op('act', lambda e: e.activation(alg, alg, AF.Exp), reads=['alg'], writes=['alg'])
                  S.op('dve', lambda e: e.tensor_scalar(alg, alg, -1.0, None, ALU.mult), reads=['alg'], writes=['alg'])
                  for t in range(NT):
                      S.op('dve', lambda e, t=t: e.tensor_tensor(sA[:, t, :], sA[:, t, :], alg, ALU.mult), reads=['sA', 'alg'], writes=['sA'])
                  flush(); stage_done('G0m')
                  g2 = sA[:].rearrange("p t n -> p (t n)")
                  S.op('pe', lambda e: e.matmul(ps[3][:, 0:64], trif, g2, start=True, stop=True), reads=['sA', 'cf'], writes=['ps3'])
                  S.op('pe', lambda e: e.matmul(ps[3][:, 64:128], onesf, g2, start=True, stop=True), reads=['sA', 'cf'], writes=['ps3'])
                  S.op('dve', lambda e: e.tensor_copy(gc[:].rearrange("p t n -> p (t n)"), ps[3][:, 0:64]), reads=['ps3'], writes=['gc'])
                  S.op('act', lambda e: e.activation(egl[:].rearrange("p t n -> p (t n)"), ps[3][:, 64:128], AF.Exp), reads=['ps3'], writes=['egl'])
                  S.op('dve', lambda e: e.tensor_tensor(edec[:].rearrange("p t n -> p (t n)"), ps[3][:, 64:128], gc[:].rearrange("p t n -> p (t n)"), ALU.subtract),
                       reads=['ps3', 'gc'], writes=['edec'])
                  S.op('act', lambda e: e.activation(edec[:], edec[:], AF.Exp), reads=['edec'], writes=['edec'])
                  flush(); stage_done('G0n')
                  for t in range(NT):
                      S.op('dve', lambda e, t=t: e.tensor_copy(sC[:, t, :], scal[:, t, 4:8]), reads=['scal', 'sC'], writes=['sC'])
                  S.op('act', lambda e: e.activation(beta[:], sC[:], AF.Sigmoid), reads=['sC'], writes=['beta'])
                  S.op('act', lambda e: e.activation(kbs[:], gc[:], AF.Exp), reads=['gc'], writes=['kbs'])
                  S.op('dve', lambda e: e.tensor_tensor(kbs[:], kbs[:], beta[:], ALU.mult), reads=['kbs', 'beta'], writes=['kbs'])
                  flush(); stage_done('G0a')
                  for t in range(NT):
                      bank = t % 2
                      for kc in range(8):
                          S.op('pe', lambda e, t=t, kc=kc, bank=bank: e.matmul(ps[bank][:, :], xnT[:, kc, t * 128:(t + 1) * 128], wz[:, kc, :], start=(kc == 0), stop=(kc == 7)),
                               reads=['xnT%d' % t, 'wz'], writes=['ps%d' % bank], sig=(kc == 7))
                      S.op('act', lambda e, t=t, bank=bank: e.activation(rs[:, :], ps[bank][:, :], AF.Silu), reads=['ps%d' % bank], writes=['rs'])
                      S.op('dve', lambda e, t=t: e.tensor_tensor(gate[:, t, :], rs[:, :], gnw4[:, :], ALU.mult), reads=['rs', 'gnw4'], writes=['gate%d' % t])
                  S.op('pool', lambda e: e.memset(pre[:, 0:4], 0.0), writes=['pre_pad'])
                  flush()
                  stage_done('G1')
                  def gdn_prep(h, si):
                      G = 'g%d_' % si
                      for wi, (cc, dst, dkey) in enumerate(((h, qf, 'qf'), (4 + h, kf, 'kf'), (8 + h, vfb, 'vfb'))):
                          S.dma('pool', wq3[wi][:], w_in[:, cc * 128:(cc + 1) * 128].rearrange("(k p) n -> p k n", p=128), writes=['wq3_%d' % wi])
                          for k in range(4):
                              S.op('dve', lambda e, k=k, cc=cc: e.tensor_scalar(dg[:, k, :], identf, cw[:, cc, k:k + 1], None, ALU.mult), reads=['cf', 'cw'], writes=['dg'])
                          for tb in range(4):
                              bank = 2 + tb % 2
                              for kc in range(8):
                                  S.op('pe', lambda e, wi=wi, kc=kc, tb=tb, bank=bank: e.matmul(ps[bank][:, :], wq3[wi][:, kc, :], xnT[:, kc, tb * 512:(tb + 1) * 512], start=(kc == 0), stop=(kc == 7)),
                                       reads=['xnT%d' % (4 * tb + i) for i in range(4)] + ['wq3_%d' % wi], writes=['ps%d' % bank], sig=(kc == 7))
                              copy_op(evac_engine(), pre[:, 4 + tb * 512:4 + (tb + 1) * 512], ps[bank][:, :], reads=['ps%d' % bank], writes=['ps%d' % bank, 'pre%d' % tb])
                              yield
                          for tb in range(4):
                              bank = 4 + tb % 2
                              rd = ['pre%d' % tb, 'dg', 'pre_pad'] + (['pre%d' % (tb - 1)] if tb > 0 else [])
                              for k in range(4):
                                  S.op('pe', lambda e, k=k, tb=tb, bank=bank: e.matmul(ps[bank][:, :], dg[:, k, :], pre[:, tb * 512 + k + 1:tb * 512 + k + 513], start=(k == 0), stop=(k == 3)),
                                       reads=rd, writes=['ps%d' % bank], sig=(k == 3))
                              S.op('act', lambda e, dst=dst, tb=tb, bank=bank: e.activation(dst[:, tb * 512:(tb + 1) * 512], ps[bank][:, :], AF.Silu),
                                   reads=['ps%d' % bank], writes=['ps%d' % bank, dkey + '%d' % tb])
                              yield
                      for (src, skey, dstb, dbkey, scl) in ((qf, 'qf', qTb, 'qTb', 128.0 ** -0.5), (kf, 'kf', kTb, 'kTb', 1.0)):
                          def l2_front(tb, src=src, skey=skey):
                              cs = slice(tb * 512, (tb + 1) * 512)
                              bb = tb % 2
                              S.op('act', lambda e: e.activation(sqb[bb][:, :], src[:, cs], AF.Square), reads=[skey + '%d' % tb], writes=['sqb%d' % bb])
                              S.op('pe', lambda e: e.matmul(ps[6 + bb][:, :], onesb, sqb[bb][:, :], start=True, stop=True), reads=['sqb%d' % bb, 'cb'], writes=['ps%d' % (6 + bb)])

                          def l2_back(tb, src=src, skey=skey, dstb=dstb, dbkey=dbkey, scl=scl):
                              cs = slice(tb * 512, (tb + 1) * 512)
                              bb = tb % 2
                              S.op('act', lambda e: e.activation(rsb[bb][:, :], ps[6 + bb][:, :], AF.Ln, bias=EPS), reads=['ps%d' % (6 + bb)], writes=['ps%d' % (6 + bb), 'rsb%d' % bb])
                              S.op('act', lambda e: e.activation(rsb[bb][:, :], rsb[bb][:, :], AF.Exp, scale=-0.5), reads=['rsb%d' % bb], writes=['rsb%d' % bb])
                              S.op('dve', lambda e: e.scalar_tensor_tensor(dstb[:, cs], src[:, cs], scl, rsb[bb][:, :], ALU.mult, ALU.mult),
                                   reads=[skey + '%d' % tb, 'rsb%d' % bb], writes=[dbkey + '%d' % tb])
                          l2_front(0)
                          for tb in range(4):
                              if tb + 1 < 4:
                                  l2_front(tb + 1)
                              l2_back(tb)
                              yield
                      for t in range(NT):
                          ts_ = slice(t * 128, (t + 1) * 128)
                          bk, bv = 2 + t % 2, 4 + t % 2
                          kb_, vb_ = 'ps%d' % bk, 'ps%d' % bv
                          pbk = psb(bk)
                          S.op('pe', lambda e, ts_=ts_, pbk=pbk: e.transpose(pbk[:, 0:128], kTb[:, ts_], identb), reads=['kTb%d' % (t // 4), 'cb'], writes=[kb_])
                          S.op('act', lambda e, t=t, pbk=pbk: e.activation(kbg[:, t, :], pbk[:, 0:128], AF.Copy, scale=kbs[:, t, h:h + 1]), reads=[kb_, 'kbs'], writes=[kb_, 'kbg%d' % t])
                          S.op('dve', lambda e, t=t, pbk=pbk: e.tensor_scalar(kdec[si][:, t, :], pbk[:, 0:128], edec[:, t, h:h + 1], None, ALU.mult), reads=[kb_, 'edec'], writes=[kb_, G + 'kdec%d' % t])
                          pbv = psb(bv)
                          S.op('pe', lambda e, ts_=ts_, pbv=pbv: e.transpose(pbv[:, 0:128], vfb[:, ts_], identb), reads=['vfb%d' % (t // 4), 'cb'], writes=[vb_])
                          S.op('dve', lambda e, t=t, pbv=pbv: e.tensor_scalar(vb[:, t, :], pbv[:, 0:128], beta[:, t, h:h + 1], None, ALU.mult), reads=[vb_, 'beta'], writes=[vb_, 'vb%d' % t])
                          if t % 2 == 1:
                              yield
                      for t0 in range(0, NT, NI):
                          def setup_front(ii):
                              t = t0 + ii
                              ts_ = slice(t * 128, (t + 1) * 128)
                              I = str(ii)
                              gcol = gc[:, t, h:h + 1]
                              S.op('dve', lambda e, ii=ii, gcol=gcol: e.tensor_scalar(dgc[ii][:, :], identf, gcol, None, ALU.mult), reads=['cf', 'gc'], writes=['dgc' + I])
                              S.op('pe', lambda e, ii=ii: e.matmul(ps[ii][:, 0:128], onesf, dgc[ii][:, :], start=True, stop=True), reads=['dgc' + I, 'cf'], writes=['ps' + I])
                              Gp = ps[ii][:, 0:128]
                              S.op('pe', lambda e, ii=ii, ts_=ts_: e.matmul(ps[ii][:, 128:256], kTb[:, ts_], kTb[:, ts_], start=True, stop=True), reads=['kTb%d' % (t // 4)], writes=['ps' + I])
                              S.op('pe', lambda e, ii=ii, ts_=ts_: e.matmul(ps[ii][:, 256:384], kTb[:, ts_], qTb[:, ts_], start=True, stop=True), reads=['kTb%d' % (t // 4), 'qTb%d' % (t // 4)], writes=['ps' + I])
                              S.op('dve', lambda e, ii=ii, gcol=gcol, Gp=Gp: e.tensor_scalar(t2[ii][:, :], Gp, gcol, 0.0, ALU.subtract, ALU.min), reads=['ps' + I, 'gc'], writes=['ps' + I, 't2' + I])
                              S.op('act', lambda e, ii=ii: e.activation(t2[ii][:, :], t2[ii][:, :], AF.Exp), reads=['t2' + I], writes=['t2' + I])
                              S.op('pool', lambda e, ii=ii: e.tensor_tensor(t2[ii][:, :], t2[ii][:, :], mask_le_f, ALU.mult), reads=['t2' + I, 'cf'], writes=['t2' + I])
                              S.op('dve', lambda e, ii=ii, gcol=gcol, Gp=Gp: e.tensor_scalar(t1[ii][:, :], Gp, gcol, 0.0, ALU.subtract, ALU.max), reads=['ps' + I, 'gc'], writes=['ps' + I, 't1' + I])
                              S.op('act', lambda e, ii=ii: e.activation(t1[ii][:, :], t1[ii][:, :], AF.Exp, scale=-1.0), reads=['t1' + I], writes=['t1' + I])
                              S.op('pool', lambda e, ii=ii: e.tensor_tensor(t1[ii][:, :], t1[ii][:, :], mask_gt_f, ALU.mult), reads=['t1' + I, 'cf'], writes=['t1' + I])
                              S.op('act', lambda e, ii=ii, Gp=Gp: e.activation(eg[ii][:, :], Gp, AF.Exp), reads=['ps' + I], writes=['ps' + I, 'eg' + I])
                              S.op('pool', lambda e, ii=ii, ts_=ts_: e.tensor_tensor(qgT[si][:, ts_], qTb[:, ts_], eg[ii][:, :], ALU.mult), reads=['eg' + I, 'qTb%d' % (t // 4)], writes=[G + 'qgT%d' % t])
                              S.op('dve', lambda e, ii=ii, t=t, h=h: e.scalar_tensor_tensor(Pm[ii][0][:, :], ps[ii][:, 128:256], beta[:, t, h:h + 1], t1[ii][:, :], ALU.mult, ALU.mult),
                                   reads=['ps' + I, 'beta', 't1' + I], writes=['ps' + I, 'P0_' + I])
                              S.op('dve', lambda e, ii=ii, t=t: e.tensor_tensor(qkm[si][:, t, :], ps[ii][:, 256:384], t2[ii][:, :], ALU.mult), reads=['ps' + I, 't2' + I], writes=['ps' + I, G + 'qkm%d' % t])
                              pbu = psb(ii)
                              S.op('pe', lambda e, ii=ii, pbu=pbu: e.transpose(pbu[:, 768:896], Pm[ii][0][:, :], identb), reads=['P0_' + I, 'cb'], writes=['ps' + I])
                              S.op('act', lambda e, ii=ii, pbu=pbu: e.activation(Qm[ii][0][:, :], pbu[:, 768:896], AF.Copy), reads=['ps' + I], writes=['ps' + I, 'Q0_' + I])
                          def setup_tail(ii):
                              I = str(ii)
                              S.op('pool', lambda e, ii=ii: e.tensor_tensor(Uo16[ii][:, :], Qm[ii][0][:, :], off16_b, ALU.mult), reads=['Q0_' + I, 'cb'], writes=['Uo16_' + I])
                              S.op('pool', lambda e, ii=ii: e.tensor_tensor(Uo32[ii][:, :], Qm[ii][0][:, :], off32_b, ALU.mult), reads=['Q0_' + I, 'cb'], writes=['Uo32_' + I])
                              S.op('pool', lambda e, ii=ii: e.tensor_tensor(Lo64[ii][:, :], Pm[ii][0][:, :], off64_b, ALU.mult), reads=['P0_' + I, 'cb'], writes=['Lo64_' + I])
                              S.op('pool', lambda e, ii=ii: e.tensor_tensor(Pm[ii][0][:, :], Pm[ii][0][:, :], bd16_b, ALU.mult), reads=['P0_' + I, 'cb'], writes=['P0_' + I])
                              S.op('pool', lambda e, ii=ii: e.tensor_tensor(Qm[ii][0][:, :], Qm[ii][0][:, :], bd16_b, ALU.mult), reads=['Q0_' + I, 'cb'], writes=['Q0_' + I])
                              S.op('dve', lambda e, ii=ii: e.tensor_tensor(Ym[ii][0][:, :], identb, Qm[ii][0][:, :], ALU.subtract), reads=['Q0_' + I, 'cb'], writes=['Y0_' + I])
                          lockstep(NI, setup_front)
                          lockstep(NI, setup_tail)
                          for lv in range(1, 4):
                              a, b = (lv - 1) % 2, lv % 2
                              for ii in range(NI):
                                  I = str(ii)
                                  IB = 'ps%d' % (4 + ii)
                                  S.op('pe', lambda e, ii=ii, a=a: e.matmul(ps[4 + ii][:, 0:128], Qm[ii][a][:, :], Pm[ii][a][:, :], start=True, stop=True), reads=['Q%d_' % a + I, 'P%d_' % a + I], writes=[IB])
                                  if lv <= 2:
                                      S.op('pe', lambda e, ii=ii, a=a: e.matmul(ps[4 + ii][:, 128:256], Pm[ii][a][:, :], Qm[ii][a][:, :], start=True, stop=True), reads=['Q%d_' % a + I, 'P%d_' % a + I], writes=[IB])
                                  S.op('act', lambda e, ii=ii, b=b: e.activation(Pm[ii][b][:, :], ps[4 + ii][:, 0:128], AF.Copy), reads=[IB], writes=[IB, 'P%d_' % b + I])
                                  if lv <= 2:
                                      S.op('dve', lambda e, ii=ii, b=b: e.tensor_copy(Qm[ii][b][:, :], ps[4 + ii][:, 128:256]), reads=[IB], writes=[IB, 'Q%d_' % b + I])
                              for ii in range(NI):
                                  I = str(ii)
                                  IB = 'ps%d' % (4 + ii)
                                  S.op('pe', lambda e, ii=ii, a=a, b=b: e.matmul(ps[4 + ii][:, 256:384], Pm[ii][b][:, :], Ym[ii][a][:, :], start=True, stop=True), reads=['P%d_' % b + I, 'Y%d_' % a + I], writes=[IB])
                                  S.op('dve', lambda e, ii=ii, a=a, b=b: e.tensor_tensor(Ym[ii][b][:, :], Ym[ii][a][:, :], ps[4 + ii][:, 256:384], ALU.add), reads=[IB, 'Y%d_' % a + I], writes=[IB, 'Y%d_' % b + I])
                          def tr_to(ii, src, skey, dst, dkey):
                              I = str(ii)
                              IB = 'ps%d' % (4 + ii)
                              pbt = psb(4 + ii)[:, 512:640]
                              S.op('pe', lambda e: e.transpose(pbt, src[:, :], identb), reads=[skey + I, 'cb'], writes=[IB])
                              S.op('act', lambda e: e.activation(dst[:, :], pbt, AF.Copy), reads=[IB], writes=[IB, dkey + I])
                          for ii in range(NI):
                              tr_to(ii, Ym[ii][1], 'Y1_', Pm[ii][0], 'P0_')
                          for (Uo, ukey, d_i, dt_i, m_i) in ((Uo16, 'Uo16_', 0, 1, 0), (Uo32, 'Uo32_', 1, 0, 1)):
                              for ii in range(NI):
                                  I = str(ii)
                                  IB = 'ps%d' % (4 + ii)
                                  S.op('pe', lambda e, ii=ii, Uo=Uo, d_i=d_i: e.matmul(ps[4 + ii][:, 0:128], Uo[ii][:, :], Pm[ii][d_i][:, :], start=True, stop=True), reads=[ukey + I, 'P%d_' % d_i + I], writes=[IB])
                                  S.op('act', lambda e, ii=ii, m_i=m_i: e.activation(Qm[ii][m_i][:, :], ps[4 + ii][:, 0:128], AF.Copy), reads=[IB], writes=[IB, 'Q%d_' % m_i + I])
                              for ii in range(NI):
                                  I = str(ii)
                                  IB = 'ps%d' % (4 + ii)
                                  S.op('pe', lambda e, ii=ii, dt_i=dt_i, m_i=m_i: e.matmul(ps[4 + ii][:, 128:256], Ym[ii][dt_i][:, :], Qm[ii][m_i][:, :], start=True, stop=True), reads=['Y%d_' % dt_i + I, 'Q%d_' % m_i + I], writes=[IB])
                                  S.op('dve', lambda e, ii=ii, d_i=d_i: e.tensor_tensor(Pm[ii][1 - d_i][:, :], Pm[ii][d_i][:, :], ps[4 + ii][:, 128:256], ALU.subtract), reads=[IB, 'P%d_' % d_i + I], writes=[IB, 'P%d_' % (1 - d_i) + I])
                              for ii in range(NI):
                                  tr_to(ii, Pm[ii][1 - d_i], 'P%d_' % (1 - d_i), Ym[ii][1 - dt_i], 'Y%d_' % (1 - dt_i))
                          for ii in range(NI):
                              I = str(ii)
                              IB = 'ps%d' % (4 + ii)
                              S.op('pe', lambda e, ii=ii: e.matmul(ps[4 + ii][:, 0:128], Lo64[ii][:, :], Ym[ii][1][:, :], start=True, stop=True), reads=['Lo64_' + I, 'Y1_' + I], writes=[IB])
                              S.op('act', lambda e, ii=ii: e.activation(Qm[ii][0][:, :], ps[4 + ii][:, 0:128], AF.Copy), reads=[IB], writes=[IB, 'Q0_' + I])
                          for ii in range(NI):
                              I = str(ii)
                              IB = 'ps%d' % (4 + ii)
                              S.op('pe', lambda e, ii=ii: e.matmul(ps[4 + ii][:, 128:256], Pm[ii][0][:, :], Qm[ii][0][:, :], start=True, stop=True), reads=['P0_' + I, 'Q0_' + I], writes=[IB])
                              S.op('dve', lambda e, ii=ii: e.tensor_tensor(Ym[ii][0][:, :], Ym[ii][1][:, :], ps[4 + ii][:, 128:256], ALU.subtract), reads=[IB, 'Y1_' + I], writes=[IB, 'Y0_' + I])
                          for ii in range(NI):
                              t = t0 + ii
                              I = str(ii)
                              IB = 'ps%d' % (4 + ii)
                              TT = Ym[ii][0]
                              S.op('pe', lambda e, ii=ii, TT=TT, t=t: e.matmul(ps[4 + ii][:, 0:128], TT[:, :], vb[:, t, :], start=True, stop=True), reads=['Y0_' + I, 'vb%d' % t], writes=[IB])
                              S.op('pe', lambda e, ii=ii, TT=TT, t=t: e.matmul(ps[4 + ii][:, 128:256], kbg[:, t, :], TT[:, :], start=True, stop=True), reads=['Y0_' + I, 'kbg%d' % t], writes=[IB])
                              S.op('act', lambda e, ii=ii, t=t: e.activation(u_[si][:, t, :], ps[4 + ii][:, 0:128], AF.Copy), reads=[IB], writes=[IB, G + 'u%d' % t])
                              S.op('dve', lambda e, ii=ii, t=t: e.tensor_copy(wT[si][:, t, :], ps[4 + ii][:, 128:256]), reads=[IB], writes=[IB, G + 'wT%d' % t])
                      yield

                  def gdn_scan_multi(heads):
                      for (h, si, j) in heads:
                          S.op('pool', lambda e, j=j: e.memset(Sf[j][:, :], 0.0), writes=['Sf%d' % j])
                          S.op('pool', lambda e, j=j: e.memset(Sb[j][:, :], 0.0), writes=['Sb%d' % j])

                      def scan_post(t, h, si, j):
                          ts_ = slice(t * 128, (t + 1) * 128)
                          b2 = t % 2
                          B2 = '%d_%d' % (j, b2)
                          pk = 'ps%d' % (4 * j + 1 + b2)
                          tk = 'ps%d' % (4 * j + 3)
                          po = ps[4 * j + 1 + b2][:, 0:128]
                          S.op('act', lambda e: e.activation(ojk[j][b2][:, :], po, AF.Square, accum_out=ost[j][b2][:, 0:1]), reads=[pk], writes=[pk, 'ojk' + B2, 'ost' + B2])
                          S.op('act', lambda e: e.activation(ost[j][b2][:, 1:2], ost[j][b2][:, 0:1], AF.Ln, bias=EPS, scale=1.0 / 128), reads=['ost' + B2], writes=['ost1' + B2])
                          S.op('act', lambda e: e.activation(ost[j][b2][:, 2:3], ost[j][b2][:, 1:2], AF.Exp, scale=-0.5), reads=['ost1' + B2], writes=['ost2' + B2])
                          S.op('dve', lambda e: e.scalar_tensor_tensor(og[j][b2][:, :], po, ost[j][b2][:, 2:3], gate[:, t, h * 128:(h + 1) * 128], ALU.mult, ALU.mult),
                               reads=[pk, 'ost2' + B2, 'gate%d' % t], writes=[pk, 'og' + B2])
                          pbo = psb(4 * j + 3)[:, b2 * 128:(b2 + 1) * 128]
                          S.op('pe', lambda e: e.transpose(pbo, og[j][b2][:, :], identb), reads=['og' + B2, 'cb'], writes=[tk])
                          copy_op('act' if t % 2 else 'dve', oT_g[:, h, ts_], pbo, reads=[tk], writes=[tk, 'oT_g%d_%d' % (h, t)])

                      for t in range(NT):
                          ts_ = slice(t * 128, (t + 1) * 128)
                          b2 = t % 2
                          def _step(idx, t=t, ts_=ts_, b2=b2):
                              (h, si, j) = heads[idx]
                              G = 'g%d_' % si
                              B2 = '%d_%d' % (j, b2)
                              k0 = 'ps%d' % (4 * j)
                              pk = 'ps%d' % (4 * j + 1 + b2)
                              S.op('pe', lambda e, t=t, si=si, j=j: e.matmul(ps[4 * j][:, 0:128], wT[si][:, t, :], Sb[j][:, :], start=True, stop=True), reads=[G + 'wT%d' % t, 'Sb%d' % j], writes=[k0])
                              S.op('dve', lambda e, t=t, si=si, j=j, b2=b2: e.tensor_tensor(vn[j][b2][:, :], u_[si][:, t, :], ps[4 * j][:, 0:128], ALU.subtract), reads=[G + 'u%d' % t, k0], writes=[k0, 'vn' + B2])
                              S.op('pe', lambda e, t=t, si=si, j=j, b2=b2: e.matmul(ps[4 * j][:, 128:256], kdec[si][:, t, :], vn[j][b2][:, :], start=True, stop=True), reads=[G + 'kdec%d' % t, 'vn' + B2], writes=[k0])
                              S.op('pe', lambda e, ts_=ts_, si=si, j=j, b2=b2: e.matmul(ps[4 * j + 1 + b2][:, 0:128], qgT[si][:, ts_], Sb[j][:, :], start=True, stop=False), reads=[G + 'qgT%d' % t, 'Sb%d' % j], writes=[pk], sig=False)
                              S.op('pe', lambda e, t=t, si=si, j=j, b2=b2: e.matmul(ps[4 * j + 1 + b2][:, 0:128], qkm[si][:, t, :], vn[j][b2][:, :], start=False, stop=True), reads=[G + 'qkm%d' % t, 'vn' + B2], writes=[pk])
                              S.op('dve', lambda e, t=t, h=h, j=j: e.scalar_tensor_tensor(Sf[j][:, :], Sf[j][:, :], egl[:, t, h:h + 1], ps[4 * j][:, 128:256], ALU.mult, ALU.add), reads=['Sf%d' % j, 'egl', k0], writes=[k0, 'Sf%d' % j])
                              S.op('pool', lambda e, j=j: e.tensor_copy(Sb[j][:, :], Sf[j][:, :]), reads=['Sf%d' % j], writes=['Sb%d' % j])
                          lockstep(len(heads), _step)
                          if t >= 1:
                              lockstep(len(heads), lambda idx, t=t: scan_post(t - 1, *heads[idx]))
                      lockstep(len(heads), lambda idx: scan_post(NT - 1, *heads[idx]))

                  def drain(gen):
                      for _ in gen:
                          pass
                  for hp in range(2):
                      drain(gdn_prep(2 * hp, 0))
                      drain(gdn_prep(2 * hp + 1, 1))
                      gdn_scan_multi([(2 * hp, 0, 0), (2 * hp + 1, 1, 1)])
                  flush()
                  stage_done('G', [('oTg%d' % hh, oT_g[:, hh, :], 128, S_LEN) for hh in range(4)])
              oT_f = sb("oT_f", [64, 8, S_LEN], BF16, mx)
              with ExitStack() as pf:
                  wf = sb("wf", [128, 8, 1536], BF16, pf)
                  vaug2 = sb("vaug2", [128, NT, 584], BF16, pf)
                  vaug = vaug2[:, :, 0:520].rearrange("p t (h d) -> p t h d", h=8)
                  qTA = [sb("qTA%d" % i, [128, S_LEN], BF16, pf) for i in range(2)]
                  kTA = [sb("kTA%d" % i, [128, S_LEN], BF16, pf) for i in range(2)]
                  qTB = [sb("qTB%d" % i, [128, S_LEN], BF16, pf) for i in range(2)]
                  kTB = [sb("kTB%d" % i, [128, S_LEN], BF16, pf) for i in range(2)]
                  caqA = sb("caqA", [128, NT, 70], BF16, pf)
                  cakA = sb("cakA", [128, NT, 70], BF16, pf)
                  caqB = sb("caqB", [128, NT, 38], BF16, pf)
                  cakB = sb("cakB", [128, NT, 38], BF16, pf)
                  PT = [sb("PT%d" % i, [128, 512], BF16, pf) for i in range(3)]
                  osb = [sb("osb%d" % i, [65, 512], F32, pf) for i in range(2)]
                  rc = [sb("rc%d" % i, [65, 512], F32, pf) for i in range(2)]
                  for i3 in range(3):
                      S.dma('pool', wf[:, :, i3 * 512:(i3 + 1) * 512],
                            w_in[:, C_FQ + i3 * 512:C_FQ + (i3 + 1) * 512].rearrange("(k p) n -> p k n", p=128), writes=['wf%d' % i3])
                  S.op('pool', lambda e: e.memset(vaug[:, :, :, 64:65], 1.0), writes=['vaug1'])
                  S.op('pool', lambda e: e.memset(vaug2[:, :, 520:584], 0.0), writes=['vaug1'])
                  for i2 in range(2):
                      for (tl, nm, r0, r1) in ((qTA, 'qTA', 64, 128), (kTA, 'kTA', 64, 128), (qTB, 'qTB', 0, 64), (kTB, 'kTB', 0, 64)):
                          S.op('pool', lambda e, i2=i2, tl=tl, r0=r0, r1=r1: e.memset(tl[i2][r0:r1, :], 0.0), writes=['%s%db%d' % (nm, i2, tb_) for tb_ in range(4)])
                  for (c_, nm) in ((caqA, 'caqA'), (cakA, 'cakA'), (caqB, 'caqB'), (cakB, 'cakB')):
                      S.op('pool', lambda e, c_=c_: e.memset(c_[:], 0.0), writes=[nm])
                  S.op('pool', lambda e: e.memset(caqA[:, :, 67:70], 1.0), reads=['caqA'], writes=['caqA'])
                  S.op('pool', lambda e: e.memset(cakA[:, :, 64:67], 1.0), reads=['cakA'], writes=['cakA'])
                  S.op('pool', lambda e: e.memset(caqB[:, :, 35:38], 1.0), reads=['caqB'], writes=['caqB'])
                  S.op('pool', lambda e: e.memset(cakB[:, :, 32:35], 1.0), reads=['cakB'], writes=['cakB'])
                  for t in range(NT):
                      bank = t % 2
                      for kc in range(8):
                          S.op('pe', lambda e, t=t, kc=kc, bank=bank: e.matmul(ps[bank][:, :], xnT[:, kc, t * 128:(t + 1) * 128], wf[:, kc, 1024:1536],
                                                                               start=(kc == 0), stop=(kc == 7)),
                               reads=['xnT%d' % t, 'wf2'], writes=['ps%d' % bank], sig=(kc == 7))
                      copy_op(evac_engine(), vaug[:, t, :, 0:64], ps[bank][:, :].rearrange("p (h d) -> p h d", h=8),
                              reads=['ps%d' % bank], writes=['vaug%d' % t])
                  pending = []
                  pending0 = []
                  for hp in range(4):
                      pb_ = hp % 2
                      hA, hB = 2 * hp, 2 * hp + 1
                      qA, kA, qB, kB = qTA[pb_], kTA[pb_], qTB[pb_], kTB[pb_]
                      nqA, nkA, nqB, nkB = 'qTA%d' % pb_, 'kTA%d' % pb_, 'qTB%d' % pb_, 'kTB%d' % pb_
                      for j in range(3):
                          S.op('dve', lambda e, j=j, hA=hA: e.tensor_copy(caqA[:, :, 64 + j], c123[:, j, :, hA]), reads=['c%d' % (j + 1), 'caqA'], writes=['caqA'])
                          S.op('dve', lambda e, j=j, hA=hA: e.tensor_scalar(cakA[:, :, 67 + j], c123[:, j, :, hA], -1.0, None, ALU.mult), reads=['c%d' % (j + 1), 'cakA'], writes=['cakA'])
                          S.op('dve', lambda e, j=j, hB=hB: e.tensor_copy(caqB[:, :, 32 + j], c123[:, j, :, hB]), reads=['c%d' % (j + 1), 'caqB'], writes=['caqB'])
                          S.op('dve', lambda e, j=j, hB=hB: e.tensor_scalar(cakB[:, :, 35 + j], c123[:, j, :, hB], -1.0, None, ALU.mult), reads=['c%d' % (j + 1), 'cakB'], writes=['cakB'])
                      for tb in range(4):
                          cs = slice(tb * 512, (tb + 1) * 512)
                          xk = ['xnT%d' % (4 * tb + i) for i in range(4)]
                          for kc in range(8):
                              S.op('pe', lambda e, kc=kc, cs=cs, hA=hA: e.matmul(ps[2][:, :], wf[:, kc, hA * 64:hA * 64 + 128], xnT[:, kc, cs], start=(kc == 0), stop=(kc == 7)),
                                   reads=xk + ['wf0'], writes=['ps2'], sig=(kc == 7))
                          S.op('act', lambda e, cs=cs, qA=qA: e.activation(qA[0:64, cs], ps[2][0:64, :], AF.Copy, scale=0.125), reads=['ps2'], writes=['ps2', nqA + 'a%d' % tb])
                          S.op('dve', lambda e, cs=cs, qB=qB: e.tensor_scalar(qB[64:128, cs], ps[2][64:128, :], 0.125, None, ALU.mult), reads=['ps2'], writes=['ps2', nqB + 'a%d' % tb])
                          for kc in range(8):
                              S.op('pe', lambda e, kc=kc, cs=cs, hA=hA: e.matmul(ps[3][:, :], wf[:, kc, 512 + hA * 64:512 + hA * 64 + 128], xnT[:, kc, cs], start=(kc == 0), stop=(kc == 7)),
                                   reads=xk + ['wf1'], writes=['ps3'], sig=(kc == 7))
                          S.op('dve', lambda e, cs=cs, kA=kA: e.tensor_copy(kA[0:64, cs], ps[3][0:64, :]), reads=['ps3'], writes=['ps3', nkA + 'a%d' % tb])
                          S.op('act', lambda e, cs=cs, kB=kB: e.activation(kB[64:128, cs], ps[3][64:128, :], AF.Copy), reads=['ps3'], writes=['ps3', nkB + 'a%d' % tb])
                          for (src_c, M, r0, dstT, dname, bank, eng) in ((caqA, 70, 64, qA, nqA, 4, 'act'), (cakA, 70, 64, kA, nkA, 5, 'dve'),
                                                                     (caqB, 38, 32, qB, nqB, 4, 'act'), (cakB, 38, 32, kB, nkB, 5, 'dve')):
                              ckey = {id(caqA): 'caqA', id(cakA): 'cakA', id(caqB): 'caqB', id(cakB): 'cakB'}[id(src_c)]
                              for i in range(4):
                                  t = 4 * tb + i
                                  S.op('pe', lambda e, t=t, i=i, src_c=src_c, M=M, bank=bank: e.matmul(ps[bank][0:M, i * 128:(i + 1) * 128], src_c[:, t, :], identb, start=True, stop=True),
                                       reads=[ckey, 'cb'], writes=['ps%d' % bank], sig=(i == 3))
                              copy_op(eng, dstT[r0:r0 + 6, cs], ps[bank][r0:r0 + 6, :], reads=['ps%d' % bank], writes=['ps%d' % bank, dname + 'b%d' % tb])
                      for (h, qTh, kTh, qk, kk) in ((hA, qA, kA, nqA, nkA), (hB, qB, kB, nqB, nkB)):
                          qkeys = lambda qg: [qk + 'a%d' % qg, qk + 'b%d' % qg]
                          kkeys = lambda kt: [kk + 'a%d' % (kt // 4), kk + 'b%d' % (kt // 4)]
                          step = 0
                          for qg in range(4):
                              ob = 6 + (qg % 2)
                              nk = 4 * qg + 4
                              items = []
                              for kt in range(nk):
                                  j = kt - 4 * qg
                                  c0 = max(j, 0) * 128
                                  items.append((kt, j, c0))

                              def emit_qk(kt, j, c0, sbk, qg=qg, qTh=qTh, kTh=kTh):
                                  S.op('pe', lambda e: e.matmul(ps[sbk][:, c0:512], kTh[0:128, kt * 128:(kt + 1) * 128],
                                                                qTh[0:128, qg * 512 + c0:(qg + 1) * 512], start=True, stop=True),
                                       reads=kkeys(kt) + qkeys(qg), writes=['ps%d' % sbk])

                              def emit_exp_pv(kt, j, c0, sbk, pi, first, last, h=h, ob=ob):
                                  S.op('act', lambda e: e.activation(PT[pi][:, c0:512], ps[sbk][:, c0:512], AF.Exp),
                                       reads=['ps%d' % sbk], writes=['PT%d' % pi])
                                  if j >= 0:
                                      S.op('dve', lambda e: e.tensor_tensor(PT[pi][:, c0:c0 + 128], PT[pi][:, c0:c0 + 128], mask_le_b, ALU.mult),
                                           reads=['PT%d' % pi, 'cb'], writes=['PT%d' % pi])
                                  S.op('pe', lambda e: e.matmul(ps[ob][:, c0:512], vaug2[:, kt, h * 65:h * 65 + 128], PT[pi][:, c0:512], start=first, stop=last),
                                       reads=['PT%d' % pi, 'vaug%d' % kt, 'vaug1'], writes=['ps%d' % ob], sig=last)

                              emit_qk(*items[0], sbk=(step % 2))
                              for ii, it in enumerate(items):
                                  if ii + 1 < len(items):
                                      emit_qk(*items[ii + 1], sbk=((step + 1) % 2))
                                  if ii == 1:
                                      while pending0:
                                          pending0.pop(0)()
                                  if ii == 3:
                                      while pending:
                                          pending.pop(0)()
                                  for _d in range(NDUMMY):
                                      S.op('pe', lambda e: e.matmul(ps[5][:, 0:256], identb, wf[:, 0, 0:256], start=True, stop=True), reads=['cb', 'wf0'], writes=['ps5'], sig=False)
                                  emit_exp_pv(*it, sbk=(step % 2), pi=(step % 3), first=(ii == 0), last=(ii == len(items) - 1))
                                  step += 1
                              o_ = osb[qg % 2]; r_ = rc[qg % 2]
                              S.op('dve', lambda e, o_=o_, ob=ob: e.tensor_copy(o_[:, :], ps[ob][0:65, :]), reads=['ps%d' % ob], writes=['ps%d' % ob, 'osb%d' % (qg % 2)])

                              def fin0(o_=o_, r_=r_, qg=qg, h=h):
                                  S.op('act', lambda e: e.activation(r_[64:65, :], o_[64:65, :], AF.Ln), reads=['osb%d' % (qg % 2)], writes=['rc%d' % (qg % 2)])
                                  S.op('act', lambda e: e.activation(r_[64:65, :], r_[64:65, :], AF.Exp, scale=-1.0), reads=['rc%d' % (qg % 2)], writes=['rc%d' % (qg % 2)])
                              pending0.append(fin0)

                              def fin(o_=o_, r_=r_, qg=qg, h=h):
                                  S.op('pe', lambda e: e.matmul(ps[4][0:64, :], onesf[64:65, 0:64], r_[64:65, :], start=True, stop=True),
                                       reads=['rc%d' % (qg % 2), 'cf'], writes=['ps4'])
                                  S.op('dve', lambda e: e.tensor_tensor(oT_f[0:64, h, qg * 512:(qg + 1) * 512], o_[0:64, :], ps[4][0:64, :], ALU.mult),
                                       reads=['osb%d' % (qg % 2), 'ps4'], writes=['ps4', 'oT_f%d_%d' % (h, qg)])
                              pending.append(fin)
                  while pending0:
                      pending0.pop(0)()
                  while pending:
                      pending.pop(0)()
                  flush()
                  stage_done('F', [('oTf0', oT_f[0:64, 0, :], 64, S_LEN), ('oTf7', oT_f[0:64, 7, :], 64, S_LEN), ('qT', qTB[1][0:128, :], 128, S_LEN), ('kT', kTB[1][0:128, :], 128, S_LEN)])
              h_sb = es.enter_context(nc.sbuf_tensor("h_sb", [128, NT, D], F32, side="right"))
              with ExitStack() as po:
                  wo_g = sb("wo_g", [128, 4, D], BF16, po)
                  wo_f = sb("wo_f", [128, 4, D], BF16, po)
                  oT_p = sb("oT_p", [128, 4, S_LEN], BF16, po)
                  S.dma('pool', wo_g[:], w_out[0:512, :].rearrange("(k p) n -> p k n", p=128), writes=['wo_g'])
                  S.dma('pool', wo_f[:], w_out[512:1024, :].rearrange("(k p) n -> p k n", p=128), writes=['wo_f'])
                  for hh in range(8):
                      S.dma('sp', oT_p[(hh % 2) * 64:(hh % 2) * 64 + 64, hh // 2, :], oT_f[0:64, hh, :],
                            reads=['oT_f%d_%d' % (hh, qq) for qq in range(4)], writes=['oT_p%d_%d' % (hh // 2, hh % 2)])
                  for t in range(NT):
                      S.dma('sp', h_sb[:, t, :], x[t * 128:(t + 1) * 128, :], writes=['h%d' % t])
                      ts_ = slice(t * 128, (t + 1) * 128)
                      for nh in range(2):
                          bank = (2 * t + nh) % 4
                          ns = slice(nh * 512, (nh + 1) * 512)
                          steps = []
                          if not skip_gdn:
                              for g in range(4):
                                  steps.append((oT_g[:, g, ts_], wo_g[:, g, ns], ['oT_g%d_%d' % (g, t), 'wo_g']))
                          for hp in range(4):
                              steps.append((oT_p[:, hp, ts_], wo_f[:, hp, ns], ['oT_p%d_0' % hp, 'oT_p%d_1' % hp, 'wo_f']))
                          for si, (l_, r_, rd) in enumerate(steps):
                              S.op('pe', lambda e, l_=l_, r_=r_, si=si, bank=bank, n=len(steps): e.matmul(ps[bank][:, :], l_, r_, start=(si == 0), stop=(si == n - 1)),
                                   reads=rd, writes=['ps%d' % bank], sig=(si == len(steps) - 1))
                          S.op('dve', lambda e, t=t, ns=ns, bank=bank: e.tensor_tensor(h_sb[:, t, ns], h_sb[:, t, ns], ps[bank][:, :], ALU.add),
                               reads=['h%d' % t, 'ps%d' % bank], writes=['h%d' % t])
                  flush()
                  stage_done('O', [('h0', h_sb[:, 0, :], 128, D)])
          if True:
              with ExitStack() as pff:
                  hnT = sb("hnT", [128, 8, S_LEN], BF16, pff)
                  gF = sb("gF", [128, D], F32, pff)
                  wgu = sb("wgu", [128, 11, 8, 256], BF16, pff)
                  wdn = sb("wdn", [128, 11, D], BF16, pff)
                  actT = sb("actT", [128, 11, 512], BF16, pff)
                  sg = [sb("sg%d" % i, [128, 512], F32, pff) for i in range(2)]
                  S.dma('sp', gF[:], g_ffn.partition_broadcast(128), writes=['gF'])

                  def load_half(hf):
                      for jj in range(11):
                          j = hf * 11 + jj
                          S.dma('pool', wgu[:, jj, :, 0:128], w_gu[:, j * 128:(j + 1) * 128].rearrange("(k p) n -> p k n", p=128), writes=['wgu%d' % jj])
                          S.dma('pool', wgu[:, jj, :, 128:256], w_gu[:, DFF + j * 128:DFF + (j + 1) * 128].rearrange("(k p) n -> p k n", p=128), writes=['wgu%d' % jj])
                      for jj in range(11):
                          j = hf * 11 + jj
                          S.dma('pool', wdn[:, jj, :], w_dn[j * 128:(j + 1) * 128, :], writes=['wdn%d' % jj])
                  load_half(0)
                  def dstF(t, pb, bkey):
                      copy_op('act', hnT[:, :, t * 128:(t + 1) * 128], pb.rearrange("p (k t) -> p k t", k=8), reads=[bkey], writes=[bkey, 'hnT%d' % t])
                  norm_T_all('F', lambda t: h_sb[:, t, :], lambda t: 'h%d' % t, gF[:, :], 'gF', pff, dstF)
                  for hf in range(2):
                      if hf == 1:
                          load_half(1)
                      for tb in range(4):
                          cs = slice(tb * 512, (tb + 1) * 512)
                          hk = ['hnT%d' % (4 * tb + i) for i in range(4)]
                          for jj in range(11):
                              bg = 2 + (jj % 2) * 2
                              bu = bg + 1
                              for kc in range(8):
                                  S.op('pe', lambda e, jj=jj, kc=kc, cs=cs, bg=bg: e.matmul(ps[bg][:, :], wgu[:, jj, kc, 0:128], hnT[:, kc, cs], start=(kc == 0), stop=(kc == 7)),
                                       reads=hk + ['wgu%d' % jj], writes=['ps%d' % bg], sig=(kc == 7))
                              for kc in range(8):
                                  S.op('pe', lambda e, jj=jj, kc=kc, cs=cs, bu=bu: e.matmul(ps[bu][:, :], wgu[:, jj, kc, 128:256], hnT[:, kc, cs], start=(kc == 0), stop=(kc == 7)),
                                       reads=hk + ['wgu%d' % jj], writes=['ps%d' % bu], sig=(kc == 7))
                              s_ = sg[jj % 2]
                              S.op('act', lambda e, s_=s_, bg=bg: e.activation(s_[:, :], ps[bg][:, :], AF.Silu), reads=['ps%d' % bg], writes=['sg%d' % (jj % 2)])
                              S.op('dve', lambda e, s_=s_, bu=bu, jj=jj: e.tensor_tensor(actT[:, jj, :], s_[:, :], ps[bu][:, :], ALU.mult),
                                   reads=['sg%d' % (jj % 2), 'ps%d' % bu], writes=['actT%d' % jj])
                          for i in range(4):
                              t = 4 * tb + i
                              for nh in range(2):
                                  bank = nh
                                  ns = slice(nh * 512, (nh + 1) * 512)
                                  for jj in range(11):
                                      S.op('pe', lambda e, jj=jj, i=i, ns=ns, bank=bank: e.matmul(ps[bank][:, :], actT[:, jj, i * 128:(i + 1) * 128], wdn[:, jj, ns],
                                                                                                 start=(jj == 0), stop=(jj == 10)),
                                           reads=['actT%d' % jj, 'wdn%d' % jj], writes=['ps%d' % bank], sig=(jj == 10))
                                  S.op('dve', lambda e, t=t, ns=ns, bank=bank: e.tensor_tensor(h_sb[:, t, ns], h_sb[:, t, ns], ps[bank][:, :], ALU.add),
                                       reads=['h%d' % t, 'ps%d' % bank], writes=['h%d' % t])
                  flush()
                  stage_done('FFN', [('h0f', h_sb[:, 0, :], 128, D)])
              with ExitStack() as pp:
                  gP = sb("gP", [128, D], F32, pp)
                  gL = sb("gL", [128, D], F32, pp)
                  wpg = sb("wpg", [128, 8, D], BF16, pp)
                  wpp = sb("wpp", [128, 2, D], BF16, pp)
                  lnT = [sb("lnT%d" % i, [128, 8, 128], BF16, pp) for i in range(NT)]
                  pt = [sb("pt%d" % i, [128, 256], F32, pp) for i in range(3)]
                  ptb = [sb("ptb%d" % i, [128, 256], BF16, pp) for i in range(3)]
                  pT = [sb("pT%d" % i, [128, 2, 128], BF16, pp) for i in range(NT)]
                  sig_ = [sb("sig%d" % i, [128, D], F32, pp) for i in range(2)]
                  yo = [sb("yo%d" % i, [128, D], F32, pp) for i in range(2)]
                  fss = sb("fss", [128, NT], F32, pp); flv = sb("flv", [128, NT], F32, pp); frs = sb("frs", [128, NT], F32, pp)
                  fjunk = [sb("fjunk%d" % i, [128, D], BF16, pp) for i in range(2)]
                  S.dma('sp', gP[:], g_ple.partition_broadcast(128), writes=['gP'])
                  S.dma('sp', gL[:], g_fin.partition_broadcast(128), writes=['gL'])
                  S.dma('pool', wpg[:], w_pg.rearrange("(k p) n -> p k n", p=128), writes=['wpg'])
                  S.dma('pool', wpp[:], w_pp.rearrange("(k p) n -> p k n", p=128), writes=['wpp'])

                  def ple_back(t):
                      b3 = t
                      b = t % 2
                      for nh in range(2):
                          ns = slice(nh * 512, (nh + 1) * 512)
                          bg = 4 + nh
                          bq = 6 + nh
                          for kc in range(8):
                              S.op('pe', lambda e, kc=kc, ns=ns, bg=bg: e.matmul(ps[bg][:, :], lnT[b3][:, kc, :], wpg[:, kc, ns], start=(kc == 0), stop=(kc == 7)),
                                   reads=['lnT%d' % b3, 'wpg'], writes=['ps%d' % bg], sig=(kc == 7))
                          for kc in range(2):
                              S.op('pe', lambda e, kc=kc, ns=ns, bq=bq: e.matmul(ps[bq][:, :], pT[b3][:, kc, :], wpp[:, kc, ns], start=(kc == 0), stop=(kc == 1)),
                                   reads=['pT%d' % b3, 'wpp'], writes=['ps%d' % bq], sig=(kc == 1))
                          S.op('act', lambda e, ns=ns, bg=bg: e.activation(sig_[b][:, ns], ps[bg][:, :], AF.Sigmoid), reads=['ps%d' % bg], writes=['ps%d' % bg, 'sig%d_%d' % (b, nh)])
                          S.op('dve', lambda e, ns=ns, bq=bq: e.tensor_tensor(sig_[b][:, ns], sig_[b][:, ns], ps[bq][:, :], ALU.mult),
                               reads=['sig%d_%d' % (b, nh), 'ps%d' % bq], writes=['ps%d' % bq, 'sig%d_%d' % (b, nh)])
                          S.op('dve', lambda e, ns=ns: e.tensor_tensor(h_sb[:, t, ns], h_sb[:, t, ns], sig_[b][:, ns], ALU.add),
                               reads=['h%d' % t, 'sig%d_%d' % (b, nh)], writes=['h%d' % t])
                      S.op('act', lambda e: e.activation(fjunk[b][:, :], h_sb[:, t, :], AF.Square, accum_out=fss[:, t:t + 1]),
                           reads=['h%d' % t], writes=['fjunk%d' % b, 'fss%d' % t])
                      if t % 4 == 3:
                          g = t // 4
                          gs = slice(4 * g, 4 * g + 4)
                          S.op('act', lambda e: e.activation(flv[:, gs], fss[:, gs], AF.Ln, bias=EPS, scale=1.0 / D),
                               reads=['fss%d' % tt for tt in range(4 * g, 4 * g + 4)], writes=['flv%d' % g])
                          S.op('act', lambda e: e.activation(frs[:, gs], flv[:, gs], AF.Exp, scale=-0.5), reads=['flv%d' % g], writes=['frs%d' % g])
                          for tt in range(4 * g, 4 * g + 4):
                              bb = tt % 2
                              S.op('dve', lambda e, tt=tt, bb=bb: e.scalar_tensor_tensor(yo[bb][:, :], h_sb[:, tt, :], frs[:, tt:tt + 1], gL[:, :], ALU.mult, ALU.mult),
                                   reads=['h%d' % tt, 'frs%d' % g, 'gL'], writes=['yo%d' % bb])
                              S.dma('sp', y[tt * 128:(tt + 1) * 128, :], yo[bb][:, :], reads=['yo%d' % bb], is_out=True)

                  def dstP(t, pb, bkey):
                      b3 = t % 3
                      copy_op('act', lnT[t][:, :, :], pb.rearrange("p (k t) -> p k t", k=8), reads=[bkey], writes=[bkey, 'lnT%d' % t])
                      S.op('pool', lambda e: e.tensor_copy(ptb[b3][:, :], pt[b3][:, :]), reads=['pt%d' % b3], writes=['ptb%d' % b3])
                      if t + 2 < NT:
                          preP(t + 2)
                      bank = 2 + t % 2
                      pb2 = psb(bank)
                      for kc in range(2):
                          S.op('pe', lambda e, kc=kc: e.transpose(pb2[:, kc * 128:(kc + 1) * 128], ptb[b3][:, kc * 128:(kc + 1) * 128], identb),
                               reads=['ptb%d' % b3, 'cb'], writes=['ps%d' % bank], sig=(kc == 1))
                      copy_op('dve', pT[t][:, :, :], pb2[:, 0:256].rearrange("p (k t) -> p k t", k=2), reads=['ps%d' % bank], writes=['ps%d' % bank, 'pT%d' % t])

                  def preP(t):
                      S.dma('sp', pt[t % 3][:], p[t * 128:(t + 1) * 128, :], writes=['pt%d' % (t % 3)])
                  preP(0); preP(1)
                  norm_T_all('P', lambda t: h_sb[:, t, :], lambda t: 'h%d' % t, gP[:, :], 'gP', pp, dstP)
                  for t_ in range(NT):
                      ple_back(t_)
                  S.finish()
                  flush()
    return nc


_CACHE = {}


def kernel(**inputs):
    f32 = lambda a: np.ascontiguousarray(np.asarray(a, dtype=np.float32))
    x = f32(inputs['x']); p = f32(inputs['p'])
    B = x.shape[0]
    shared = {
        'w_in': f32(inputs['w_in'][0]), 'w_out': f32(inputs['w_out'][0]),
        'w_gate_up': f32(inputs['w_gate_up'][0]), 'w_down': f32(inputs['w_down'][0]),
        'w_ple_gate': f32(inputs['w_ple_gate'][0]), 'w_ple_proj': f32(inputs['w_ple_proj'][0]),
        'attn_norm_w': f32(inputs['attn_norm_w'][0]), 'ffn_norm_w': f32(inputs['ffn_norm_w'][0]),
        'ple_norm_w': f32(inputs['ple_norm_w'][0]), 'final_norm_w': f32(inputs['final_norm_w']),
        'conv_wT': f32(np.asarray(inputs['conv_w'][0]).T.reshape(12, 128, 4).transpose(1, 0, 2).reshape(128, 48)),
        'smallv': f32(np.concatenate([np.asarray(inputs['dt_bias'][0]).reshape(-1), np.asarray(inputs['a_log'][0]).reshape(-1), np.zeros(56)])),
        'gdn_norm_w': f32(inputs['gdn_norm_w'][0]), 'fox_f_bias': f32(inputs['fox_f_bias'][0]),
        'cst': make_consts(),
    }
    if 'nc' not in _CACHE:
        _CACHE['nc'] = build_program(skip_gdn=SKIP_GDN)
    nc = _CACHE['nc']
    in_maps = []
    for b in range(B):
        m = dict(shared)
        m['x'] = f32(x[b]); m['p'] = f32(p[0, b])
        in_maps.append(m)
    res = run_bass_kernel_spmd(nc, in_maps, core_ids=list(range(B)))
    return np.stack([np.asarray(r['y'], dtype=np.float32) for r in res.results], axis=0)


SKIP_GDN = False
NDUMMY = 2
```

```python
import numpy as np
from contextlib import ExitStack
import concourse.bass as bass
import concourse.mybir as mybir
from concourse.bass_utils import run_bass_kernel_spmd

F32 = mybir.dt.float32
BF16 = mybir.dt.bfloat16
AF = mybir.ActivationFunctionType
ALU = mybir.AluOpType


class Sched:
    ENGS = ('pe', 'act', 'dve', 'pool', 'sp')
    NDMA = {'sp': 8, 'pool': 8, 'act': 4}

    def __init__(self, nc, es):
        self.nc = nc
        self.sem = {}
        for k in self.ENGS:
            self.sem[k] = es.enter_context(nc.semaphore("sem_" + k))
        self.dpool = {}
        for q, n in self.NDMA.items():
            keys = []
            for i in range(n):
                key = "d_%s%d" % (q, i)
                self.sem[key] = es.enter_context(nc.semaphore(key))
                keys.append(key)
            self.dpool[q] = keys
        self.tot = {k: 0 for k in self.sem}
        self.seen = {k: {} for k in self.ENGS}
        self.prog = {k: [] for k in self.ENGS}
        self.last_w = {}
        self.readers = {}
        self.rr = {q: 0 for q in self.dpool}
        self.out_tokens = []
        self.nops = 0
        self.dead = False

    def _deps(self, reads, writes):
        deps = []
        for r in reads:
            t = self.last_w.get(r)
            if t:
                deps.append(t)
        for w in writes:
            t = self.last_w.get(w)
            if t:
                deps.append(t)
            deps.extend(self.readers.get(w, ()))
        return deps

    def _resolve(self, eng, deps):
        need = {}
        seen = self.seen[eng]
        for (k, v) in deps:
            if k == eng and eng == 'pe':
                continue
            if seen.get(k, 0) >= v:
                continue
            if need.get(k, 0) < v:
                need[k] = v
        for k, v in need.items():
            seen[k] = v
        return list(need.items())

    def _commit(self, token, reads, writes):
        for w in writes:
            self.last_w[w] = token
            self.readers[w] = []
        for r in reads:
            self.readers.setdefault(r, []).append(token)

    def op(self, eng, fn, reads=(), writes=(), sig=True):
        if self.dead:
            return
        waits = self._resolve(eng, self._deps(reads, writes))
        if sig:
            self.tot[eng] += 1
            token = (eng, self.tot[eng])
            inc = (eng, 1)
        else:
            token = (eng, self.tot[eng] + 1)
            inc = None
        self.prog[eng].append((waits, fn, inc))
        self._commit(token, reads, writes)
        self.nops += 1

    def dma(self, q, out, in_, reads=(), writes=(), is_out=False):
        if self.dead:
            return
        pool = self.dpool[q]
        key = pool[self.rr[q] % len(pool)]
        self.rr[q] += 1
        deps = self._deps(reads, writes)
        if self.tot[key] > 0:
            deps.append((key, self.tot[key]))
        waits = self._resolve(q, deps)
        self.tot[key] += 16
        token = (key, self.tot[key])
        self.prog[q].append((waits, (lambda e, o=out, i=in_: e.dma_start(out=o, in_=i)), (key, 16)))
        self._commit(token, reads, writes)
        if is_out:
            self.out_tokens.append(token)
        self.nops += 1

    def barrier(self):
        if self.dead:
            return
        allt = [(k, v) for k, v in self.tot.items() if v > 0]
        for e in self.ENGS:
            waits = self._resolve(e, allt)
            if waits:
                self.prog[e].append((waits, None, None))

    def finish(self):
        self.barrier()

    def emit(self, block):
        def mk(k):
            def f(e):
                for waits, fn, inc in self.prog[k]:
                    for (sk, v) in waits:
                        e.wait_ge(self.sem[sk], v)
                    if fn is not None:
                        ins = fn(e)
                        if inc is not None:
                            ins.then_inc(self.sem[inc[0]], inc[1])
            return f
        block.tensor(mk('pe'))
        block.scalar(mk('act'))
        block.vector(mk('dve'))
        block.gpsimd(mk('pool'))
        block.sync(mk('sp'))
        self.prog = {k: [] for k in self.ENGS}


S_LEN = 2048
D = 1024
NT = 16
DFF = 2816
NFF = 22
EPS = 1e-6
C_GQKV, C_GZ, C_GA, C_GB, C_FQ, C_FK, C_FV, C_FF = 0, 1536, 2048, 2052, 2056, 2568, 3080, 3592


def make_consts():
    c = np.zeros((128, 11, 128), np.float32)
    i = np.arange(128)
    c[:, 0, :] = np.eye(128)
    c[:, 1, :] = (i[:, None] <= i[None, :])
    c[:, 2, :] = 1.0
    c[:, 3, :] = (i[:, None] < i[None, :])
    c[:, 4, :] = 0.0
    c[127, 4, :] = 1.0
    c[:, 5, :] = (i[:, None] > i[None, :])
    c[:, 6, :] = -1.0 * (i[:, None] > i[None, :])
    c[:, 7, :] = (i[:, None] // 16 == i[None, :] // 16)
    for n_, sz in ((8, 16), (9, 32), (10, 64)):
        c[:, n_, :] = (i[:, None] // (2 * sz) == i[None, :] // (2 * sz)) & (i[:, None] // sz != i[None, :] // sz)
    return c.reshape(128, 11 * 128)


class _Stop(Exception):
    pass


def build_program(skip_gdn=False, dbg=(), stop_after=None):
    nc = bass.Bass("TRN2", target_bir_lowering=False)

    def din(name, shape):
        return nc.dram_tensor(name, list(shape), F32, kind="ExternalInput").ap()
    x = din("x", [S_LEN, D]); p = din("p", [S_LEN, 256])
    w_in = din("w_in", [D, 3600]); w_out = din("w_out", [D, D])
    w_gu = din("w_gate_up", [D, 2 * DFF]); w_dn = din("w_down", [DFF, D])
    w_pg = din("w_ple_gate", [D, D]); w_pp = din("w_ple_proj", [256, D])
    g_attn = din("attn_norm_w", [D]); g_ffn = din("ffn_norm_w", [D])
    g_ple = din("ple_norm_w", [D]); g_fin = din("final_norm_w", [D])
    conv_wT = din("conv_wT", [128, 12 * 4]); smallv = din("smallv", [64])
    gdn_nw = din("gdn_norm_w", [128]); fox_fb = din("fox_f_bias", [8])
    cst = din("cst", [128, 11 * 128])
    y = nc.dram_tensor("y", [S_LEN, D], F32, kind="ExternalOutput").ap()
    dbg_out = {}

    with ExitStack() as es:
        def sb(name, shape, dt, stack=None):
            try:
                return (stack or es).enter_context(nc.sbuf_tensor(name, list(shape), dt))
            except AssertionError:
                print("ALLOC FAIL", name, shape, dt, "live:", [(k, v) for k, v in alloc_log])
                raise
            finally:
                alloc_log.append((name, int(np.prod(shape[1:])) * (4 if dt == F32 else 2)))
        alloc_log = []
        ps = [es.enter_context(nc.psum_tensor("ps%d" % i, [128, 512], F32)) for i in range(8)]
        S = Sched(nc, es)

        def flush():
            S.barrier()
            with nc.Block() as block:
                S.emit(block)

        def stage_done(name, dumps=()):
            for (nm, ap2d, rows, cols) in dumps:
                if nm in dbg:
                    d = nc.dram_tensor("dbg_" + nm, [rows, cols], ap2d.dtype, kind="ExternalOutput").ap()
                    S.dma('sp', d, ap2d, is_out=True)
                    flush()
            if stop_after == name:
                S.finish()
                flush()
                S.dead = True

        cf = sb("cf", [128, 11, 128], F32)
        cb = sb("cb", [128, 11, 128], BF16)
        S.dma('sp', cf[:].rearrange("p a b -> p (a b)"), cst, writes=['cf'])
        S.dma('pool', cb[:].rearrange("p a b -> p (a b)"), cst, writes=['cb'])
        identb = cb[:, 0, :]; identf = cf[:, 0, :]; trif = cf[:, 1, :]; onesf = cf[:, 2, :]
        mask_le_b = cb[:, 1, :]; mask_lt_f = cf[:, 3, :]; mask_le_f = cf[:, 1, :]; sel127 = cf[:, 4, :]; mask_gt_f = cf[:, 5, :]; mask_gt_neg_f = cf[:, 6, :]

        def psb(i):
            return ps[i][:, :].bitcast(BF16)

        cnt = {'ev': 0}

        def evac_engine():
            cnt['ev'] += 1
            return 'act' if cnt['ev'] % 2 else 'dve'

        def copy_op(eng, out, in_, reads, writes):
            if eng == 'act':
                S.op('act', lambda e: e.activation(out, in_, AF.Copy), reads=reads, writes=writes)
            else:
                S.op(eng, lambda e: e.tensor_copy(out, in_), reads=reads, writes=writes)

        def rms_norm_T(src, src_key, gain, gain_key, dstT, dst_key, c0, scratch, tag, bank):
            junk, ssq, xs = scratch
            k = tag
            S.op('act', lambda e: e.activation(junk[:, :], src, AF.Square, accum_out=ssq[:, 0:1]),
                 reads=[src_key], writes=['junk' + k, 'ssq' + k])
            S.op('act', lambda e: e.activation(ssq[:, 1:2], ssq[:, 0:1], AF.Ln, bias=EPS, scale=1.0 / D),
                 reads=['ssq' + k], writes=['ssq1' + k])
            S.op('act', lambda e: e.activation(ssq[:, 2:3], ssq[:, 1:2], AF.Exp, scale=-0.5),
                 reads=['ssq1' + k], writes=['rstd' + k])
            S.op('dve', lambda e: e.scalar_tensor_tensor(xs[:, :], src, ssq[:, 2:3], gain, ALU.mult, ALU.mult),
                 reads=[src_key, 'rstd' + k, gain_key], writes=['xs' + k])
            pb = psb(bank)
            for kc in range(8):
                S.op('pe', lambda e, kc=kc: e.transpose(pb[:, kc * 128:(kc + 1) * 128], xs[:, kc * 128:(kc + 1) * 128], identb),
                     reads=['xs' + k, 'cb'], writes=['ps%d' % bank], sig=(kc == 7))
            copy_op(evac_engine(), dstT[:, :, c0:c0 + 128], pb.rearrange("p (k t) -> p k t", k=8),
                    reads=['ps%d' % bank], writes=[dst_key])


        def lockstep(n, body):
            recs = []
            for ii_ in range(n):
                rec = []
                S.op = lambda *a, _r=rec, **k: _r.append((a, k))
                try:
                    body(ii_)
                finally:
                    del S.op
                recs.append(rec)
            for k_ in range(max(len(r) for r in recs)):
                for r in recs:
                    if k_ < len(r):
                        a_, kw_ = r[k_]
                        S.op(*a_, **kw_)


        def norm_T_all(tag, src_of, srckey_of, gain, gain_key, stack, emit_dst, banks=(0, 1), pre_tile=None):
            ssq = sb("ssq_" + tag, [128, NT], F32, stack)
            lnv = sb("lnv_" + tag, [128, NT], F32, stack)
            rstd = sb("rstd_" + tag, [128, NT], F32, stack)
            junk = [sb("junk_%s%d" % (tag, i), [128, D], BF16, stack) for i in range(2)]
            xs = [sb("xs_%s%d" % (tag, i), [128, D], BF16, stack) for i in range(2)]

            def stats(g):
                for t in range(4 * g, 4 * g + 4):
                    if pre_tile is not None:
                        pre_tile(t)
                    S.op('act', lambda e, t=t: e.activation(junk[t % 2][:, :], src_of(t), AF.Square, accum_out=ssq[:, t:t + 1]),
                         reads=[srckey_of(t)], writes=['junk_%s%d' % (tag, t % 2), 'ssq_%s%d' % (tag, t)])
                gs = slice(4 * g, 4 * g + 4)
                S.op('act', lambda e: e.activation(lnv[:, gs], ssq[:, gs], AF.Ln, bias=EPS, scale=1.0 / D),
                     reads=['ssq_%s%d' % (tag, t) for t in range(4 * g, 4 * g + 4)], writes=['lnv_%s%d' % (tag, g)])
                S.op('act', lambda e: e.activation(rstd[:, gs], lnv[:, gs], AF.Exp, scale=-0.5),
                     reads=['lnv_%s%d' % (tag, g)], writes=['rstd_%s%d' % (tag, g)])

            def scale(t):
                b = t % 2
                S.op('dve', lambda e, t=t, b=b: e.scalar_tensor_tensor(xs[b][:, :], src_of(t), rstd[:, t:t + 1], gain, ALU.mult, ALU.mult),
                     reads=[srckey_of(t), 'rstd_%s%d' % (tag, t // 4), gain_key], writes=['xs_%s%d' % (tag, b)])

            def apply(g):
                for t in range(4 * g, 4 * g + 4):
                    b = t % 2
                    bank = banks[b]
                    if t == 0:
                        scale(0)
                    if t + 1 < NT:
                        scale(t + 1)
                    pb = psb(bank)
                    for kc in range(8):
                        S.op('pe', lambda e, kc=kc, b=b, pb=pb: e.transpose(pb[:, kc * 128:(kc + 1) * 128], xs[b][:, kc * 128:(kc + 1) * 128], identb),
                             reads=['xs_%s%d' % (tag, b), 'cb'], writes=['ps%d' % bank], sig=(kc == 7))
                    emit_dst(t, pb, 'ps%d' % bank)
            stats(0)
            for g in range(4):
                if g + 1 < 4:
                    stats(g + 1)
                apply(g)
            return rstd

        if True:
          with ExitStack() as mx:
              oT_g = sb("oT_g", [128, 4, S_LEN], BF16, mx)
              xnT = sb("xnT", [128, 8, S_LEN], BF16, mx)
              scal = sb("scal", [128, NT, 16], F32, mx)
              cfx = sb("cfx", [128, NT, 8], F32, mx)
              c123 = sb("c123", [128, 3, NT, 8], BF16, mx)
              with ExitStack() as pa:
                  gA = sb("gA", [128, D], F32, pa)
                  xall = sb("xall", [128, NT, D], F32, pa)
                  wsm = sb("wsm", [128, 8, 16], BF16, pa)
                  S.dma('sp', gA[:], g_attn.partition_broadcast(128), writes=['gA'])
                  S.dma('pool', wsm[:, :, 0:8], w_in[:, C_GA:C_GA + 8].rearrange("(k p) n -> p k n", p=128), writes=['wsm'])
                  S.dma('pool', wsm[:, :, 8:16], w_in[:, C_FF:C_FF + 8].rearrange("(k p) n -> p k n", p=128), writes=['wsm'])
                  for t in range(NT):
                      S.dma('sp', xall[:, t, :], x[t * 128:(t + 1) * 128, :], writes=['xall%d' % t])

                  def dstA(t, pb, bkey):
                      copy_op('act', xnT[:, :, t * 128:(t + 1) * 128], pb.rearrange("p (k t) -> p k t", k=8), reads=[bkey], writes=[bkey, 'xnT%d' % t])
                  norm_T_all('A', lambda t: xall[:, t, :], lambda t: 'xall%d' % t, gA[:, :], 'gA', pa, dstA)
                  for t in range(NT):
                      for kc in range(8):
                          S.op('pe', lambda e, t=t, kc=kc: e.matmul(ps[2][:, t * 16:(t + 1) * 16], xnT[:, kc, t * 128:(t + 1) * 128],
                                                                   wsm[:, kc, :], start=(kc == 0), stop=(kc == 7)),
                               reads=['xnT%d' % t, 'wsm'], writes=['ps2'], sig=(kc == 7 and t == NT - 1))
                  S.op('dve', lambda e: e.tensor_copy(scal[:].rearrange("p t n -> p (t n)"), ps[2][:, 0:NT * 16]),
                       reads=['ps2'], writes=['scal'])
                  flush()
                  stage_done('A', [('xnT', xnT[:, 0, :], 128, S_LEN), ('scal', scal[:].rearrange("p t n -> p (t n)"), 128, NT * 16)])
              with ExitStack() as pb_:
                  fbt = sb("fbt", [128, 8], F32, pb_)
                  tA = sb("tA", [128, NT, 8], F32, pb_)
                  tB = sb("tB", [128, NT, 8], F32, pb_)
                  tC = sb("tC", [128, NT, 8], F32, pb_)
                  pre = sb("pre", [128, NT, 8], F32, pb_)
                  S.dma('sp', fbt[:], fox_fb.partition_broadcast(128), writes=['fbt'])
                  ff = scal[:, :, 8:16]
                  for t in range(NT):
                      S.op('dve', lambda e, t=t: e.tensor_tensor(tA[:, t, :], scal[:, t, 8:16], fbt[:, :], ALU.add),
                           reads=['scal', 'fbt'], writes=['tA'])
                  S.op('act', lambda e: e.activation(tB[:], tA[:], AF.Abs), reads=['tA'], writes=['tB'])
                  S.op('act', lambda e: e.activation(tB[:], tB[:], AF.Exp, scale=-1.0), reads=['tB'], writes=['tB'])
                  S.op('act', lambda e: e.activation(tB[:], tB[:], AF.Ln, bias=1.0), reads=['tB'], writes=['tB'])
                  S.op('dve', lambda e: e.tensor_single_scalar(tC[:], tA[:], 0.0, ALU.min), reads=['tA'], writes=['tC'])
                  S.op('dve', lambda e: e.tensor_tensor(tA[:], tC[:], tB[:], ALU.subtract), reads=['tB', 'tC'], writes=['tA'])
                  flush(); stage_done('B1')
                  lf2 = tA[:].rearrange("p t n -> p (t n)")
                  S.op('pe', lambda e: e.matmul(ps[3][:, 0:128], trif, lf2, start=True, stop=True), reads=['tA', 'cf'], writes=['ps3'])
                  S.op('pe', lambda e: e.matmul(ps[3][:, 128:256], onesf, lf2, start=True, stop=True), reads=['tA', 'cf'], writes=['ps3'])
                  S.op('dve', lambda e: e.tensor_copy(tB[:].rearrange("p t n -> p (t n)"), ps[3][:, 0:128]), reads=['ps3'], writes=['tB'])
                  S.op('act', lambda e: e.activation(tC[:].rearrange("p t n -> p (t n)"), ps[3][:, 128:256], AF.Copy), reads=['ps3'], writes=['tC'])
                  flush(); stage_done('B2')
                  S.op('dve', lambda e: e.memset(pre[:, 0, :], 0.0), writes=['pre'])
                  for t in range(1, NT):
                      S.op('dve', lambda e, t=t: e.tensor_tensor(pre[:, t, :], pre[:, t - 1, :], tC[:, t - 1, :], ALU.add),
                           reads=['pre', 'tC'], writes=['pre'])
                  S.op('dve', lambda e: e.tensor_tensor(cfx[:], tB[:], pre[:], ALU.add), reads=['tB', 'pre'], writes=['cfx'])
                  flush(); stage_done('B3')
                  S.op('dve', lambda e: e.tensor_copy(c123[:, 0, :, :], cfx[:]), reads=['cfx'], writes=['c1'])
                  S.op('dve', lambda e: e.tensor_tensor(tA[:], cfx[:], c123[:, 0, :, :], ALU.subtract), reads=['cfx', 'c1'], writes=['tA'])
                  S.op('dve', lambda e: e.tensor_copy(c123[:, 1, :, :], tA[:]), reads=['tA'], writes=['c2'])
                  S.op('dve', lambda e: e.tensor_tensor(tB[:], tA[:], c123[:, 1, :, :], ALU.subtract), reads=['tA', 'c2'], writes=['tB'])
                  S.op('dve', lambda e: e.tensor_copy(c123[:, 2, :, :], tB[:]), reads=['tB'], writes=['c3'])
                  flush()
                  stage_done('B', [('cfx', cfx[:].rearrange("p t n -> p (t n)"), 128, NT * 8)])


              if not skip_gdn:
                with ExitStack() as pg:
                  wz = sb("wz", [128, 8, 512], BF16, pg)
                  gate = sb("gate", [128, NT, 512], BF16, pg)
                  gnw4 = sb("gnw4", [128, 512], F32, pg)
                  smv = sb("smv", [128, 64], F32, pg)
                  dtb = smv[:, 0:4]
                  alg = smv[:, 4:8]
                  cw = sb("cw", [128, 12, 4], F32, pg)
                  sA = sb("sA", [128, NT, 4], F32, pg); sB = sb("sB", [128, NT, 4], F32, pg); sC = sb("sC", [128, NT, 4], F32, pg)
                  beta = sb("beta", [128, NT, 4], F32, pg); gc = sb("gc", [128, NT, 4], F32, pg)
                  egl = sb("egl", [128, NT, 4], F32, pg); edec = sb("edec", [128, NT, 4], F32, pg); kbs = sb("kbs", [128, NT, 4], F32, pg)
                  wq3 = [sb("wq3_%d" % i, [128, 8, 128], BF16, pg) for i in range(3)]
                  pre = sb("gpre", [128, 4 + S_LEN], BF16, pg)
                  dg = sb("dg", [128, 4, 128], BF16, pg)
                  qf = sb("qf", [128, S_LEN], F32, pg); kf = sb("kf", [128, S_LEN], F32, pg); vfb = sb("vfb", [128, S_LEN], BF16, pg)
                  sqb = [sb("sqb%d" % i, [128, 512], BF16, pg) for i in range(2)]
                  rsb = [sb("rsb%d" % i, [128, 512], F32, pg) for i in range(2)]
                  onesb = cb[:, 2, :]
                  rs = sb("rs", [128, 512], F32, pg)
                  qTb = sb("qTb", [128, S_LEN], BF16, pg); kTb = sb("kTb", [128, S_LEN], BF16, pg)
                  kbg = sb("kbg", [128, NT, 128], BF16, pg); vb = sb("vb", [128, NT, 128], BF16, pg)
                  kdec = [sb("kdec%d" % i, [128, NT, 128], BF16, pg) for i in range(2)]
                  u_ = [sb("u_%d" % i, [128, NT, 128], BF16, pg) for i in range(2)]
                  wT = [sb("wT%d" % i, [128, NT, 128], BF16, pg) for i in range(2)]
                  qkm = [sb("qkm%d" % i, [128, NT, 128], BF16, pg) for i in range(2)]
                  qgT = [sb("qgT%d" % i, [128, S_LEN], BF16, pg) for i in range(2)]
                  NI = 4
                  dgc = [sb("dgc%d" % i, [128, 128], F32, pg) for i in range(NI)]
                  t1 = [sb("t1_%d" % i, [128, 128], F32, pg) for i in range(NI)]
                  t2 = [sb("t2_%d" % i, [128, 128], F32, pg) for i in range(NI)]
                  eg = [sb("eg%d" % i, [128, 128], F32, pg) for i in range(NI)]
                  Pm = [[sb("Pm%d_%d" % (i, j), [128, 128], BF16, pg) for j in range(2)] for i in range(NI)]
                  Qm = [[sb("Qm%d_%d" % (i, j), [128, 128], BF16, pg) for j in range(2)] for i in range(NI)]
                  Ym = [[sb("Ym%d_%d" % (i, j), [128, 128], BF16, pg) for j in range(2)] for i in range(NI)]
                  Uo16 = [sb("Uo16_%d" % i, [128, 128], BF16, pg) for i in range(NI)]
                  Uo32 = [sb("Uo32_%d" % i, [128, 128], BF16, pg) for i in range(NI)]
                  Lo64 = [sb("Lo64_%d" % i, [128, 128], BF16, pg) for i in range(NI)]
                  bd16_b = cb[:, 7, :]; off16_b = cb[:, 8, :]; off32_b = cb[:, 9, :]; off64_b = cb[:, 10, :]
                  Sf = [sb("Sf%d" % j, [128, 128], F32, pg) for j in range(2)]; Sb = [sb("Sb%d" % j, [128, 128], BF16, pg) for j in range(2)]
                  vn = [[sb("vn%d_%d" % (j, i), [128, 128], BF16, pg) for i in range(2)] for j in range(2)]
                  og = [[sb("og%d_%d" % (j, i), [128, 128], BF16, pg) for i in range(2)] for j in range(2)]
                  ost = [[sb("ost%d_%d" % (j, i), [128, 4], F32, pg) for i in range(2)] for j in range(2)]
                  ojk = [[sb("ojk%d_%d" % (j, i), [128, 128], BF16, pg) for i in range(2)] for j in range(2)]
                  S.dma('pool', wz[:], w_in[:, C_GZ:C_GZ + 512].rearrange("(k p) n -> p k n", p=128), writes=['wz'])
                  for i in range(4):
                      S.dma('sp', gnw4[:, i * 128:(i + 1) * 128], gdn_nw.partition_broadcast(128), writes=['gnw4'])
                  S.dma('sp', smv[:], smallv.partition_broadcast(128), writes=['dtb', 'alg'])
                  S.dma('sp', cw[:].rearrange("p a b -> p (a b)"), conv_wT, writes=['cw'])
                  flush(); stage_done('G00')
                  for t in range(NT):
                      S.op('dve', lambda e, t=t: e.tensor_tensor(sA[:, t, :], scal[:, t, 0:4], dtb, ALU.add), reads=['scal', 'dtb'], writes=['sA'])
                  S.op('act', lambda e: e.activation(sB[:], sA[:], AF.Abs), reads=['sA'], writes=['sB'])
                  S.op('act', lambda e: e.activation(sB[:], sB[:], AF.Exp, scale=-1.0), reads=['sB'], writes=['sB'])
                  S.op('act', lambda e: e.activation(sB[:], sB[:], AF.Ln, bias=1.0), reads=['sB'], writes=['sB'])
                  S.op('dve', lambda e: e.tensor_single_scalar(sC[:], sA[:], 0.0, ALU.max), reads=['sA'], writes=['sC'])
                  S.op('dve', lambda e: e.tensor_tensor(sA[:], sC[:], sB[:], ALU.add), reads=['sB', 'sC'], writes=['sA'])
                  S.op('act', lambda e: e.activation(alg, alg, AF.Exp), reads=['alg'], writes=['alg'])
                  S.op('dve', lambda e: e.tensor_scalar(alg, alg, -1.0, None, ALU.mult), reads=['alg'], writes=['alg'])
                  for t in range(NT):
                      S.op('dve', lambda e, t=t: e.tensor_tensor(sA[:, t, :], sA[:, t, :], alg, ALU.mult), reads=['sA', 'alg'], writes=['sA'])
                  flush(); stage_done('G0m')
                  g2 = sA[:].rearrange("p t n -> p (t n)")
                  S.op('pe', lambda e: e.matmul(ps[3][:, 0:64], trif, g2, start=True, stop=True), reads=['sA', 'cf'], writes=['ps3'])
                  S.op('pe', lambda e: e.matmul(ps[3][:, 64:128], onesf, g2, start=True, stop=True), reads=['sA', 'cf'], writes=['ps3'])
                  S.op('dve', lambda e: e.tensor_copy(gc[:].rearrange("p t n -> p (t n)"), ps[3][:, 0:64]), reads=['ps3'], writes=['gc'])
                  S.op('act', lambda e: e.activation(egl[:].rearrange("p t n -> p (t n)"), ps[3][:, 64:128], AF.Exp), reads=['ps3'], writes=['egl'])
                  S.op('dve', lambda e: e.tensor_tensor(edec[:].rearrange("p t n -> p (t n)"), ps[3][:, 64:128], gc[:].rearrange("p t n -> p (t n)"), ALU.subtract),
                       reads=['ps3', 'gc'], writes=['edec'])
                  S.op('act', lambda e: e.activation(edec[:], edec[:], AF.Exp), reads=['edec'], writes=['edec'])
                  flush(); stage_done('G0n')
                  for t in range(NT):
                      S.op('dve', lambda e, t=t: e.tensor_copy(sC[:, t, :], scal[:, t, 4:8]), reads=['scal', 'sC'], writes=['sC'])
                  S.op('act', lambda e: e.activation(beta[:], sC[:], AF.Sigmoid), reads=['sC'], writes=['beta'])
                  S.op('act', lambda e: e.activation(kbs[:], gc[:], AF.Exp), reads=['gc'], writes=['kbs'])
                  S.op('dve', lambda e: e.tensor_tensor(kbs[:], kbs[:], beta[:], ALU.mult), reads=['kbs', 'beta'], writes=['kbs'])
                  flush(); stage_done('G0a')
                  for t in range(NT):
                      bank = t % 2
                      for kc in range(8):
                          S.op('pe', lambda e, t=t, kc=kc, bank=bank: e.matmul(ps[bank][:, :], xnT[:, kc, t * 128:(t + 1) * 128], wz[:, kc, :], start=(kc == 0), stop=(kc == 7)),
                               reads=['xnT%d' % t, 'wz'], writes=['ps%d' % bank], sig=(kc == 7))
                      S.op('act', lambda e, t=t, bank=bank: e.activation(rs[:, :], ps[bank][:, :], AF.Silu), reads=['ps%d' % bank], writes=['rs'])
                      S.op('dve', lambda e, t=t: e.tensor_tensor(gate[:, t, :], rs[:, :], gnw4[:, :], ALU.mult), reads=['rs', 'gnw4'], writes=['gate%d' % t])
                  S.op('pool', lambda e: e.memset(pre[:, 0:4], 0.0), writes=['pre_pad'])
                  flush()
                  stage_done('G1')
                  def gdn_prep(h, si):
                      G = 'g%d_' % si
                      for wi, (cc, dst, dkey) in enumerate(((h, qf, 'qf'), (4 + h, kf, 'kf'), (8 + h, vfb, 'vfb'))):
                          S.dma('pool', wq3[wi][:], w_in[:, cc * 128:(cc + 1) * 128].rearrange("(k p) n -> p k n", p=128), writes=['wq3_%d' % wi])
                          for k in range(4):
                              S.op('dve', lambda e, k=k, cc=cc: e.tensor_scalar(dg[:, k, :], identf, cw[:, cc, k:k + 1], None, ALU.mult), reads=['cf', 'cw'], writes=['dg'])
                          for tb in range(4):
                              bank = 2 + tb % 2
                              for kc in range(8):
                                  S.op('pe', lambda e, wi=wi, kc=kc, tb=tb, bank=bank: e.matmul(ps[bank][:, :], wq3[wi][:, kc, :], xnT[:, kc, tb * 512:(tb + 1) * 512], start=(kc == 0), stop=(kc == 7)),
                                       reads=['xnT%d' % (4 * tb + i) for i in range(4)] + ['wq3_%d' % wi], writes=['ps%d' % bank], sig=(kc == 7))
                              copy_op(evac_engine(), pre[:, 4 + tb * 512:4 + (tb + 1) * 512], ps[bank][:, :], reads=['ps%d' % bank], writes=['ps%d' % bank, 'pre%d' % tb])
                              yield
                          for tb in range(4):
                              bank = 4 + tb % 2
                              rd = ['pre%d' % tb, 'dg', 'pre_pad'] + (['pre%d' % (tb - 1)] if tb > 0 else [])
                              for k in range(4):
                                  S.op('pe', lambda e, k=k, tb=tb, bank=bank: e.matmul(ps[bank][:, :], dg[:, k, :], pre[:, tb * 512 + k + 1:tb * 512 + k + 513], start=(k == 0), stop=(k == 3)),
                                       reads=rd, writes=['ps%d' % bank], sig=(k == 3))
                              S.op('act', lambda e, dst=dst, tb=tb, bank=bank: e.activation(dst[:, tb * 512:(tb + 1) * 512], ps[bank][:, :], AF.Silu),
                                   reads=['ps%d' % bank], writes=['ps%d' % bank, dkey + '%d' % tb])
                              yield
                      for (src, skey, dstb, dbkey, scl) in ((qf, 'qf', qTb, 'qTb', 128.0 ** -0.5), (kf, 'kf', kTb, 'kTb', 1.0)):
                          def l2_front(tb, src=src, skey=skey):
                              cs = slice(tb * 512, (tb + 1) * 512)
                              bb = tb % 2
                              S.op('act', lambda e: e.activation(sqb[bb][:, :], src[:, cs], AF.Square), reads=[skey + '%d' % tb], writes=['sqb%d' % bb])
                              S.op('pe', lambda e: e.matmul(ps[6 + bb][:, :], onesb, sqb[bb][:, :], start=True, stop=True), reads=['sqb%d' % bb, 'cb'], writes=['ps%d' % (6 + bb)])

                          def l2_back(tb, src=src, skey=skey, dstb=dstb, dbkey=dbkey, scl=scl):
                              cs = slice(tb * 512, (tb + 1) * 512)
                              bb = tb % 2
                              S.op('act', lambda e: e.activation(rsb[bb][:, :], ps[6 + bb][:, :], AF.Ln, bias=EPS), reads=['ps%d' % (6 + bb)], writes=['ps%d' % (6 + bb), 'rsb%d' % bb])
                              S.op('act', lambda e: e.activation(rsb[bb][:, :], rsb[bb][:, :], AF.Exp, scale=-0.5), reads=['rsb%d' % bb], writes=['rsb%d' % bb])
                              S.op('dve', lambda e: e.scalar_tensor_tensor(dstb[:, cs], src[:, cs], scl, rsb[bb][:, :], ALU.mult, ALU.mult),
                                   reads=[skey + '%d' % tb, 'rsb%d' % bb], writes=[dbkey + '%d' % tb])
                          l2_front(0)
                          for tb in range(4):
                              if tb + 1 < 4:
                                  l2_front(tb + 1)
                              l2_back(tb)
                              yield
                      for t in range(NT):
                          ts_ = slice(t * 128, (t + 1) * 128)
                          bk, bv = 2 + t % 2, 4 + t % 2
                          kb_, vb_ = 'ps%d' % bk, 'ps%d' % bv
                          pbk = psb(bk)
                          S.op('pe', lambda e, ts_=ts_, pbk=pbk: e.transpose(pbk[:, 0:128], kTb[:, ts_], identb), reads=['kTb%d' % (t // 4), 'cb'], writes=[kb_])
                          S.op('act', lambda e, t=t, pbk=pbk: e.activation(kbg[:, t, :], pbk[:, 0:128], AF.Copy, scale=kbs[:, t, h:h + 1]), reads=[kb_, 'kbs'], writes=[kb_, 'kbg%d' % t])
                          S.op('dve', lambda e, t=t, pbk=pbk: e.tensor_scalar(kdec[si][:, t, :], pbk[:, 0:128], edec[:, t, h:h + 1], None, ALU.mult), reads=[kb_, 'edec'], writes=[kb_, G + 'kdec%d' % t])
                          pbv = psb(bv)
                          S.op('pe', lambda e, ts_=ts_, pbv=pbv: e.transpose(pbv[:, 0:128], vfb[:, ts_], identb), reads=['vfb%d' % (t // 4), 'cb'], writes=[vb_])
                          S.op('dve', lambda e, t=t, pbv=pbv: e.tensor_scalar(vb[:, t, :], pbv[:, 0:128], beta[:, t, h:h + 1], None, ALU.mult), reads=[vb_, 'beta'], writes=[vb_, 'vb%d' % t])
                          if t % 2 == 1:
                              yield
                      for t0 in range(0, NT, NI):
                          def setup_front(ii):
                              t = t0 + ii
                              ts_ = slice(t * 128, (t + 1) * 128)
                              I = str(ii)
                              gcol = gc[:, t, h:h + 1]
                              S.op('dve', lambda e, ii=ii, gcol=gcol: e.tensor_scalar(dgc[ii][:, :], identf, gcol, None, ALU.mult), reads=['cf', 'gc'], writes=['dgc' + I])
                              S.op('pe', lambda e, ii=ii: e.matmul(ps[ii][:, 0:128], onesf, dgc[ii][:, :], start=True, stop=True), reads=['dgc' + I, 'cf'], writes=['ps' + I])
                              Gp = ps[ii][:, 0:128]
                              S.op('pe', lambda e, ii=ii, ts_=ts_: e.matmul(ps[ii][:, 128:256], kTb[:, ts_], kTb[:, ts_], start=True, stop=True), reads=['kTb%d' % (t // 4)], writes=['ps' + I])
                              S.op('pe', lambda e, ii=ii, ts_=ts_: e.matmul(ps[ii][:, 256:384], kTb[:, ts_], qTb[:, ts_], start=True, stop=True), reads=['kTb%d' % (t // 4), 'qTb%d' % (t // 4)], writes=['ps' + I])
                              S.op('dve', lambda e, ii=ii, gcol=gcol, Gp=Gp: e.tensor_scalar(t2[ii][:, :], Gp, gcol, 0.0, ALU.subtract, ALU.min), reads=['ps' + I, 'gc'], writes=['ps' + I, 't2' + I])
                              S.op('act', lambda e, ii=ii: e.activation(t2[ii][:, :], t2[ii][:, :], AF.Exp), reads=['t2' + I], writes=['t2' + I])
                              S.op('pool', lambda e, ii=ii: e.tensor_tensor(t2[ii][:, :], t2[ii][:, :], mask_le_f, ALU.mult), reads=['t2' + I, 'cf'], writes=['t2' + I])
                              S.op('dve', lambda e, ii=ii, gcol=gcol, Gp=Gp: e.tensor_scalar(t1[ii][:, :], Gp, gcol, 0.0, ALU.subtract, ALU.max), reads=['ps' + I, 'gc'], writes=['ps' + I, 't1' + I])
                              S.op('act', lambda e, ii=ii: e.activation(t1[ii][:, :], t1[ii][:, :], AF.Exp, scale=-1.0), reads=['t1' + I], writes=['t1' + I])
                              S.op('pool', lambda e, ii=ii: e.tensor_tensor(t1[ii][:, :], t1[ii][:, :], mask_gt_f, ALU.mult), reads=['t1' + I, 'cf'], writes=['t1' + I])
                              S.op('act', lambda e, ii=ii, Gp=Gp: e.activation(eg[ii][:, :], Gp, AF.Exp), reads=['ps' + I], writes=['ps' + I, 'eg' + I])
                              S.op('pool', lambda e, ii=ii, ts_=ts_: e.tensor_tensor(qgT[si][:, ts_], qTb[:, ts_], eg[ii][:, :], ALU.mult), reads=['eg' + I, 'qTb%d' % (t // 4)], writes=[G + 'qgT%d' % t])
                              S.op('dve', lambda e, ii=ii, t=t, h=h: e.scalar_tensor_tensor(Pm[ii][0][:, :], ps[ii][:, 128:256], beta[:, t, h:h + 1], t1[ii][:, :], ALU.mult, ALU.mult),
                                   reads=['ps' + I, 'beta', 't1' + I], writes=['ps' + I, 'P0_' + I])
                              S.op('dve', lambda e, ii=ii, t=t: e.tensor_tensor(qkm[si][:, t, :], ps[ii][:, 256:384], t2[ii][:, :], ALU.mult), reads=['ps' + I, 't2' + I], writes=['ps' + I, G + 'qkm%d' % t])
                              pbu = psb(ii)
                              S.op('pe', lambda e, ii=ii, pbu=pbu: e.transpose(pbu[:, 768:896], Pm[ii][0][:, :], identb), reads=['P0_' + I, 'cb'], writes=['ps' + I])
                              S.op('act', lambda e, ii=ii, pbu=pbu: e.activation(Qm[ii][0][:, :], pbu[:, 768:896], AF.Copy), reads=['ps' + I], writes=['ps' + I, 'Q0_' + I])
                          def setup_tail(ii):
                              I = str(ii)
                              S.op('pool', lambda e, ii=ii: e.tensor_tensor(Uo16[ii][:, :], Qm[ii][0][:, :], off16_b, ALU.mult), reads=['Q0_' + I, 'cb'], writes=['Uo16_' + I])
                              S.op('pool', lambda e, ii=ii: e.tensor_tensor(Uo32[ii][:, :], Qm[ii][0][:, :], off32_b, ALU.mult), reads=['Q0_' + I, 'cb'], writes=['Uo32_' + I])
                              S.op('pool', lambda e, ii=ii: e.tensor_tensor(Lo64[ii][:, :], Pm[ii][0][:, :], off64_b, ALU.mult), reads=['P0_' + I, 'cb'], writes=['Lo64_' + I])
                              S.op('pool', lambda e, ii=ii: e.tensor_tensor(Pm[ii][0][:, :], Pm[ii][0][:, :], bd16_b, ALU.mult), reads=['P0_' + I, 'cb'], writes=['P0_' + I])
                              S.op('pool', lambda e, ii=ii: e.tensor_tensor(Qm[ii][0][:, :], Qm[ii][0][:, :], bd16_b, ALU.mult), reads=['Q0_' + I, 'cb'], writes=['Q0_' + I])
                              S.op('dve', lambda e, ii=ii: e.tensor_tensor(Ym[ii][0][:, :], identb, Qm[ii][0][:, :], ALU.subtract), reads=['Q0_' + I, 'cb'], writes=['Y0_' + I])
                          lockstep(NI, setup_front)
                          lockstep(NI, setup_tail)
                          for lv in range(1, 4):
                              a, b = (lv - 1) % 2, lv % 2
                              for ii in range(NI):
                                  I = str(ii)
                                  IB = 'ps%d' % (4 + ii)
                                  S.op('pe', lambda e, ii=ii, a=a: e.matmul(ps[4 + ii][:, 0:128], Qm[ii][a][:, :], Pm[ii][a][:, :], start=True, stop=True), reads=['Q%d_' % a + I, 'P%d_' % a + I], writes=[IB])
                                  if lv <= 2:
                                      S.op('pe', lambda e, ii=ii, a=a: e.matmul(ps[4 + ii][:, 128:256], Pm[ii][a][:, :], Qm[ii][a][:, :], start=True, stop=True), reads=['Q%d_' % a + I, 'P%d_' % a + I], writes=[IB])
                                  S.op('act', lambda e, ii=ii, b=b: e.activation(Pm[ii][b][:, :], ps[4 + ii][:, 0:128], AF.Copy), reads=[IB], writes=[IB, 'P%d_' % b + I])
                                  if lv <= 2:
                                      S.op('dve', lambda e, ii=ii, b=b: e.tensor_copy(Qm[ii][b][:, :], ps[4 + ii][:, 128:256]), reads=[IB], writes=[IB, 'Q%d_' % b + I])
                              for ii in range(NI):
                                  I = str(ii)
                                  IB = 'ps%d' % (4 + ii)
                                  S.op('pe', lambda e, ii=ii, a=a, b=b: e.matmul(ps[4 + ii][:, 256:384], Pm[ii][b][:, :], Ym[ii][a][:, :], start=True, stop=True), reads=['P%d_' % b + I, 'Y%d_' % a + I], writes=[IB])
                                  S.op('dve', lambda e, ii=ii, a=a, b=b: e.tensor_tensor(Ym[ii][b][:, :], Ym[ii][a][:, :], ps[4 + ii][:, 256:384], ALU.add), reads=[IB, 'Y%d_' % a + I], writes=[IB, 'Y%d_' % b + I])
                          def tr_to(ii, src, skey, dst, dkey):
                              I = str(ii)
                              IB = 'ps%d' % (4 + ii)
                              pbt = psb(4 + ii)[:, 512:640]
                              S.op('pe', lambda e: e.transpose(pbt, src[:, :], identb), reads=[skey + I, 'cb'], writes=[IB])
                              S.op('act', lambda e: e.activation(dst[:, :], pbt, AF.Copy), reads=[IB], writes=[IB, dkey + I])
                          for ii in range(NI):
                              tr_to(ii, Ym[ii][1], 'Y1_', Pm[ii][0], 'P0_')
                          for (Uo, ukey, d_i, dt_i, m_i) in ((Uo16, 'Uo16_', 0, 1, 0), (Uo32, 'Uo32_', 1, 0, 1)):
                              for ii in range(NI):
                                  I = str(ii)
                                  IB = 'ps%d' % (4 + ii)
                                  S.op('pe', lambda e, ii=ii, Uo=Uo, d_i=d_i: e.matmul(ps[4 + ii][:, 0:128], Uo[ii][:, :], Pm[ii][d_i][:, :], start=True, stop=True), reads=[ukey + I, 'P%d_' % d_i + I], writes=[IB])
                                  S.op('act', lambda e, ii=ii, m_i=m_i: e.activation(Qm[ii][m_i][:, :], ps[4 + ii][:, 0:128], AF.Copy), reads=[IB], writes=[IB, 'Q%d_' % m_i + I])
                              for ii in range(NI):
                                  I = str(ii)
                                  IB = 'ps%d' % (4 + ii)
                                  S.op('pe', lambda e, ii=ii, dt_i=dt_i, m_i=m_i: e.matmul(ps[4 + ii][:, 128:256], Ym[ii][dt_i][:, :], Qm[ii][m_i][:, :], start=True, stop=True), reads=['Y%d_' % dt_i + I, 'Q%d_' % m_i + I], writes=[IB])
                                  S.op('dve', lambda e, ii=ii, d_i=d_i: e.tensor_tensor(Pm[ii][1 - d_i][:, :], Pm[ii][d_i][:, :], ps[4 + ii][:, 128:256], ALU.subtract), reads=[IB, 'P%d_' % d_i + I], writes=[IB, 'P%d_' % (1 - d_i) + I])
                              for ii in range(NI):
                                  tr_to(ii, Pm[ii][1 - d_i], 'P%d_' % (1 - d_i), Ym[ii][1 - dt_i], 'Y%d_' % (1 - dt_i))
                          for ii in range(NI):
                              I = str(ii)
                              IB = 'ps%d' % (4 + ii)
                              S.op('pe', lambda e, ii=ii: e.matmul(ps[4 + ii][:, 0:128], Lo64[ii][:, :], Ym[ii][1][:, :], start=True, stop=True), reads=['Lo64_' + I, 'Y1_' + I], writes=[IB])
                              S.op('act', lambda e, ii=ii: e.activation(Qm[ii][0][:, :], ps[4 + ii][:, 0:128], AF.Copy), reads=[IB], writes=[IB, 'Q0_' + I])
                          for ii in range(NI):
                              I = str(ii)
                              IB = 'ps%d' % (4 + ii)
                              S.op('pe', lambda e, ii=ii: e.matmul(ps[4 + ii][:, 128:256], Pm[ii][0][:, :], Qm[ii][0][:, :], start=True, stop=True), reads=['P0_' + I, 'Q0_' + I], writes=[IB])
                              S.op('dve', lambda e, ii=ii: e.tensor_tensor(Ym[ii][0][:, :], Ym[ii][1][:, :], ps[4 + ii][:, 128:256], ALU.subtract), reads=[IB, 'Y1_' + I], writes=[IB, 'Y0_' + I])
                          for ii in range(NI):
                              t = t0 + ii
                              I = str(ii)
                              IB = 'ps%d' % (4 + ii)
                              TT = Ym[ii][0]
                              S.op('pe', lambda e, ii=ii, TT=TT, t=t: e.matmul(ps[4 + ii][:, 0:128], TT[:, :], vb[:, t, :], start=True, stop=True), reads=['Y0_' + I, 'vb%d' % t], writes=[IB])
                              S.op('pe', lambda e, ii=ii, TT=TT, t=t: e.matmul(ps[4 + ii][:, 128:256], kbg[:, t, :], TT[:, :], start=True, stop=True), reads=['Y0_' + I, 'kbg%d' % t], writes=[IB])
                              S.op('act', lambda e, ii=ii, t=t: e.activation(u_[si][:, t, :], ps[4 + ii][:, 0:128], AF.Copy), reads=[IB], writes=[IB, G + 'u%d' % t])
                              S.op('dve', lambda e, ii=ii, t=t: e.tensor_copy(wT[si][:, t, :], ps[4 + ii][:, 128:256]), reads=[IB], writes=[IB, G + 'wT%d' % t])
                      yield

                  def gdn_scan_multi(heads):
                      for (h, si, j) in heads:
                          S.op('pool', lambda e, j=j: e.memset(Sf[j][:, :], 0.0), writes=['Sf%d' % j])
                          S.op('pool', lambda e, j=j: e.memset(Sb[j][:, :], 0.0), writes=['Sb%d' % j])

                      def scan_post(t, h, si, j):
                          ts_ = slice(t * 128, (t + 1) * 128)
                          b2 = t % 2
                          B2 = '%d_%d' % (j, b2)
                          pk = 'ps%d' % (4 * j + 1 + b2)
                          tk = 'ps%d' % (4 * j + 3)
                          po = ps[4 * j + 1 + b2][:, 0:128]
                          S.op('act', lambda e: e.activation(ojk[j][b2][:, :], po, AF.Square, accum_out=ost[j][b2][:, 0:1]), reads=[pk], writes=[pk, 'ojk' + B2, 'ost' + B2])
                          S.op('act', lambda e: e.activation(ost[j][b2][:, 1:2], ost[j][b2][:, 0:1], AF.Ln, bias=EPS, scale=1.0 / 128), reads=['ost' + B2], writes=['ost1' + B2])
                          S.op('act', lambda e: e.activation(ost[j][b2][:, 2:3], ost[j][b2][:, 1:2], AF.Exp, scale=-0.5), reads=['ost1' + B2], writes=['ost2' + B2])
                          S.op('dve', lambda e: e.scalar_tensor_tensor(og[j][b2][:, :], po, ost[j][b2][:, 2:3], gate[:, t, h * 128:(h + 1) * 128], ALU.mult, ALU.mult),
                               reads=[pk, 'ost2' + B2, 'gate%d' % t], writes=[pk, 'og' + B2])
                          pbo = psb(4 * j + 3)[:, b2 * 128:(b2 + 1) * 128]
                          S.op('pe', lambda e: e.transpose(pbo, og[j][b2][:, :], identb), reads=['og' + B2, 'cb'], writes=[tk])
                          copy_op('act' if t % 2 else 'dve', oT_g[:, h, ts_], pbo, reads=[tk], writes=[tk, 'oT_g%d_%d' % (h, t)])

                      for t in range(NT):
                          ts_ = slice(t * 128, (t + 1) * 128)
                          b2 = t % 2
                          def _step(idx, t=t, ts_=ts_, b2=b2):
                              (h, si, j) = heads[idx]
                              G = 'g%d_' % si
                              B2 = '%d_%d' % (j, b2)
                              k0 = 'ps%d' % (4 * j)
                              pk = 'ps%d' % (4 * j + 1 + b2)
                              S.op('pe', lambda e, t=t, si=si, j=j: e.matmul(ps[4 * j][:, 0:128], wT[si][:, t, :], Sb[j][:, :], start=True, stop=True), reads=[G + 'wT%d' % t, 'Sb%d' % j], writes=[k0])
                              S.op('dve', lambda e, t=t, si=si, j=j, b2=b2: e.tensor_tensor(vn[j][b2][:, :], u_[si][:, t, :], ps[4 * j][:, 0:128], ALU.subtract), reads=[G + 'u%d' % t, k0], writes=[k0, 'vn' + B2])
                              S.op('pe', lambda e, t=t, si=si, j=j, b2=b2: e.matmul(ps[4 * j][:, 128:256], kdec[si][:, t, :], vn[j][b2][:, :], start=True, stop=True), reads=[G + 'kdec%d' % t, 'vn' + B2], writes=[k0])
                              S.op('pe', lambda e, ts_=ts_, si=si, j=j, b2=b2: e.matmul(ps[4 * j + 1 + b2][:, 0:128], qgT[si][:, ts_], Sb[j][:, :], start=True, stop=False), reads=[G + 'qgT%d' % t, 'Sb%d' % j], writes=[pk], sig=False)
                              S.op('pe', lambda e, t=t, si=si, j=j, b2=b2: e.matmul(ps[4 * j + 1 + b2][:, 0:128], qkm[si][:, t, :], vn[j][b2][:, :], start=False, stop=True), reads=[G + 'qkm%d' % t, 'vn' + B2], writes=[pk])
                              S.op('dve', lambda e, t=t, h=h, j=j: e.scalar_tensor_tensor(Sf[j][:, :], Sf[j][:, :], egl[:, t, h:h + 1], ps[4 * j][:, 128:256], ALU.mult, ALU.add), reads=['Sf%d' % j, 'egl', k0], writes=[k0, 'Sf%d' % j])
                              S.op('pool', lambda e, j=j: e.tensor_copy(Sb[j][:, :], Sf[j][:, :]), reads=['Sf%d' % j], writes=['Sb%d' % j])
                          lockstep(len(heads), _step)
                          if t >= 1:
                              lockstep(len(heads), lambda idx, t=t: scan_post(t - 1, *heads[idx]))
                      lockstep(len(heads), lambda idx: scan_post(NT - 1, *heads[idx]))

                  def drain(gen):
                      for _ in gen:
                          pass
                  for hp in range(2):
                      drain(gdn_prep(2 * hp, 0))
                      drain(gdn_prep(2 * hp + 1, 1))
                      gdn_scan_multi([(2 * hp, 0, 0), (2 * hp + 1, 1, 1)])
                  flush()
                  stage_done('G', [('oTg%d' % hh, oT_g[:, hh, :], 128, S_LEN) for hh in range(4)])
              oT_f = sb("oT_f", [64, 8, S_LEN], BF16, mx)
              with ExitStack() as pf:
                  wf = sb("wf", [128, 8, 1536], BF16, pf)
                  vaug2 = sb("vaug2", [128, NT, 584], BF16, pf)
                  vaug = vaug2[:, :, 0:520].rearrange("p t (h d) -> p t h d", h=8)
                  qTA = [sb("qTA%d" % i, [128, S_LEN], BF16, pf) for i in range(2)]
                  kTA = [sb("kTA%d" % i, [128, S_LEN], BF16, pf) for i in range(2)]
                  qTB = [sb("qTB%d" % i, [128, S_LEN], BF16, pf) for i in range(2)]
                  kTB = [sb("kTB%d" % i, [128, S_LEN], BF16, pf) for i in range(2)]
                  caqA = sb("caqA", [128, NT, 70], BF16, pf)
                  cakA = sb("cakA", [128, NT, 70], BF16, pf)
                  caqB = sb("caqB", [128, NT, 38], BF16, pf)
                  cakB = sb("cakB", [128, NT, 38], BF16, pf)
                  PT2 = [[sb("PT%d_%d" % (a_, i), [128, 512], BF16, pf) for i in range(3)] for a_ in range(2)]
                  osb2 = [[sb("osb%d_%d" % (a_, i), [65, 512], F32, pf) for i in range(2)] for a_ in range(2)]
                  rc2 = [[sb("rc%d_%d" % (a_, i), [65, 512], F32, pf) for i in range(2)] for a_ in range(2)]
                  for i3 in range(3):
                      S.dma('pool', wf[:, :, i3 * 512:(i3 + 1) * 512],
                            w_in[:, C_FQ + i3 * 512:C_FQ + (i3 + 1) * 512].rearrange("(k p) n -> p k n", p=128), writes=['wf%d' % i3])
                  S.op('pool', lambda e: e.memset(vaug[:, :, :, 64:65], 1.0), writes=['vaug1'])
                  S.op('pool', lambda e: e.memset(vaug2[:, :, 520:584], 0.0), writes=['vaug1'])
                  for i2 in range(2):
                      for (tl, nm, r0, r1) in ((qTA, 'qTA', 64, 128), (kTA, 'kTA', 64, 128), (qTB, 'qTB', 0, 64), (kTB, 'kTB', 0, 64)):
                          S.op('pool', lambda e, i2=i2, tl=tl, r0=r0, r1=r1: e.memset(tl[i2][r0:r1, :], 0.0), writes=['%s%db%d' % (nm, i2, tb_) for tb_ in range(4)])
                  for (c_, nm) in ((caqA, 'caqA'), (cakA, 'cakA'), (caqB, 'caqB'), (cakB, 'cakB')):
                      S.op('pool', lambda e, c_=c_: e.memset(c_[:], 0.0), writes=[nm])
                  S.op('pool', lambda e: e.memset(caqA[:, :, 67:70], 1.0), reads=['caqA'], writes=['caqA'])
                  S.op('pool', lambda e: e.memset(cakA[:, :, 64:67], 1.0), reads=['cakA'], writes=['cakA'])
                  S.op('pool', lambda e: e.memset(caqB[:, :, 35:38], 1.0), reads=['caqB'], writes=['caqB'])
                  S.op('pool', lambda e: e.memset(cakB[:, :, 32:35], 1.0), reads=['cakB'], writes=['cakB'])
                  for t in range(NT):
                      bank = t % 2
                      for kc in range(8):
                          S.op('pe', lambda e, t=t, kc=kc, bank=bank: e.matmul(ps[bank][:, :], xnT[:, kc, t * 128:(t + 1) * 128], wf[:, kc, 1024:1536],
                                                                               start=(kc == 0), stop=(kc == 7)),
                               reads=['xnT%d' % t, 'wf2'], writes=['ps%d' % bank], sig=(kc == 7))
                      copy_op(evac_engine(), vaug[:, t, :, 0:64], ps[bank][:, :].rearrange("p (h d) -> p h d", h=8),
                              reads=['ps%d' % bank], writes=['vaug%d' % t])
                  pending = []
                  pending0 = []
                  for hp in range(4):
                      pb_ = hp % 2
                      hA, hB = 2 * hp, 2 * hp + 1
                      qA, kA, qB, kB = qTA[pb_], kTA[pb_], qTB[pb_], kTB[pb_]
                      nqA, nkA, nqB, nkB = 'qTA%d' % pb_, 'kTA%d' % pb_, 'qTB%d' % pb_, 'kTB%d' % pb_
                      for j in range(3):
                          S.op('dve', lambda e, j=j, hA=hA: e.tensor_copy(caqA[:, :, 64 + j], c123[:, j, :, hA]), reads=['c%d' % (j + 1), 'caqA'], writes=['caqA'])
                          S.op('dve', lambda e, j=j, hA=hA: e.tensor_scalar(cakA[:, :, 67 + j], c123[:, j, :, hA], -1.0, None, ALU.mult), reads=['c%d' % (j + 1), 'cakA'], writes=['cakA'])
                          S.op('dve', lambda e, j=j, hB=hB: e.tensor_copy(caqB[:, :, 32 + j], c123[:, j, :, hB]), reads=['c%d' % (j + 1), 'caqB'], writes=['caqB'])
                          S.op('dve', lambda e, j=j, hB=hB: e.tensor_scalar(cakB[:, :, 35 + j], c123[:, j, :, hB], -1.0, None, ALU.mult), reads=['c%d' % (j + 1), 'cakB'], writes=['cakB'])
                      for tb in range(4):
                          cs = slice(tb * 512, (tb + 1) * 512)
                          xk = ['xnT%d' % (4 * tb + i) for i in range(4)]
                          for kc in range(8):
                              S.op('pe', lambda e, kc=kc, cs=cs, hA=hA: e.matmul(ps[2][:, :], wf[:, kc, hA * 64:hA * 64 + 128], xnT[:, kc, cs], start=(kc == 0), stop=(kc == 7)),
                                   reads=xk + ['wf0'], writes=['ps2'], sig=(kc == 7))
                          S.op('act', lambda e, cs=cs, qA=qA: e.activation(qA[0:64, cs], ps[2][0:64, :], AF.Copy, scale=0.125), reads=['ps2'], writes=['ps2', nqA + 'a%d' % tb])
                          S.op('dve', lambda e, cs=cs, qB=qB: e.tensor_scalar(qB[64:128, cs], ps[2][64:128, :], 0.125, None, ALU.mult), reads=['ps2'], writes=['ps2', nqB + 'a%d' % tb])
                          for kc in range(8):
                              S.op('pe', lambda e, kc=kc, cs=cs, hA=hA: e.matmul(ps[3][:, :], wf[:, kc, 512 + hA * 64:512 + hA * 64 + 128], xnT[:, kc, cs], start=(kc == 0), stop=(kc == 7)),
                                   reads=xk + ['wf1'], writes=['ps3'], sig=(kc == 7))
                          S.op('dve', lambda e, cs=cs, kA=kA: e.tensor_copy(kA[0:64, cs], ps[3][0:64, :]), reads=['ps3'], writes=['ps3', nkA + 'a%d' % tb])
                          S.op('act', lambda e, cs=cs, kB=kB: e.activation(kB[64:128, cs], ps[3][64:128, :], AF.Copy), reads=['ps3'], writes=['ps3', nkB + 'a%d' % tb])
                          for (src_c, M, r0, dstT, dname, bank, eng) in ((caqA, 70, 64, qA, nqA, 4, 'act'), (cakA, 70, 64, kA, nkA, 5, 'dve'),
                                                                     (caqB, 38, 32, qB, nqB, 4, 'act'), (cakB, 38, 32, kB, nkB, 5, 'dve')):
                              ckey = {id(caqA): 'caqA', id(cakA): 'cakA', id(caqB): 'caqB', id(cakB): 'cakB'}[id(src_c)]
                              for i in range(4):
                                  t = 4 * tb + i
                                  S.op('pe', lambda e, t=t, i=i, src_c=src_c, M=M, bank=bank: e.matmul(ps[bank][0:M, i * 128:(i + 1) * 128], src_c[:, t, :], identb, start=True, stop=True),
                                       reads=[ckey, 'cb'], writes=['ps%d' % bank], sig=(i == 3))
                              copy_op(eng, dstT[r0:r0 + 6, cs], ps[bank][r0:r0 + 6, :], reads=['ps%d' % bank], writes=['ps%d' % bank, dname + 'b%d' % tb])
                      pair_ = ((hA, qA, kA, nqA, nkA), (hB, qB, kB, nqB, nkB))

                      def attn_head(hs, pair_=pair_):
                          (h, qTh, kTh, qk, kk) = pair_[hs]
                          PTh = PT2[hs]
                          qkeys = lambda qg: [qk + 'a%d' % qg, qk + 'b%d' % qg]
                          kkeys = lambda kt: [kk + 'a%d' % (kt // 4), kk + 'b%d' % (kt // 4)]
                          step = 0
                          for qg in range(4):
                              ob = (6 if hs == 0 else 4) + (qg % 2)
                              nk = 4 * qg + 4
                              items = []
                              for kt in range(nk):
                                  j = kt - 4 * qg
                                  c0 = max(j, 0) * 128
                                  items.append((kt, j, c0))

                              def emit_qk(kt, j, c0, sbk, qg=qg, qTh=qTh, kTh=kTh):
                                  S.op('pe', lambda e: e.matmul(ps[sbk][:, c0:512], kTh[0:128, kt * 128:(kt + 1) * 128],
                                                                qTh[0:128, qg * 512 + c0:(qg + 1) * 512], start=True, stop=True),
                                       reads=kkeys(kt) + qkeys(qg), writes=['ps%d' % sbk])

                              def emit_exp_pv(kt, j, c0, sbk, pi, first, last, h=h, ob=ob, PTh=PTh, hs=hs):
                                  S.op('act', lambda e: e.activation(PTh[pi][:, c0:512], ps[sbk][:, c0:512], AF.Exp),
                                       reads=['ps%d' % sbk], writes=['ps%d' % sbk, 'PT%d_%d' % (hs, pi)])
                                  if j >= 0:
                                      S.op('dve', lambda e: e.tensor_tensor(PTh[pi][:, c0:c0 + 128], PTh[pi][:, c0:c0 + 128], mask_le_b, ALU.mult),
                                           reads=['PT%d_%d' % (hs, pi), 'cb'], writes=['PT%d_%d' % (hs, pi)])
                                  S.op('pe', lambda e: e.matmul(ps[ob][:, c0:512], vaug2[:, kt, h * 65:h * 65 + 128], PTh[pi][:, c0:512], start=first, stop=last),
                                       reads=['PT%d_%d' % (hs, pi), 'vaug%d' % kt, 'vaug1'], writes=['ps%d' % ob], sig=last)

                              emit_qk(*items[0], sbk=(2 * hs + step % 2))
                              for ii, it in enumerate(items):
                                  if ii + 1 < len(items):
                                      emit_qk(*items[ii + 1], sbk=(2 * hs + (step + 1) % 2))
                                  if ii == 1:
                                      while pending0:
                                          pending0.pop(0)()
                                  if ii == 3:
                                      while pending:
                                          pending.pop(0)()
                                  for _d in range(0):
                                      S.op('pe', lambda e: e.matmul(ps[5][:, 0:256], identb, wf[:, 0, 0:256], start=True, stop=True), reads=['cb', 'wf0'], writes=['ps5'], sig=False)
                                  emit_exp_pv(*it, sbk=(2 * hs + step % 2), pi=(step % 3), first=(ii == 0), last=(ii == len(items) - 1))
                                  step += 1
                              o_ = osb2[hs][qg % 2]; r_ = rc2[hs][qg % 2]; OK_ = 'osb%d_%d' % (hs, qg % 2); RK_ = 'rc%d_%d' % (hs, qg % 2)
                              S.op('dve', lambda e, o_=o_, ob=ob: e.tensor_copy(o_[:, :], ps[ob][0:65, :]), reads=['ps%d' % ob], writes=['ps%d' % ob, OK_])

                              def fin0(o_=o_, r_=r_, qg=qg, h=h, OK_=OK_, RK_=RK_):
                                  S.op('act', lambda e: e.activation(r_[64:65, :], o_[64:65, :], AF.Ln), reads=[OK_], writes=[RK_])
                                  S.op('act', lambda e: e.activation(r_[64:65, :], r_[64:65, :], AF.Exp, scale=-1.0), reads=[RK_], writes=[RK_])
                              pending0.append(fin0)

                              def fin(o_=o_, r_=r_, qg=qg, h=h, ob=ob, OK_=OK_, RK_=RK_):
                                  S.op('pe', lambda e: e.matmul(ps[ob][0:64, :], onesf[64:65, 0:64], r_[64:65, :], start=True, stop=True),
                                       reads=[RK_, 'cf'], writes=['ps%d' % ob])
                                  S.op('dve', lambda e: e.tensor_tensor(oT_f[0:64, h, qg * 512:(qg + 1) * 512], o_[0:64, :], ps[ob][0:64, :], ALU.mult),
                                       reads=[OK_, 'ps%d' % ob], writes=['ps%d' % ob, 'oT_f%d_%d' % (h, qg)])
                              pending.append(fin)
                          while pending0:
                              pending0.pop(0)()
                          while pending:
                              pending.pop(0)()
                      lockstep(2, attn_head)
                  while pending0:
                      pending0.pop(0)()
                  while pending:
                      pending.pop(0)()
                  flush()
                  stage_done('F', [('oTf0', oT_f[0:64, 0, :], 64, S_LEN), ('oTf7', oT_f[0:64, 7, :], 64, S_LEN), ('qT', qTB[1][0:128, :], 128, S_LEN), ('kT', kTB[1][0:128, :], 128, S_LEN)])
              h_sb = es.enter_context(nc.sbuf_tensor("h_sb", [128, NT, D], F32, side="right"))
              with ExitStack() as po:
                  wo_g = sb("wo_g", [128, 4, D], BF16, po)
                  wo_f = sb("wo_f", [128, 4, D], BF16, po)
                  oT_p = sb("oT_p", [128, 4, S_LEN], BF16, po)
                  S.dma('pool', wo_g[:], w_out[0:512, :].rearrange("(k p) n -> p k n", p=128), writes=['wo_g'])
                  S.dma('pool', wo_f[:], w_out[512:1024, :].rearrange("(k p) n -> p k n", p=128), writes=['wo_f'])
                  for hh in range(8):
                      S.dma('sp', oT_p[(hh % 2) * 64:(hh % 2) * 64 + 64, hh // 2, :], oT_f[0:64, hh, :],
                            reads=['oT_f%d_%d' % (hh, qq) for qq in range(4)], writes=['oT_p%d_%d' % (hh // 2, hh % 2)])
                  for t in range(NT):
                      S.dma('sp', h_sb[:, t, :], x[t * 128:(t + 1) * 128, :], writes=['h%d' % t])
                      ts_ = slice(t * 128, (t + 1) * 128)
                      for nh in range(2):
                          bank = (2 * t + nh) % 4
                          ns = slice(nh * 512, (nh + 1) * 512)
                          steps = []
                          if not skip_gdn:
                              for g in range(4):
                                  steps.append((oT_g[:, g, ts_], wo_g[:, g, ns], ['oT_g%d_%d' % (g, t), 'wo_g']))
                          for hp in range(4):
                              steps.append((oT_p[:, hp, ts_], wo_f[:, hp, ns], ['oT_p%d_0' % hp, 'oT_p%d_1' % hp, 'wo_f']))
                          for si, (l_, r_, rd) in enumerate(steps):
                              S.op('pe', lambda e, l_=l_, r_=r_, si=si, bank=bank, n=len(steps): e.matmul(ps[bank][:, :], l_, r_, start=(si == 0), stop=(si == n - 1)),
                                   reads=rd, writes=['ps%d' % bank], sig=(si == len(steps) - 1))
                          S.op('dve', lambda e, t=t, ns=ns, bank=bank: e.tensor_tensor(h_sb[:, t, ns], h_sb[:, t, ns], ps[bank][:, :], ALU.add),
                               reads=['h%d' % t, 'ps%d' % bank], writes=['h%d' % t])
                  flush()
                  stage_done('O', [('h0', h_sb[:, 0, :], 128, D)])
          if True:
              with ExitStack() as pff:
                  hnT = sb("hnT", [128, 8, S_LEN], BF16, pff)
                  gF = sb("gF", [128, D], F32, pff)
                  wgu = sb("wgu", [128, 11, 8, 256], BF16, pff)
                  wdn = sb("wdn", [128, 11, D], BF16, pff)
                  actT = sb("actT", [128, 11, 512], BF16, pff)
                  sg = [sb("sg%d" % i, [128, 512], F32, pff) for i in range(2)]
                  S.dma('sp', gF[:], g_ffn.partition_broadcast(128), writes=['gF'])

                  def load_half(hf):
                      for jj in range(11):
                          j = hf * 11 + jj
                          S.dma('pool', wgu[:, jj, :, 0:128], w_gu[:, j * 128:(j + 1) * 128].rearrange("(k p) n -> p k n", p=128), writes=['wgu%d' % jj])
                          S.dma('pool', wgu[:, jj, :, 128:256], w_gu[:, DFF + j * 128:DFF + (j + 1) * 128].rearrange("(k p) n -> p k n", p=128), writes=['wgu%d' % jj])
                      for jj in range(11):
                          j = hf * 11 + jj
                          S.dma('pool', wdn[:, jj, :], w_dn[j * 128:(j + 1) * 128, :], writes=['wdn%d' % jj])
                  load_half(0)
                  def dstF(t, pb, bkey):
                      copy_op('act', hnT[:, :, t * 128:(t + 1) * 128], pb.rearrange("p (k t) -> p k t", k=8), reads=[bkey], writes=[bkey, 'hnT%d' % t])
                  norm_T_all('F', lambda t: h_sb[:, t, :], lambda t: 'h%d' % t, gF[:, :], 'gF', pff, dstF)
                  for hf in range(2):
                      if hf == 1:
                          load_half(1)
                      for tb in range(4):
                          cs = slice(tb * 512, (tb + 1) * 512)
                          hk = ['hnT%d' % (4 * tb + i) for i in range(4)]
                          for jj in range(11):
                              bg = 2 + (jj % 2) * 2
                              bu = bg + 1
                              for kc in range(8):
                                  S.op('pe', lambda e, jj=jj, kc=kc, cs=cs, bg=bg: e.matmul(ps[bg][:, :], wgu[:, jj, kc, 0:128], hnT[:, kc, cs], start=(kc == 0), stop=(kc == 7)),
                                       reads=hk + ['wgu%d' % jj], writes=['ps%d' % bg], sig=(kc == 7))
                              for kc in range(8):
                                  S.op('pe', lambda e, jj=jj, kc=kc, cs=cs, bu=bu: e.matmul(ps[bu][:, :], wgu[:, jj, kc, 128:256], hnT[:, kc, cs], start=(kc == 0), stop=(kc == 7)),
                                       reads=hk + ['wgu%d' % jj], writes=['ps%d' % bu], sig=(kc == 7))
                              s_ = sg[jj % 2]
                              S.op('act', lambda e, s_=s_, bg=bg: e.activation(s_[:, :], ps[bg][:, :], AF.Silu), reads=['ps%d' % bg], writes=['sg%d' % (jj % 2)])
                              S.op('dve', lambda e, s_=s_, bu=bu, jj=jj: e.tensor_tensor(actT[:, jj, :], s_[:, :], ps[bu][:, :], ALU.mult),
                                   reads=['sg%d' % (jj % 2), 'ps%d' % bu], writes=['actT%d' % jj])
                          for i in range(4):
                              t = 4 * tb + i
                              for nh in range(2):
                                  bank = nh
                                  ns = slice(nh * 512, (nh + 1) * 512)
                                  for jj in range(11):
                                      S.op('pe', lambda e, jj=jj, i=i, ns=ns, bank=bank: e.matmul(ps[bank][:, :], actT[:, jj, i * 128:(i + 1) * 128], wdn[:, jj, ns],
                                                                                                 start=(jj == 0), stop=(jj == 10)),
                                           reads=['actT%d' % jj, 'wdn%d' % jj], writes=['ps%d' % bank], sig=(jj == 10))
                                  S.op('dve', lambda e, t=t, ns=ns, bank=bank: e.tensor_tensor(h_sb[:, t, ns], h_sb[:, t, ns], ps[bank][:, :], ALU.add),
                                       reads=['h%d' % t, 'ps%d' % bank], writes=['h%d' % t])
                  flush()
                  stage_done('FFN', [('h0f', h_sb[:, 0, :], 128, D)])
              with ExitStack() as pp:
                  gP = sb("gP", [128, D], F32, pp)
                  gL = sb("gL", [128, D], F32, pp)
                  wpg = sb("wpg", [128, 8, D], BF16, pp)
                  wpp = sb("wpp", [128, 2, D], BF16, pp)
                  lnT = [sb("lnT%d" % i, [128, 8, 128], BF16, pp) for i in range(NT)]
                  pt = [sb("pt%d" % i, [128, 256], F32, pp) for i in range(3)]
                  ptb = [sb("ptb%d" % i, [128, 256], BF16, pp) for i in range(3)]
                  pT = [sb("pT%d" % i, [128, 2, 128], BF16, pp) for i in range(NT)]
                  sig_ = [sb("sig%d" % i, [128, D], F32, pp) for i in range(2)]
                  yo = [sb("yo%d" % i, [128, D], F32, pp) for i in range(2)]
                  fss = sb("fss", [128, NT], F32, pp); flv = sb("flv", [128, NT], F32, pp); frs = sb("frs", [128, NT], F32, pp)
                  fjunk = [sb("fjunk%d" % i, [128, D], BF16, pp) for i in range(2)]
                  S.dma('sp', gP[:], g_ple.partition_broadcast(128), writes=['gP'])
                  S.dma('sp', gL[:], g_fin.partition_broadcast(128), writes=['gL'])
                  S.dma('pool', wpg[:], w_pg.rearrange("(k p) n -> p k n", p=128), writes=['wpg'])
                  S.dma('pool', wpp[:], w_pp.rearrange("(k p) n -> p k n", p=128), writes=['wpp'])

                  def ple_back(t):
                      b3 = t
                      b = t % 2
                      for nh in range(2):
                          ns = slice(nh * 512, (nh + 1) * 512)
                          bg = 4 + nh
                          bq = 6 + nh
                          for kc in range(8):
                              S.op('pe', lambda e, kc=kc, ns=ns, bg=bg: e.matmul(ps[bg][:, :], lnT[b3][:, kc, :], wpg[:, kc, ns], start=(kc == 0), stop=(kc == 7)),
                                   reads=['lnT%d' % b3, 'wpg'], writes=['ps%d' % bg], sig=(kc == 7))
                          for kc in range(2):
                              S.op('pe', lambda e, kc=kc, ns=ns, bq=bq: e.matmul(ps[bq][:, :], pT[b3][:, kc, :], wpp[:, kc, ns], start=(kc == 0), stop=(kc == 1)),
                                   reads=['pT%d' % b3, 'wpp'], writes=['ps%d' % bq], sig=(kc == 1))
                          S.op('act', lambda e, ns=ns, bg=bg: e.activation(sig_[b][:, ns], ps[bg][:, :], AF.Sigmoid), reads=['ps%d' % bg], writes=['ps%d' % bg, 'sig%d_%d' % (b, nh)])
                          S.op('dve', lambda e, ns=ns, bq=bq: e.tensor_tensor(sig_[b][:, ns], sig_[b][:, ns], ps[bq][:, :], ALU.mult),
                               reads=['sig%d_%d' % (b, nh), 'ps%d' % bq], writes=['ps%d' % bq, 'sig%d_%d' % (b, nh)])
                          S.op('dve', lambda e, ns=ns: e.tensor_tensor(h_sb[:, t, ns], h_sb[:, t, ns], sig_[b][:, ns], ALU.add),
                               reads=['h%d' % t, 'sig%d_%d' % (b, nh)], writes=['h%d' % t])
                      S.op('act', lambda e: e.activation(fjunk[b][:, :], h_sb[:, t, :], AF.Square, accum_out=fss[:, t:t + 1]),
                           reads=['h%d' % t], writes=['fjunk%d' % b, 'fss%d' % t])
                      if t % 4 == 3:
                          g = t // 4
                          gs = slice(4 * g, 4 * g + 4)
                          S.op('act', lambda e: e.activation(flv[:, gs], fss[:, gs], AF.Ln, bias=EPS, scale=1.0 / D),
                               reads=['fss%d' % tt for tt in range(4 * g, 4 * g + 4)], writes=['flv%d' % g])
                          S.op('act', lambda e: e.activation(frs[:, gs], flv[:, gs], AF.Exp, scale=-0.5), reads=['flv%d' % g], writes=['frs%d' % g])
                          for tt in range(4 * g, 4 * g + 4):
                              bb = tt % 2
                              S.op('dve', lambda e, tt=tt, bb=bb: e.scalar_tensor_tensor(yo[bb][:, :], h_sb[:, tt, :], frs[:, tt:tt + 1], gL[:, :], ALU.mult, ALU.mult),
                                   reads=['h%d' % tt, 'frs%d' % g, 'gL'], writes=['yo%d' % bb])
                              S.dma('sp', y[tt * 128:(tt + 1) * 128, :], yo[bb][:, :], reads=['yo%d' % bb], is_out=True)

                  def dstP(t, pb, bkey):
                      b3 = t % 3
                      copy_op('act', lnT[t][:, :, :], pb.rearrange("p (k t) -> p k t", k=8), reads=[bkey], writes=[bkey, 'lnT%d' % t])
                      S.op('pool', lambda e: e.tensor_copy(ptb[b3][:, :], pt[b3][:, :]), reads=['pt%d' % b3], writes=['ptb%d' % b3])
                      if t + 2 < NT:
                          preP(t + 2)
                      bank = 2 + t % 2
                      pb2 = psb(bank)
                      for kc in range(2):
                          S.op('pe', lambda e, kc=kc: e.transpose(pb2[:, kc * 128:(kc + 1) * 128], ptb[b3][:, kc * 128:(kc + 1) * 128], identb),
                               reads=['ptb%d' % b3, 'cb'], writes=['ps%d' % bank], sig=(kc == 1))
                      copy_op('dve', pT[t][:, :, :], pb2[:, 0:256].rearrange("p (k t) -> p k t", k=2), reads=['ps%d' % bank], writes=['ps%d' % bank, 'pT%d' % t])

                  def preP(t):
                      S.dma('sp', pt[t % 3][:], p[t * 128:(t + 1) * 128, :], writes=['pt%d' % (t % 3)])
                  preP(0); preP(1)
                  norm_T_all('P', lambda t: h_sb[:, t, :], lambda t: 'h%d' % t, gP[:, :], 'gP', pp, dstP)
                  for t_ in range(NT):
                      ple_back(t_)
                  S.finish()
                  flush()
    return nc


_CACHE = {}


def kernel(**inputs):
    f32 = lambda a: np.ascontiguousarray(np.asarray(a, dtype=np.float32))
    x = f32(inputs['x']); p = f32(inputs['p'])
    B = x.shape[0]
    shared = {
        'w_in': f32(inputs['w_in'][0]), 'w_out': f32(inputs['w_out'][0]),
        'w_gate_up': f32(inputs['w_gate_up'][0]), 'w_down': f32(inputs['w_down'][0]),
        'w_ple_gate': f32(inputs['w_ple_gate'][0]), 'w_ple_proj': f32(inputs['w_ple_proj'][0]),
        'attn_norm_w': f32(inputs['attn_norm_w'][0]), 'ffn_norm_w': f32(inputs['ffn_norm_w'][0]),
        'ple_norm_w': f32(inputs['ple_norm_w'][0]), 'final_norm_w': f32(inputs['final_norm_w']),
        'conv_wT': f32(np.asarray(inputs['conv_w'][0]).T.reshape(12, 128, 4).transpose(1, 0, 2).reshape(128, 48)),
        'smallv': f32(np.concatenate([np.asarray(inputs['dt_bias'][0]).reshape(-1), np.asarray(inputs['a_log'][0]).reshape(-1), np.zeros(56)])),
        'gdn_norm_w': f32(inputs['gdn_norm_w'][0]), 'fox_f_bias': f32(inputs['fox_f_bias'][0]),
        'cst': make_consts(),
    }
    if 'nc' not in _CACHE:
        _CACHE['nc'] = build_program(skip_gdn=SKIP_GDN)
    nc = _CACHE['nc']
    in_maps = []
    for b in range(B):
        m = dict(shared)
        m['x'] = f32(x[b]); m['p'] = f32(p[0, b])
        in_maps.append(m)
    res = run_bass_kernel_spmd(nc, in_maps, core_ids=list(range(B)))
    return np.stack([np.asarray(r['y'], dtype=np.float32) for r in res.results], axis=0)


SKIP_GDN = False
NDUMMY = 2
```
